# Optimizing a Trainium2 kernel written in Bass

```python
import math
import jax
import jax.numpy as jnp
from jax import lax
import numpy as np

D_MODEL = 1024
BATCH = 2
SEQ = 16384
DEPTH = 2
DEC_BATCH = 16
DEC_SEQ = 2048
PAST_LEN = 128

GRID_W = 64
EPS = 1e-6
ROPE_THETA = 500000.0
ROPE_FRACTION = 4
N_BRANCH = 4
BRANCH_W = D_MODEL // N_BRANCH
N_IN_SEGMENTS = 11
DA_HEADS = 4
DA_DIM = BRANCH_W // (2 * DA_HEADS)
DA_VDIM = 2 * DA_DIM
Q_BLOCK = 128
NA_HEADS = 4
NA_DIM = BRANCH_W // NA_HEADS
NA_ROWS = 8
NA_COLS = 16
S5_W = BRANCH_W
S5_GROUP = 16
S5_GROUPS = S5_W // S5_GROUP
S5_STATE = 64
S5_DT_MIN = 0.001
S5_DT_MAX = 0.1
RET_HEADS = 4
RET_DK = BRANCH_W // RET_HEADS
RET_DV = BRANCH_W // RET_HEADS
RET_CHUNK = 128
D_FF = 2816
CONV_WIDTH = 3

kernel_name = 'hybrid_bidir_encoder_dual_batch'


def rms_norm(x, g):
    xf = x.astype(jnp.float32)
    y = xf * lax.rsqrt(jnp.mean(xf * xf, axis=-1, keepdims=True) + EPS)
    return (y * g.astype(jnp.float32)).astype(x.dtype)


def split_heads(t, n_heads):
    B, L, _ = t.shape
    return t.reshape(B, L, n_heads, -1).transpose(0, 2, 1, 3)


def merge_heads(t):
    B, H, L, d = t.shape
    return t.transpose(0, 2, 1, 3).reshape(B, L, H * d)


def partial_rotary(t, pos):
    rot = t.shape[-1] // ROPE_FRACTION
    half = rot // 2
    inv_freq = jnp.power(jnp.float32(ROPE_THETA), -jnp.arange(half, dtype=jnp.float32) / half)
    ang = pos.astype(jnp.float32)[:, None] * inv_freq[None, :]
    cos = jnp.cos(ang).astype(t.dtype)
    sin = jnp.sin(ang).astype(t.dtype)
    t1, t2, rest = t[..., :half], t[..., half:rot], t[..., rot:]
    return jnp.concatenate([t1 * cos - t2 * sin, t2 * cos + t1 * sin, rest], axis=-1)


def diff_attention(q, k, v, lam):
    B, H, _, L, d = q.shape
    nb = L // Q_BLOCK
    qb = q.reshape(B, H, 2, nb, Q_BLOCK, d).transpose(3, 0, 1, 2, 4, 5)
    scale = d ** -0.5

    def block(q_i):
        s = jnp.einsum('bhcqd,bhckd->bhcqk', q_i, k).astype(jnp.float32) * scale
        p = jax.nn.softmax(s, axis=-1)
        a = p[:, :, 0] - lam * p[:, :, 1]
        return jnp.einsum('bhqk,bhkv->bhqv', a.astype(v.dtype), v)

    o = lax.map(block, qb)
    return o.transpose(1, 2, 0, 3, 4).reshape(B, H, L, v.shape[-1])


def neighbourhood_attention(q, k, v, rpb):
    B, H, L, d = q.shape
    rows = L // GRID_W
    kr = min(NA_ROWS, rows)
    kg = k.reshape(B, H, rows, GRID_W, d)
    vg = v.reshape(B, H, rows, GRID_W, d)
    qs = q.reshape(B, H, rows, GRID_W, d).transpose(2, 0, 1, 3, 4)
    col = jnp.arange(GRID_W)
    col_idx = jnp.clip(col - NA_COLS // 2, 0, GRID_W - NA_COLS)[:, None] + jnp.arange(NA_COLS)[None, :]
    dcol = col_idx - col[:, None] + (NA_COLS - 1)
    row_ids = jnp.arange(rows)
    row_start = jnp.clip(row_ids - kr // 2, 0, rows - kr)
    scale = d ** -0.5
    rpb32 = rpb.astype(jnp.float32)

    def one_row(args):
        r, rs, q_r = args
        k_win = lax.dynamic_slice_in_dim(kg, rs, kr, axis=2)[:, :, :, col_idx, :]
        v_win = lax.dynamic_slice_in_dim(vg, rs, kr, axis=2)[:, :, :, col_idx, :]
        drow = rs + jnp.arange(kr) - r + (NA_ROWS - 1)
        bias = rpb32[:, drow[None, :, None], dcol[:, None, :]]
        s = jnp.einsum('bhwd,bhrwcd->bhwrc', q_r, k_win).astype(jnp.float32) * scale + bias[None]
        p = jax.nn.softmax(s.reshape(B, H, GRID_W, kr * NA_COLS), axis=-1).reshape(s.shape)
        return jnp.einsum('bhwrc,bhrwcd->bhwd', p.astype(v_win.dtype), v_win)

    o = lax.map(one_row, (row_ids, row_start, qs))
    return o.transpose(1, 2, 0, 3, 4).reshape(B, H, L, d)


def _complex_affine_combine(e1, e2):
    a1r, a1i, b1r, b1i = e1
    a2r, a2i, b2r, b2i = e2
    return (a2r * a1r - a2i * a1i,
            a2r * a1i + a2i * a1r,
            a2r * b1r - a2i * b1i + b2r,
            a2r * b1i + a2i * b1r + b2i)


def s5_scan(u, a_re, a_im, log_dt, b_re, b_im, c_re, c_im, reverse):
    f32 = jnp.float32
    lam_re = jnp.minimum(a_re.astype(f32), -1e-4)
    lam_im = a_im.astype(f32)
    dt = jnp.exp(log_dt.astype(f32))[:, None]
    mag = jnp.exp(lam_re * dt)
    abar_re = mag * jnp.cos(lam_im * dt)
    abar_im = mag * jnp.sin(lam_im * dt)
    den = lam_re * lam_re + lam_im * lam_im
    f_re = ((abar_re - 1.0) * lam_re + abar_im * lam_im) / den
    f_im = (abar_im * lam_re - (abar_re - 1.0) * lam_im) / den
    br = b_re.astype(f32)
    bi = b_im.astype(f32)
    bb_re = f_re[..., None] * br - f_im[..., None] * bi
    bb_im = f_re[..., None] * bi + f_im[..., None] * br
    bu_re = jnp.einsum('gph,blgh->blgp', bb_re, u)
    bu_im = jnp.einsum('gph,blgh->blgp', bb_im, u)
    ar = jnp.broadcast_to(abar_re, bu_re.shape)
    ai = jnp.broadcast_to(abar_im, bu_re.shape)
    _, _, s_re, s_im = lax.associative_scan(_complex_affine_combine, (ar, ai, bu_re, bu_im), reverse=reverse, axis=1)
    return (jnp.einsum('ghp,blgp->blgh', c_re.astype(f32), s_re)
            - jnp.einsum('ghp,blgp->blgh', c_im.astype(f32), s_im))


def s5_layer(su, a_re, a_im, log_dt, b_re, b_im, c_re, c_im, d, glu_w, glu_b):
    B, L, _ = su.shape
    uf = su.astype(jnp.float32)
    u = uf.reshape(B, L, S5_GROUPS, S5_GROUP)
    y = d.astype(jnp.float32) * uf
    for direction in range(2):
        y = y + s5_scan(u, a_re[direction], a_im[direction], log_dt[direction], b_re[direction],
                        b_im[direction], c_re[direction], c_im[direction],
                        reverse=(direction == 1)).reshape(B, L, S5_W)
    y = jax.nn.gelu(y)
    y = y * jax.nn.sigmoid(y @ glu_w.astype(jnp.float32) + glu_b.astype(jnp.float32))
    return y.astype(su.dtype)


def causal_retention(q, k, v, log_gamma, include_diag):
    B, H, L, dk = q.shape
    dv = v.shape[-1]
    C = RET_CHUNK
    n = L // C
    qc = q.reshape(B, H, n, C, dk)
    kc = k.reshape(B, H, n, C, dk)
    vc = v.reshape(B, H, n, C, dv)
    idx = jnp.arange(C, dtype=jnp.float32)
    lg = log_gamma.astype(jnp.float32)
    rel = idx[:, None] - idx[None, :]
    mask = (rel >= 0) if include_diag else (rel > 0)
    decay_in = jnp.where(mask[None], jnp.exp(lg[:, None, None] * jnp.where(mask, rel, 0.0)[None]), 0.0)
    s = jnp.einsum('bhnid,bhnjd->bhnij', qc, kc) * decay_in[None, :, None].astype(q.dtype)
    inner = jnp.einsum('bhnij,bhnje->bhnie', s, vc)
    zeta = jnp.exp(lg[:, None] * (C - 1.0 - idx)[None]).astype(q.dtype)
    kv = jnp.einsum('bhnjd,hj,bhnje->nbhde', kc, zeta, vc)
    chunk_decay = jnp.exp(lg * C).astype(q.dtype)[None, :, None, None]

    def step(state, kv_n):
        return state * chunk_decay + kv_n, state

    _, prev = lax.scan(step, jnp.zeros_like(kv[0]), kv)
    xi = jnp.exp(lg[:, None] * (idx + 1.0)[None]).astype(q.dtype)
    cross = jnp.einsum('bhnid,nbhde->bhnie', qc, prev) * xi[None, :, None, :, None]
    return (inner + cross).reshape(B, H, L, dv)


def retention_layer(rq, rk, rv, rg, ret_log_decay, ret_norm_g):
    q = split_heads(rq, RET_HEADS)
    k = split_heads(rk, RET_HEADS) * (RET_DK ** -0.5)
    v = split_heads(rv, RET_HEADS)
    log_gamma = -jnp.exp(ret_log_decay.astype(jnp.float32))
    fwd = causal_retention(q, k, v, log_gamma[0], True)
    bwd = jnp.flip(causal_retention(jnp.flip(q, 2), jnp.flip(k, 2), jnp.flip(v, 2), log_gamma[1], False), 2)
    o = rms_norm(fwd + bwd, ret_norm_g)
    return merge_heads(o) * jax.nn.silu(rg)


def token_mixer(h, lam_init, w_in, da_lambda, da_subln_g, na_rpb, s5_a_re, s5_a_im, s5_log_dt,
                s5_b_re, s5_b_im, s5_c_re, s5_c_im, s5_d, s5_glu_w, s5_glu_b, ret_log_decay,
                ret_norm_g, w_branch, w_branch_gate, w_out):
    B, L, _ = h.shape
    pos = jnp.arange(L)
    aq, ak, av, nq, nk, nv, su, rq, rk, rv, rg = jnp.split(h @ w_in, N_IN_SEGMENTS, axis=-1)

    def da_heads(t):
        return t.reshape(B, L, DA_HEADS, 2, DA_DIM).transpose(0, 2, 3, 1, 4)
    qa = partial_rotary(da_heads(aq), pos)
    ka = partial_rotary(da_heads(ak), pos)
    lam_f = da_lambda.astype(jnp.float32)
    lam = jnp.exp(jnp.sum(lam_f[0] * lam_f[1])) - jnp.exp(jnp.sum(lam_f[2] * lam_f[3])) + lam_init
    o_a = diff_attention(qa, ka, split_heads(av, DA_HEADS), lam)
    o_a = merge_heads(rms_norm(o_a, da_subln_g) * (1.0 - lam_init))

    o_b = merge_heads(neighbourhood_attention(split_heads(nq, NA_HEADS), split_heads(nk, NA_HEADS),
                                              split_heads(nv, NA_HEADS), na_rpb))

    o_c = s5_layer(su, s5_a_re, s5_a_im, s5_log_dt, s5_b_re, s5_b_im, s5_c_re, s5_c_im,
                   s5_d, s5_glu_w, s5_glu_b)

    o_d = retention_layer(rq, rk, rv, rg, ret_log_decay, ret_norm_g)

    branches = (o_a, o_b, o_c, o_d)
    merged = jax.nn.sigmoid(h @ w_branch_gate[0]) * (branches[0] @ w_branch[0])
    for i in range(1, N_BRANCH):
        merged = merged + jax.nn.sigmoid(h @ w_branch_gate[i]) * (branches[i] @ w_branch[i])
    return merged @ w_out


def conv_ffn(h, w_up, w_gate, conv_w, conv_b, w_down):
    a = h @ w_up
    a = lax.conv_general_dilated(a, conv_w.astype(a.dtype), window_strides=(1,),
                                 padding=[(CONV_WIDTH // 2, CONV_WIDTH // 2)],
                                 dimension_numbers=('NWC', 'WIO', 'NWC'),
                                 feature_group_count=a.shape[-1]) + conv_b
    return (jax.nn.gelu(a) * (h @ w_gate)) @ w_down


def layer(x, c, lam_init, norm1_g, norm2_g, ada_w, ada_b, w_in, da_lambda, da_subln_g, na_rpb,
          s5_a_re, s5_a_im, s5_log_dt, s5_b_re, s5_b_im, s5_c_re, s5_c_im, s5_d, s5_glu_w,
          s5_glu_b, ret_log_decay, ret_norm_g, w_branch, w_branch_gate, w_out, ffn_w_up,
          ffn_w_gate, ffn_conv_w, ffn_conv_b, ffn_w_down):
    mod = jax.nn.silu(c) @ ada_w + ada_b
    sh1, sc1, g1, sh2, sc2, g2 = [m[:, None, :] for m in jnp.split(mod, 6, axis=-1)]
    h = rms_norm(x, norm1_g) * (1.0 + sc1) + sh1
    x = x + g1 * token_mixer(h, lam_init, w_in, da_lambda, da_subln_g, na_rpb, s5_a_re, s5_a_im,
                             s5_log_dt, s5_b_re, s5_b_im, s5_c_re, s5_c_im, s5_d, s5_glu_w,
                             s5_glu_b, ret_log_decay, ret_norm_g, w_branch, w_branch_gate, w_out)
    h = rms_norm(x, norm2_g) * (1.0 + sc2) + sh2
    x = x + g2 * conv_ffn(h, ffn_w_up, ffn_w_gate, ffn_conv_w, ffn_conv_b, ffn_w_down)
    return x


def trunk(x, c, norm1_g, norm2_g, ada_w, ada_b, w_in, da_lambda, da_subln_g, na_rpb,
          s5_a_re, s5_a_im, s5_log_dt, s5_b_re, s5_b_im, s5_c_re, s5_c_im, s5_d, s5_glu_w,
          s5_glu_b, ret_log_decay, ret_norm_g, w_branch, w_branch_gate, w_out, ffn_w_up,
          ffn_w_gate, ffn_conv_w, ffn_conv_b, ffn_w_down, final_norm_g):
    for l in range(DEPTH):
        lam_init = 0.8 - 0.6 * math.exp(-0.3 * l)
        x = layer(x, c, lam_init, norm1_g[l], norm2_g[l], ada_w[l], ada_b[l], w_in[l], da_lambda[l],
                  da_subln_g[l], na_rpb[l], s5_a_re[l], s5_a_im[l], s5_log_dt[l], s5_b_re[l],
                  s5_b_im[l], s5_c_re[l], s5_c_im[l], s5_d[l], s5_glu_w[l], s5_glu_b[l],
                  ret_log_decay[l], ret_norm_g[l], w_branch[l], w_branch_gate[l], w_out[l],
                  ffn_w_up[l], ffn_w_gate[l], ffn_conv_w[l], ffn_conv_b[l], ffn_w_down[l])
    return rms_norm(x, final_norm_g)


def setup_inputs(seed: int = 0) -> dict:
    key = jax.random.key(seed)
    keys = iter(jax.random.split(key, 48))

    def nrm(shape, std):
        return jax.random.normal(next(keys), shape, jnp.float32) * std

    D = D_MODEL
    ret_init = jnp.log(-jnp.log1p(-jnp.power(2.0, -5.0 - jnp.arange(RET_HEADS, dtype=jnp.float32))))
    a_im_init = math.pi * jnp.arange(S5_STATE, dtype=jnp.float32)
    return {
        'x_prompt': nrm((BATCH, SEQ, D), 1.0),
        'x_sample': nrm((DEC_BATCH, DEC_SEQ, D), 1.0),
        'c_prompt': nrm((BATCH, D), 1.0),
        'c_sample': nrm((DEC_BATCH, D), 1.0),
        'norm1_g': 1.0 + nrm((DEPTH, D), 0.02),
        'norm2_g': 1.0 + nrm((DEPTH, D), 0.02),
        'ada_w': nrm((DEPTH, D, 6 * D), 0.5 * D ** -0.5),
        'ada_b': nrm((DEPTH, 6 * D), 0.02),
        'w_in': nrm((DEPTH, D, N_IN_SEGMENTS * BRANCH_W), D ** -0.5),
        'da_lambda': nrm((DEPTH, 4, DA_DIM), 0.1),
        'da_subln_g': 1.0 + nrm((DEPTH, DA_VDIM), 0.02),
        'na_rpb': nrm((DEPTH, NA_HEADS, 2 * NA_ROWS - 1, 2 * NA_COLS - 1), 0.02),
        's5_a_re': -0.5 + nrm((DEPTH, 2, S5_GROUPS, S5_STATE), 0.01),
        's5_a_im': a_im_init + nrm((DEPTH, 2, S5_GROUPS, S5_STATE), 0.01),
        's5_log_dt': jax.random.uniform(next(keys), (DEPTH, 2, S5_GROUPS), jnp.float32,
                                        math.log(S5_DT_MIN), math.log(S5_DT_MAX)),
        's5_b_re': nrm((DEPTH, 2, S5_GROUPS, S5_STATE, S5_GROUP), (2 * S5_GROUP) ** -0.5),
        's5_b_im': nrm((DEPTH, 2, S5_GROUPS, S5_STATE, S5_GROUP), (2 * S5_GROUP) ** -0.5),
        's5_c_re': nrm((DEPTH, 2, S5_GROUPS, S5_GROUP, S5_STATE), (2 * S5_STATE) ** -0.5),
        's5_c_im': nrm((DEPTH, 2, S5_GROUPS, S5_GROUP, S5_STATE), (2 * S5_STATE) ** -0.5),
        's5_d': nrm((DEPTH, S5_W), 0.5),
        's5_glu_w': nrm((DEPTH, S5_W, S5_W), S5_W ** -0.5),
        's5_glu_b': nrm((DEPTH, S5_W), 0.02),
        'ret_log_decay': ret_init[None, None, :] + nrm((DEPTH, 2, RET_HEADS), 0.05),
        'ret_norm_g': 1.0 + nrm((DEPTH, RET_DV), 0.02),
        'w_branch': nrm((DEPTH, N_BRANCH, BRANCH_W, D), BRANCH_W ** -0.5),
        'w_branch_gate': nrm((DEPTH, N_BRANCH, D, D), D ** -0.5),
        'w_out': nrm((DEPTH, D, D), D ** -0.5),
        'ffn_w_up': nrm((DEPTH, D, D_FF), D ** -0.5),
        'ffn_w_gate': nrm((DEPTH, D, D_FF), D ** -0.5),
        'ffn_conv_w': nrm((DEPTH, CONV_WIDTH, 1, D_FF), CONV_WIDTH ** -0.5),
        'ffn_conv_b': nrm((DEPTH, D_FF), 0.02),
        'ffn_w_down': nrm((DEPTH, D_FF, D), D_FF ** -0.5),
        'final_norm_g': 1.0 + nrm((D,), 0.02),
    }


def reference(x_prompt, x_sample, c_prompt, c_sample, norm1_g, norm2_g, ada_w, ada_b, w_in,
              da_lambda, da_subln_g, na_rpb, s5_a_re, s5_a_im, s5_log_dt, s5_b_re, s5_b_im,
              s5_c_re, s5_c_im, s5_d, s5_glu_w, s5_glu_b, ret_log_decay, ret_norm_g, w_branch,
              w_branch_gate, w_out, ffn_w_up, ffn_w_gate, ffn_conv_w, ffn_conv_b, ffn_w_down,
              final_norm_g):
    weights = (norm1_g, norm2_g, ada_w, ada_b, w_in, da_lambda, da_subln_g, na_rpb, s5_a_re,
               s5_a_im, s5_log_dt, s5_b_re, s5_b_im, s5_c_re, s5_c_im, s5_d, s5_glu_w, s5_glu_b,
               ret_log_decay, ret_norm_g, w_branch, w_branch_gate, w_out, ffn_w_up, ffn_w_gate,
               ffn_conv_w, ffn_conv_b, ffn_w_down, final_norm_g)
    y_prompt = trunk(x_prompt, c_prompt, *weights)
    y_sample = trunk(x_sample, c_sample, *weights)
    return (y_prompt, y_sample)
```

```python
import numpy as np
import concourse.bass as bass
import concourse.mybir as mybir

F32 = mybir.dt.float32
BF16 = mybir.dt.bfloat16
AF = mybir.ActivationFunctionType
ALU = mybir.AluOpType
AX = mybir.AxisListType

NDMA_SLOTS = 8


class _Rec:
    def __init__(self):
        self.call = None

    def __getattr__(self, name):
        def f(*a, **kw):
            self.call = (name, a, kw)
            return self
        return f


def freeze(fn):
    r = _Rec()
    fn(r)
    assert r.call is not None
    name, a, kw = r.call
    return lambda e: getattr(e, name)(*a, **kw)


class Res:
    __slots__ = ("w", "r", "name")

    def __init__(self, name=""):
        self.w = {}
        self.r = {}
        self.name = name


class Q:
    def __init__(self, fw, name, self_sync=True, dma=False):
        self.fw = fw
        self.name = name
        self.ops = []
        self.count = 0
        self.semkey = fw.new_sem(name)
        self.waited = {}
        self.self_sync = self_sync
        self.dma_count = 0
        self.dma_keys = [fw.new_sem(f"{name}_d{i}") for i in range(NDMA_SLOTS)] if dma else []


class FW:
    def __init__(self, nc):
        self.nc = nc
        self.sem_names = []
        self.PE = Q(self, "pe", self_sync=False)
        self.DVE = Q(self, "dve")
        self.ACT = Q(self, "act", dma=True)
        self.POOL = Q(self, "pool", dma=True)
        self.SP = Q(self, "sp", dma=True)
        self.queues = [self.PE, self.DVE, self.ACT, self.POOL, self.SP]

    def new_sem(self, name):
        self.sem_names.append(name)
        return len(self.sem_names) - 1

    def _deps(self, q, reads, writes):
        deps = {}
        for r in reads:
            for k, v in r.w.items():
                if deps.get(k, 0) < v:
                    deps[k] = v
        for w in writes:
            for k, v in w.w.items():
                if deps.get(k, 0) < v:
                    deps[k] = v
            for k, v in w.r.items():
                if deps.get(k, 0) < v:
                    deps[k] = v
        for k, v in deps.items():
            if k == q.semkey and not q.self_sync:
                continue
            if q.waited.get(k, 0) < v:
                q.ops.append(("wait", k, v))
                q.waited[k] = v

    def op(self, q, fn, reads=(), writes=()):
        self._deps(q, reads, writes)
        q.count += 1
        q.ops.append(("op", freeze(fn), q.semkey, 1))
        for w in writes:
            w.w[q.semkey] = q.count
        for r in reads:
            r.r[q.semkey] = q.count

    def dma(self, q, fn, reads=(), writes=()):
        i = q.dma_count
        q.dma_count += 1
        slot = i % NDMA_SLOTS
        key = q.dma_keys[slot]
        prev = 16 * (i // NDMA_SLOTS)
        if prev > 0 and q.waited.get(key, 0) < prev:
            q.ops.append(("wait", key, prev))
            q.waited[key] = prev
        self._deps(q, reads, writes)
        val = prev + 16
        q.ops.append(("op", freeze(fn), key, 16))
        for w in writes:
            w.w[key] = val
        for r in reads:
            r.r[key] = val

    def barrier(self):
        targets = {}
        for qq in self.queues:
            if qq.count > 0:
                targets[qq.semkey] = qq.count
            for slot, key in enumerate(qq.dma_keys):
                n = (qq.dma_count - slot + NDMA_SLOTS - 1) // NDMA_SLOTS if qq.dma_count > slot else 0
                if n > 0:
                    targets[key] = 16 * n
        for q in self.queues:
            for k, v in targets.items():
                if q.waited.get(k, 0) < v:
                    q.ops.append(("wait", k, v))
                    q.waited[k] = v

    def finish(self):
        q = self.SP
        for qq in self.queues:
            for slot, key in enumerate(qq.dma_keys):
                n = (qq.dma_count - slot + NDMA_SLOTS - 1) // NDMA_SLOTS if qq.dma_count > slot else 0
                if n > 0 and q.waited.get(key, 0) < 16 * n:
                    q.ops.append(("wait", key, 16 * n))
                    q.waited[key] = 16 * n
            if qq is not q and qq.count > 0 and q.waited.get(qq.semkey, 0) < qq.count:
                q.ops.append(("wait", qq.semkey, qq.count))

    def emit(self, block, sems):
        def run(q, eng):
            for o in q.ops:
                if o[0] == "wait":
                    eng.wait_ge(sems[o[1]], o[2])
                else:
                    o[1](eng).then_inc(sems[o[2]], o[3])

        @block.tensor
        def _(e):
            run(self.PE, e)

        @block.vector
        def _(e):
            run(self.DVE, e)

        @block.scalar
        def _(e):
            run(self.ACT, e)

        @block.gpsimd
        def _(e):
            run(self.POOL, e)

        @block.sync
        def _(e):
            run(self.SP, e)

    def n_instr(self):
        return sum(len(q.ops) for q in self.queues)

import contextlib
import numpy as np
import concourse.bass as bass
import concourse.mybir as mybir
from concourse.bass_utils import run_bass_kernel_spmd


D = 1024
DEPTH = 2
NSEG = 11
DFF = 2816
EPS = 1e-6


def AP(t, offset, ap):
    return bass.AP(tensor=t.tensor if hasattr(t, "tensor") else t, offset=offset, ap=[list(a) for a in ap])


class K:
    def __init__(self, seqs, debug=(), depth=DEPTH, phases=("p0", "p1")):
        self.seqs = list(seqs)
        self.T = sum(seqs)
        self.debug = set(debug)
        self.depth = depth
        self.phases = phases
        self.nc = bass.Bass("TRN2", target_bir_lowering=False)
        self.fw = FW(self.nc)
        self.es = contextlib.ExitStack()
        self.res = {}
        self.dram = {}

    def din(self, name, shape, dt=F32):
        t = self.nc.dram_tensor(name, list(shape), dt, kind="ExternalInput")
        self.dram[name] = t
        self.res[name] = Res(name)
        return t.ap()

    def dscratch(self, name, shape, dt=BF16):
        kind = "ExternalOutput" if name in self.debug else "Internal"
        if name in getattr(self, "ext_in", ()):
            kind = "ExternalInput"
        t = self.nc.dram_tensor(name, list(shape), dt, kind=kind)
        self.dram[name] = t
        self.res[name] = Res(name)
        return t.ap()

    @contextlib.contextmanager
    def phase(self):
        self.pes = contextlib.ExitStack()
        try:
            yield
        finally:
            self.fw.barrier()
            self.pes.close()
            self.pes = None

    def sb(self, name, shape, dt=F32):
        st = self.pes if getattr(self, "pes", None) is not None else self.es
        self._uid = getattr(self, "_uid", 0) + 1
        t = st.enter_context(self.nc.sbuf_tensor(f"{name}_u{self._uid}", list(shape), dt))
        self.res[name] = Res(name)
        return t

    def R(self, name):
        return self.res[name]

    def new_psum(self):
        self.psum = []
        self.psum_res = []
        self.psbig = [self.es.enter_context(self.nc.psum_tensor(f"psbig{i}", [128, 2048], F32)) for i in range(2)]
        for i in range(8):
            t = self.psbig[i // 4][:, (i % 4) * 512:(i % 4 + 1) * 512]
            self.psum.append(t)
            self.psum_res.append(Res(f"ps{i}"))
        self.psum_i = 0

    def ps(self):
        i = self.psum_i % 7
        self.psum_i += 1
        return self.psum[i], self.psum_res[i]

    def build(self):
        nc, fw = self.nc, self.fw
        T = self.T
        dep = self.depth
        self.x = self.din("x", [T, D])
        self.cT = self.din("cT", [128, 8, 3])
        self.y = self.nc.dram_tensor("y", [T, D], F32, kind="ExternalOutput").ap()
        self.res["y"] = Res("y")
        W = {}
        wshapes = {
            "norm1_g": [dep, D], "norm2_g": [dep, D], "ada_w": [dep, D, 6 * D], "ada_b": [dep, 6 * D],
            "w_in": [dep, D, 2816], "da_lambda": [dep, 4, 32], "da_subln_g": [dep, 64],
            "na_rpb": [dep, 4, 15, 31], "s5_a_re": [dep, 2, 16, 64], "s5_a_im": [dep, 2, 16, 64],
            "s5_log_dt": [dep, 2, 16], "s5_b_re": [dep, 2, 16, 64, 16], "s5_b_im": [dep, 2, 16, 64, 16],
            "s5_c_re": [dep, 2, 16, 16, 64], "s5_c_im": [dep, 2, 16, 16, 64], "s5_d": [dep, 256],
            "s5_glu_w": [dep, 256, 256], "s5_glu_b": [dep, 256], "ret_log_decay": [dep, 2, 4],
            "ret_norm_g": [dep, 64], "w_branch": [dep, 4, 256, D], "w_branch_gate": [dep, 4, D, D],
            "w_out": [dep, D, D], "ffn_w_up": [dep, D, DFF], "ffn_w_gate": [dep, D, DFF],
            "ffn_conv_w": [dep, 3, 1, DFF], "ffn_conv_b": [dep, DFF], "ffn_w_down": [dep, DFF, D],
            "final_norm_g": [D],
        }
        for k, s in wshapes.items():
            W[k] = self.din(k, s)
        self.W = W
        Lmax = max(self.seqs)
        self.rope_cos = self.din("rope_cos", [32, Lmax])
        self.rope_sin = self.din("rope_sin", [32, Lmax])
        self.ident_f = self.din("ident_f", [128, 128])
        self.ident_b = self.din("ident_b", [128, 128], BF16)
        self.Wb = {}
        for k, shp in [("w_in", [D, 2816]), ("wg", [4 * D, D]), ("wb", [4 * 256, D]), ("w_out", [D, D]),
                       ("up", [D, DFF]), ("gate", [D, DFF]), ("down", [DFF, D]), ("glu", [256, 256])]:
            for l in range(dep):
                self.Wb[(k, l)] = self.dscratch(f"wb_{k}_{l}", shp, BF16)
        for l in range(dep):
            self.dscratch(f"modd_{l}", [3, 6 * D], F32)
        for n in ["QA", "KA", "QB", "KB", "U", "RQ", "RKT", "RG"]:
            self.dscratch(n, [256, T], BF16)
        for n in ["VA", "VB", "RK", "RV"]:
            self.dscratch(n, [T, 256], BF16)
        for n in ["OA", "OB", "OC", "OD"]:
            self.dscratch(n, [256, T], BF16)
        self.dscratch("XMID", [T, D], F32)
        self.dscratch("XR", [T, D], F32)
        if hasattr(self, "extra_inputs"):
            self.extra_inputs()
        self.new_psum()
        self.idb = self.sb("idb", [128, 128], BF16)
        self.idf = self.sb("idf", [128, 128], F32)
        fw.dma(fw.SP, lambda e: e.dma_start(out=self.idb[:], in_=self.ident_b), [self.R("ident_b")], [self.R("idb")])
        fw.dma(fw.SP, lambda e: e.dma_start(out=self.idf[:], in_=self.ident_f), [self.R("ident_f")], [self.R("idf")])

        for l in range(dep):
            if "p0" in self.phases:
                with self.phase():
                    self.p0(l)
            xin_name = "x" if l == 0 else "XR"
            if "p1" in self.phases:
                with self.phase():
                    self.p1(l, xin_name)
            for ph in ("pA", "pB", "pC", "pD"):
                if ph in self.phases:
                    with self.phase():
                        getattr(self, ph)(l)
            if "p3" in self.phases:
                with self.phase():
                    self.p3(l, xin_name)
            if "p4" in self.phases:
                with self.phase():
                    self.p4(l, "y" if l == dep - 1 else "XR", l == dep - 1)
        fw.finish()
        sems = [self.es.enter_context(nc.semaphore(n)) for n in fw.sem_names]
        block = self.es.enter_context(nc.Block())
        fw.emit(block, sems)
        self.es.close()
        return nc

    def convert(self, src_ap2d, dst_name, l, bufs):
        fw = self.fw
        Rr, C = src_ap2d.shape
        dst = self.Wb[(dst_name, l)]
        rd = self.R(f"wb_{dst_name}_{l}")
        CH = 2048
        for rt in range(Rr // 128):
            for c0 in range(0, C, CH):
                cw = min(CH, C - c0)
                i = self.cv_i
                self.cv_i += 1
                fb, bb = bufs[0][i % 3], bufs[1][i % 3]
                rf, rb = self.R(f"cvf{i % 3}"), self.R(f"cvb{i % 3}")
                src = src_ap2d[rt * 128:(rt + 1) * 128, c0:c0 + cw]
                fw.dma(fw.SP, lambda e, fb=fb, src=src, cw=cw: e.dma_start(out=fb[:, 0:cw], in_=src), [], [rf])
                q = fw.DVE if i % 2 == 0 else fw.POOL
                fw.op(q, lambda e, fb=fb, bb=bb, cw=cw: e.tensor_copy(out=bb[:, 0:cw], in_=fb[:, 0:cw]), [rf], [rb])
                d = dst[rt * 128:(rt + 1) * 128, c0:c0 + cw]
                fw.dma(fw.ACT, lambda e, bb=bb, d=d, cw=cw: e.dma_start(out=d, in_=bb[:, 0:cw]), [rb], [rd])

    def p0(self, l):
        fw, W = self.fw, self.W
        if True:
            self.cvf = [self.sb(f"cvf{i}", [128, 2048], F32) for i in range(3)]
            self.cvb = [self.sb(f"cvb{i}", [128, 2048], BF16) for i in range(3)]
            self.cv_i = 0
            self.csil = self.sb("csil", [128, 8, 3], F32)
            self.adab = self.sb("adab", [3, 6 * D], F32)
            self.adaw = [self.sb(f"adaw{i}", [128, 8, 512], F32) for i in range(2)]
            self.modsb = self.sb("modsb", [3, 512], F32)
            fw.dma(fw.SP, lambda e: e.dma_start(out=self.csil[:], in_=self.cT), [self.R("cT")], [self.R("csil")])
            fw.op(fw.ACT, lambda e: e.activation(out=self.csil[:], in_=self.csil[:], func=AF.Silu),
                  [self.R("csil")], [self.R("csil")])
        bufs = (self.cvf, self.cvb)
        self.convert(W["w_in"][l], "w_in", l, bufs)
        self.convert(W["w_branch_gate"][l].rearrange("a r c -> (a r) c"), "wg", l, bufs)
        self.convert(W["w_branch"][l].rearrange("a r c -> (a r) c"), "wb", l, bufs)
        self.convert(W["w_out"][l], "w_out", l, bufs)
        self.convert(W["ffn_w_up"][l], "up", l, bufs)
        self.convert(W["ffn_w_gate"][l], "gate", l, bufs)
        self.convert(W["ffn_w_down"][l], "down", l, bufs)
        self.convert(W["s5_glu_w"][l], "glu", l, bufs)
        ab = W["ada_b"][l].partition_broadcast(3)
        fw.dma(fw.SP, lambda e: e.dma_start(out=self.adab[:], in_=ab), [], [self.R("adab")])
        modd = self.dram[f"modd_{l}"].ap()
        for cc in range(12):
            wbuf = self.adaw[cc % 2]
            rw = self.R(f"adaw{cc % 2}")
            src = W["ada_w"][l][:, cc * 512:(cc + 1) * 512].rearrange("(kt p) c -> p kt c", p=128)
            fw.dma(fw.SP, lambda e, wbuf=wbuf, src=src: e.dma_start(out=wbuf[:], in_=src), [], [rw])
            pt, pr = self.ps()
            for kt in range(8):
                fw.op(fw.PE, lambda e, pt=pt, wbuf=wbuf, kt=kt: e.matmul(
                    pt[0:3, :], lhsT=self.csil[:, kt, :], rhs=wbuf[:, kt, :], start=(kt == 0), stop=(kt == 7)),
                    [self.R("csil"), rw], [pr])
            fw.op(fw.DVE, lambda e, pt=pt, cc=cc: e.tensor_tensor(
                out=self.modsb[:], in0=pt[0:3, :], in1=self.adab[:, cc * 512:(cc + 1) * 512], op=ALU.add),
                [pr, self.R("adab")], [self.R("modsb")])
            dst = modd[:, cc * 512:(cc + 1) * 512]
            fw.dma(fw.SP, lambda e, dst=dst: e.dma_start(out=dst, in_=self.modsb[:]),
                   [self.R("modsb")], [self.R(f"modd_{l}")])

    def bcast_load(self, dst_tile, dst_res, src_row_ap, src_res):
        self.fw.dma(self.fw.SP, lambda e: e.dma_start(out=dst_tile[:], in_=src_row_ap.partition_broadcast(128)),
                    [src_res], [dst_res])

    def p1(self, l, xin_name):
        fw, W = self.fw, self.W
        TT = 512
        if True:
            self.w_in_sb = self.sb("w_in_sb", [128, 8, 2816], BF16)
            self.w_rot = self.sb("w_rot", [128, 8, 512], BF16)
            self.G1 = self.sb("G1", [128, D], F32)
            self.SH1 = self.sb("SH1", [128, D], F32)
            self.tmpg = self.sb("tmpg", [128, D], F32)
            self.xt = [self.sb(f"xt{i}", [128, D], F32) for i in range(2)]
            self.xn = [self.sb(f"xn{i}", [128, D], F32) for i in range(2)]
            self.hb = [self.sb(f"hb{i}", [128, D], BF16) for i in range(2)]
            self.sq = self.sb("sq", [128, D], F32)
            self.ss = [self.sb(f"ss{i}", [128, 1], F32) for i in range(2)]
            self.rstd = [self.sb(f"rstd{i}", [128, 1], F32) for i in range(2)]
            self.hT = [self.sb(f"hT{i}", [128, 8, TT], BF16) for i in range(2)]
            self.cos_t = [self.sb(f"cos{i}", [128, TT], F32) for i in range(2)]
            self.sin_t = [self.sb(f"sin{i}", [128, TT], F32) for i in range(2)]
            self.stg = [self.sb(f"stg{i}", [128, TT], BF16) for i in range(4)]
            self.rt1 = [self.sb(f"rt1_{i}", [128, TT], F32) for i in range(2)]
            self.rt2 = [self.sb(f"rt2_{i}", [128, TT], F32) for i in range(2)]
            self.stk = [self.sb(f"stk{i}", [128, 4, 256], BF16) for i in range(2)]
            self.stg_i = 0
            self.epsb = self.sb("epsb", [128, 1], F32)
            fw.op(fw.DVE, lambda e: e.memset(self.epsb[:], EPS), [], [self.R("epsb")])
        src = self.Wb[("w_in", l)].rearrange("(kt p) c -> p kt c", p=128)
        fw.dma(fw.SP, lambda e: e.dma_start(out=self.w_in_sb[:], in_=src), [self.R(f"wb_w_in_{l}")], [self.R("w_in_sb")])
        fw.op(fw.POOL, lambda e: e.memset(self.w_rot[:], 0.0), [], [self.R("w_rot")])
        for kt in range(8):
            sv = self.w_in_sb[:, kt, 0:512].rearrange("p (b d) -> p b d", d=32)
            dv = self.w_rot[:, kt, :].rearrange("p (b d) -> p b d", d=32)
            fw.op(fw.DVE, lambda e, sv=sv, dv=dv: e.tensor_scalar(
                out=dv[:, :, 0:4], in0=sv[:, :, 4:8], scalar1=-1.0, scalar2=None, op0=ALU.mult),
                [self.R("w_in_sb")], [self.R("w_rot")])
            fw.op(fw.DVE, lambda e, sv=sv, dv=dv: e.tensor_copy(out=dv[:, :, 4:8], in_=sv[:, :, 0:4]),
                  [self.R("w_in_sb")], [self.R("w_rot")])
        modd = self.dram[f"modd_{l}"].ap()
        rmod = self.R(f"modd_{l}")
        t0 = 0
        it = 0
        for s, L in enumerate(self.seqs):
            self.bcast_load(self.SH1, self.R("SH1"), modd[s, 0:D], rmod)
            self.bcast_load(self.tmpg, self.R("tmpg"), modd[s, D:2 * D], rmod)
            self.bcast_load(self.G1, self.R("G1"), W["norm1_g"][l], self.R("norm1_g"))
            fw.op(fw.DVE, lambda e: e.scalar_tensor_tensor(
                out=self.G1[:], in0=self.tmpg[:], scalar=1.0, in1=self.G1[:], op0=ALU.add, op1=ALU.mult),
                [self.R("tmpg"), self.R("G1")], [self.R("G1")])
            for tt in range(L // TT):
                tok0 = t0 + tt * TT
                pos0 = tt * TT
                b = it % 2
                it += 1
                hT, rhT = self.hT[b], self.R(f"hT{b}")
                self.norm_tile(self.dram[xin_name].ap(), self.R(xin_name), tok0, TT, self.G1, self.SH1, hT, rhT)
                ct, st = self.cos_t[b], self.sin_t[b]
                for blk in range(4):
                    fw.dma(fw.SP, lambda e, ct=ct, blk=blk, pos0=pos0: e.dma_start(
                        out=ct[blk * 32:(blk + 1) * 32, :], in_=self.rope_cos[:, pos0:pos0 + TT]), [], [self.R(f"cos{b}")])
                    fw.dma(fw.SP, lambda e, st=st, blk=blk, pos0=pos0: e.dma_start(
                        out=st[blk * 32:(blk + 1) * 32, :], in_=self.rope_sin[:, pos0:pos0 + TT]), [], [self.R(f"sin{b}")])
                fm = [("QA", 0, "rot"), ("KA", 256, "rot"), ("QB", 768, None), ("KB", 1024, None),
                      ("U", 1536, None), ("RQ", 1792, None), ("RKT", 2048, 0.125), ("RG", 2560, None)]
                for name, c0, mode in fm:
                    for mt in range(2):
                        cc = c0 + mt * 128
                        pt, pr = self.ps()
                        for kt in range(8):
                            fw.op(fw.PE, lambda e, pt=pt, kt=kt, cc=cc, hT=hT: e.matmul(
                                pt[:], lhsT=self.w_in_sb[:, kt, cc:cc + 128], rhs=hT[:, kt, :],
                                start=(kt == 0), stop=(kt == 7)), [self.R("w_in_sb"), rhT], [pr])
                        si = self.stg_i % 4
                        self.stg_i += 1
                        stg, rs = self.stg[si], self.R(f"stg{si}")
                        if mode == "rot":
                            pt2, pr2 = self.ps()
                            for kt in range(8):
                                fw.op(fw.PE, lambda e, pt2=pt2, kt=kt, cc=cc, hT=hT: e.matmul(
                                    pt2[:], lhsT=self.w_rot[:, kt, cc:cc + 128], rhs=hT[:, kt, :],
                                    start=(kt == 0), stop=(kt == 7)), [self.R("w_rot"), rhT], [pr2])
                            r1, r2 = self.rt1[si % 2], self.rt2[si % 2]
                            rr1, rr2 = self.R(f"rt1_{si % 2}"), self.R(f"rt2_{si % 2}")
                            fw.op(fw.DVE, lambda e, pt=pt, r1=r1, ct=ct: e.tensor_tensor(
                                out=r1[:], in0=pt[:], in1=ct[:], op=ALU.mult), [pr, self.R(f"cos{b}")], [rr1])
                            fw.op(fw.DVE, lambda e, pt2=pt2, r2=r2, st=st: e.tensor_tensor(
                                out=r2[:], in0=pt2[:], in1=st[:], op=ALU.mult), [pr2, self.R(f"sin{b}")], [rr2])
                            fw.op(fw.DVE, lambda e, r1=r1, r2=r2, stg=stg: e.tensor_tensor(
                                out=stg[:], in0=r1[:], in1=r2[:], op=ALU.add), [rr1, rr2], [rs])
                        else:
                            sc = 1.0 if mode is None else mode
                            fw.op(fw.ACT, lambda e, pt=pt, stg=stg, sc=sc: e.activation(
                                out=stg[:], in_=pt[:], func=AF.Copy, scale=sc), [pr], [rs])
                        dst = self.dram[name].ap()[mt * 128:(mt + 1) * 128, tok0:tok0 + TT]
                        fw.dma(fw.ACT, lambda e, dst=dst, stg=stg: e.dma_start(out=dst, in_=stg[:]),
                               [rs], [self.R(name)])
                tmn = [("VA", 512, 1.0), ("VB", 1280, 1.0), ("RK", 2048, 0.125), ("RV", 2304, 1.0)]
                for j in range(TT // 128):
                    sk = self.stk[j % 2]
                    rsk = self.R(f"stk{j % 2}")
                    for half in range(2):
                        pt, pr = self.ps()
                        for q2 in range(2):
                            name, c0, sc = tmn[half * 2 + q2]
                            for kt in range(8):
                                fw.op(fw.PE, lambda e, pt=pt, kt=kt, c0=c0, hT=hT, j=j, q2=q2: e.matmul(
                                    pt[:, q2 * 256:(q2 + 1) * 256], lhsT=hT[:, kt, j * 128:(j + 1) * 128],
                                    rhs=self.w_in_sb[:, kt, c0:c0 + 256], start=(kt == 0), stop=(kt == 7)),
                                    [self.R("w_in_sb"), rhT], [pr])
                        for q2 in range(2):
                            name, c0, sc = tmn[half * 2 + q2]
                            fw.op(fw.DVE, lambda e, pt=pt, sk=sk, q2=q2, half=half, sc=sc: e.tensor_scalar(
                                out=sk[:, half * 2 + q2, :], in0=pt[:, q2 * 256:(q2 + 1) * 256], scalar1=sc,
                                scalar2=None, op0=ALU.mult), [pr], [rsk])
                    for qi, (name, c0, sc) in enumerate(tmn):
                        dst = self.dram[name].ap()[tok0 + j * 128: tok0 + (j + 1) * 128, :]
                        fw.dma(fw.SP, lambda e, dst=dst, sk=sk, qi=qi: e.dma_start(out=dst, in_=sk[:, qi, :]),
                               [rsk], [self.R(name)])
            t0 += L

    def norm_tile(self, xsrc, xres, tok0, TT, G, SH, hT, rhT):
        fw = self.fw
        Gr = self.R("G1") if G is self.G1 else self.R("G2")
        SHr = self.R("SH1") if SH is self.SH1 else self.R("SH2")
        for j in range(TT // 128):
            b = j % 2
            xt, rx = self.xt[b], self.R(f"xt{b}")
            xn, rxn = self.xn[b], self.R(f"xn{b}")
            hb, rhb = self.hb[b], self.R(f"hb{b}")
            ss, rss = self.ss[b], self.R(f"ss{b}")
            rstd, rrs = self.rstd[b], self.R(f"rstd{b}")
            src = xsrc[tok0 + j * 128: tok0 + (j + 1) * 128, :]
            fw.dma(fw.SP, lambda e, xt=xt, src=src: e.dma_start(out=xt[:], in_=src), [xres], [rx])
            fw.op(fw.ACT, lambda e, xt=xt, ss=ss: e.activation(
                out=self.sq[:], in_=xt[:], func=AF.Square, accum_out=ss[:]), [rx], [self.R("sq"), rss])
            fw.op(fw.ACT, lambda e, ss=ss, rstd=rstd: e.activation(
                out=rstd[:], in_=ss[:], func=AF.Sqrt, scale=1.0 / D, bias=self.epsb[:]), [rss, self.R("epsb")], [rrs])
            fw.op(fw.DVE, lambda e, rstd=rstd: e.reciprocal(out=rstd[:], in_=rstd[:]), [rrs], [rrs])
            fw.op(fw.DVE, lambda e, xt=xt, xn=xn, rstd=rstd: e.scalar_tensor_tensor(
                out=xn[:], in0=xt[:], scalar=rstd[:], in1=G[:], op0=ALU.mult, op1=ALU.mult), [rx, rrs, Gr], [rxn])
            fw.op(fw.POOL, lambda e, xn=xn, hb=hb: e.tensor_tensor(
                out=hb[:], in0=xn[:], in1=SH[:], op=ALU.add), [rxn, SHr], [rhb])
            pt, pr = self.ps()
            ptb = pt[:].bitcast(BF16)
            for kt in range(8):
                fw.op(fw.PE, lambda e, ptb=ptb, hb=hb, kt=kt: e.transpose(
                    out=ptb[:, kt * 128:(kt + 1) * 128], in_=hb[:, kt * 128:(kt + 1) * 128], identity=self.idb[:]),
                    [rhb, self.R("idb")], [pr])
            fw.op(fw.ACT, lambda e, ptb=ptb, hT=hT, j=j: e.activation(
                out=hT[:, :, j * 128:(j + 1) * 128], in_=ptb.rearrange("p (k t) -> p k t", k=8), func=AF.Copy),
                [pr], [rhT])

import math
import numpy as np
import concourse.bass as bass


GRID_W = 64


class K2(K):
    def p3(self, l, xin_name):
        fw, W = self.fw, self.W
        TT = 512
        xin = self.dram[xin_name].ap()
        xres = self.R(xin_name)
        self.G1 = self.sb("G1", [128, D], F32)
        self.SH1 = self.sb("SH1", [128, D], F32)
        self.tmpg = self.sb("tmpg", [128, D], F32)
        g1g = self.sb("g1g", [128, D], F32)
        self.xt = [self.sb(f"xt{i}", [128, D], F32) for i in range(2)]
        self.xn = [self.sb(f"xn{i}", [128, D], F32) for i in range(2)]
        self.hb = [self.sb(f"hb{i}", [128, D], BF16) for i in range(2)]
        self.sq = self.sb("sq", [128, D], F32)
        self.ss = [self.sb(f"ss{i}", [128, 1], F32) for i in range(2)]
        self.rstd = [self.sb(f"rstd{i}", [128, 1], F32) for i in range(2)]
        self.hT = [self.sb(f"hT{i}", [128, 8, TT], BF16) for i in range(2)]
        self.epsb = self.sb("epsb", [128, 1], F32)
        fw.op(fw.DVE, lambda e: e.memset(self.epsb[:], EPS), [], [self.R("epsb")])
        wout = self.sb("wout", [128, 8, D], BF16)
        wgc = [self.sb(f"wgc{i}", [128, 8, 512], BF16) for i in range(3)]
        wbc = [self.sb(f"wbc{i}", [128, 2, D], BF16) for i in range(2)]
        oT = [self.sb(f"oT{i}", [128, 2, TT], BF16) for i in range(2)]
        gt = [self.sb(f"gt{i}", [128, TT], BF16) for i in range(2)]
        acc = self.sb("acc", [128, 8, TT], F32)
        tmpm = [self.sb(f"tmpm{i}", [128, TT], F32) for i in range(2)]
        mT = self.sb("mT", [128, 8, TT], BF16)
        xo = [self.sb(f"xo{i}", [128, 512], F32) for i in range(2)]
        xr = [self.sb(f"xr{i}", [128, D], F32) for i in range(2)]
        fw.dma(fw.SP, lambda e: e.dma_start(out=wout[:], in_=self.Wb[("w_out", l)].rearrange("(kt p) c -> p kt c", p=128)),
               [self.R(f"wb_w_out_{l}")], [self.R("wout")])
        modd = self.dram[f"modd_{l}"].ap()
        rmod = self.R(f"modd_{l}")
        onames = ["OA", "OB", "OC", "OD"]
        t0 = 0
        it = 0
        wg_i = 0
        for s, L in enumerate(self.seqs):
            self.bcast_load(self.SH1, self.R("SH1"), modd[s, 0:D], rmod)
            self.bcast_load(self.tmpg, self.R("tmpg"), modd[s, D:2 * D], rmod)
            self.bcast_load(self.G1, self.R("G1"), W["norm1_g"][l], self.R("norm1_g"))
            fw.op(fw.DVE, lambda e: e.scalar_tensor_tensor(
                out=self.G1[:], in0=self.tmpg[:], scalar=1.0, in1=self.G1[:], op0=ALU.add, op1=ALU.mult),
                [self.R("tmpg"), self.R("G1")], [self.R("G1")])
            self.bcast_load(g1g, self.R("g1g"), modd[s, 2 * D:3 * D], rmod)
            for tt in range(L // TT):
                tok0 = t0 + tt * TT
                b = it % 2
                it += 1
                hT, rhT = self.hT[b], self.R(f"hT{b}")
                self.norm_tile(xin, xres, tok0, TT, self.G1, self.SH1, hT, rhT)
                for i in range(4):
                    wb_, rwb = wbc[i % 2], self.R(f"wbc{i % 2}")
                    src = self.Wb[("wb", l)][i * 256:(i + 1) * 256, :].rearrange("(kt p) c -> p kt c", p=128)
                    fw.dma(fw.SP, lambda e, wb_=wb_, src=src: e.dma_start(out=wb_[:], in_=src),
                           [self.R(f"wb_wb_{l}")], [rwb])
                    o_, ro = oT[i % 2], self.R(f"oT{i % 2}")
                    osrc = self.dram[onames[i]].ap()[:, tok0:tok0 + TT].rearrange("(kt p) t -> p kt t", p=128)
                    fw.dma(fw.SP, lambda e, o_=o_, osrc=osrc: e.dma_start(out=o_[:], in_=osrc),
                           [self.R(onames[i])], [ro])
                    for half in range(2):
                        wg_, rwg = wgc[wg_i % 3], self.R(f"wgc{wg_i % 3}")
                        wg_i += 1
                        src = self.Wb[("wg", l)][i * D:(i + 1) * D, half * 512:(half + 1) * 512].rearrange(
                            "(kt p) c -> p kt c", p=128)
                        fw.dma(fw.SP, lambda e, wg_=wg_, src=src: e.dma_start(out=wg_[:], in_=src),
                               [self.R(f"wb_wg_{l}")], [rwg])
                        for m4 in range(4):
                            mt = half * 4 + m4
                            pg, prg = self.ps()
                            for kt in range(8):
                                fw.op(fw.PE, lambda e, pg=pg, wg_=wg_, kt=kt, m4=m4, hT=hT: e.matmul(
                                    pg[:], lhsT=wg_[:, kt, m4 * 128:(m4 + 1) * 128], rhs=hT[:, kt, :],
                                    start=(kt == 0), stop=(kt == 7)), [rwg, rhT], [prg])
                            pb, prb = self.ps()
                            for k2 in range(2):
                                fw.op(fw.PE, lambda e, pb=pb, wb_=wb_, k2=k2, mt=mt, o_=o_: e.matmul(
                                    pb[:], lhsT=wb_[:, k2, mt * 128:(mt + 1) * 128], rhs=o_[:, k2, :],
                                    start=(k2 == 0), stop=(k2 == 1)), [rwb, ro], [prb])
                            g_, rg_ = gt[mt % 2], self.R(f"gt{mt % 2}")
                            fw.op(fw.ACT, lambda e, pg=pg, g_=g_: e.activation(out=g_[:], in_=pg[:], func=AF.Sigmoid),
                                  [prg], [rg_])
                            if i == 0:
                                fw.op(fw.DVE, lambda e, pb=pb, g_=g_, mt=mt: e.tensor_tensor(
                                    out=acc[:, mt, :], in0=pb[:], in1=g_[:], op=ALU.mult), [prb, rg_], [self.R("acc")])
                            else:
                                tm, rtm = tmpm[mt % 2], self.R(f"tmpm{mt % 2}")
                                fw.op(fw.DVE, lambda e, pb=pb, g_=g_, tm=tm: e.tensor_tensor(
                                    out=tm[:], in0=pb[:], in1=g_[:], op=ALU.mult), [prb, rg_], [rtm])
                                if i < 3:
                                    fw.op(fw.POOL, lambda e, tm=tm, mt=mt: e.tensor_tensor(
                                        out=acc[:, mt, :], in0=acc[:, mt, :], in1=tm[:], op=ALU.add),
                                        [rtm, self.R("acc")], [self.R("acc")])
                                else:
                                    fw.op(fw.POOL, lambda e, tm=tm, mt=mt: e.tensor_tensor(
                                        out=mT[:, mt, :], in0=acc[:, mt, :], in1=tm[:], op=ALU.add),
                                        [rtm, self.R("acc")], [self.R("mT")])
                for j in range(TT // 128):
                    xr_, rxr = xr[j % 2], self.R(f"xr{j % 2}")
                    src = xin[tok0 + j * 128: tok0 + (j + 1) * 128, :]
                    fw.dma(fw.SP, lambda e, xr_=xr_, src=src: e.dma_start(out=xr_[:], in_=src), [xres], [rxr])
                    for ch in range(2):
                        po, pro = self.ps()
                        for kt in range(8):
                            fw.op(fw.PE, lambda e, po=po, kt=kt, j=j, ch=ch: e.matmul(
                                po[:], lhsT=mT[:, kt, j * 128:(j + 1) * 128], rhs=wout[:, kt, ch * 512:(ch + 1) * 512],
                                start=(kt == 0), stop=(kt == 7)), [self.R("mT"), self.R("wout")], [pro])
                        xo_, rxo = xo[ch], self.R(f"xo{ch}")
                        fw.op(fw.DVE, lambda e, po=po, xo_=xo_, ch=ch: e.tensor_tensor(
                            out=xo_[:], in0=po[:], in1=g1g[:, ch * 512:(ch + 1) * 512], op=ALU.mult),
                            [pro, self.R("g1g")], [rxo])
                        fw.op(fw.POOL, lambda e, xo_=xo_, xr_=xr_, ch=ch: e.tensor_tensor(
                            out=xo_[:], in0=xo_[:], in1=xr_[:, ch * 512:(ch + 1) * 512], op=ALU.add), [rxo, rxr], [rxo])
                        dst = self.dram["XMID"].ap()[tok0 + j * 128: tok0 + (j + 1) * 128, ch * 512:(ch + 1) * 512]
                        fw.dma(fw.ACT, lambda e, dst=dst, xo_=xo_: e.dma_start(out=dst, in_=xo_[:]), [rxo], [self.R("XMID")])
            t0 += L

    def p4(self, l, xout_name, final):
        fw, W = self.fw, self.W
        TT = 512
        xin = self.dram["XMID"].ap()
        xres = self.R("XMID")
        self.G2 = self.sb("G2", [128, D], F32)
        self.SH2 = self.sb("SH2", [128, D], F32)
        self.G1, self.SH1 = None, None
        self.tmpg = self.sb("tmpg", [128, D], F32)
        g2g = self.sb("g2g", [128, D], F32)
        fng = self.sb("fng", [128, D], F32)
        self.xt = [self.sb(f"xt{i}", [128, D], F32) for i in range(2)]
        self.xn = [self.sb(f"xn{i}", [128, D], F32) for i in range(2)]
        self.hb = [self.sb(f"hb{i}", [128, D], BF16) for i in range(2)]
        self.sq = self.sb("sq", [128, D], F32)
        self.ss = [self.sb(f"ss{i}", [128, 1], F32) for i in range(2)]
        self.rstd = [self.sb(f"rstd{i}", [128, 1], F32) for i in range(2)]
        hTl = [self.sb(f"hT{i}", [128, 8, TT + 2], BF16) for i in range(2)]
        self.epsb = self.sb("epsb", [128, 1], F32)
        fw.op(fw.DVE, lambda e: e.memset(self.epsb[:], EPS), [], [self.R("epsb")])
        wdn = self.sb("wdn", [128, 22, D], BF16)
        wuc = [self.sb(f"wuc{i}", [128, 8, 512], BF16) for i in range(2)]
        wgc = [self.sb(f"wgc{i}", [128, 8, 512], BF16) for i in range(2)]
        cw = self.sb("cw", [128, 22, 3], F32)
        cb = self.sb("cb", [128, 22], F32)
        aext = [self.sb(f"aext{i}", [128, TT + 2], F32) for i in range(2)]
        cv = [self.sb(f"cv{i}", [128, TT], F32) for i in range(2)]
        ge = [self.sb(f"ge{i}", [128, TT], F32) for i in range(2)]
        uT = self.sb("uT", [128, 22, TT], BF16)
        xo = [self.sb(f"xo{i}", [128, 512], F32) for i in range(2)]
        xr = [self.sb(f"xr{i}", [128, D], F32) for i in range(2)]
        yo = [self.sb(f"yo{i}", [128, D], F32) for i in range(2)]
        fw.dma(fw.SP, lambda e: e.dma_start(out=wdn[:], in_=self.Wb[("down", l)].rearrange("(kt p) c -> p kt c", p=128)),
               [self.R(f"wb_down_{l}")], [self.R("wdn")])
        with self.nc.allow_non_contiguous_dma(reason="small conv params"):
            pass
        for k3 in range(3):
            src = W["ffn_conv_w"][l][k3, 0, :].rearrange("(mt p) -> p mt", p=128)
            fw.dma(fw.SP, lambda e, src=src, k3=k3: e.dma_start(out=cw[:, :, k3], in_=src, allow_slow_non_contiguous=True), [], [self.R("cw")])
        src = W["ffn_conv_b"][l].rearrange("(mt p) -> p mt", p=128)
        fw.dma(fw.SP, lambda e, src=src: e.dma_start(out=cb[:], in_=src, allow_slow_non_contiguous=True), [], [self.R("cb")])
        if final:
            self.bcast_load(fng, self.R("fng"), W["final_norm_g"], self.R("final_norm_g"))
        modd = self.dram[f"modd_{l}"].ap()
        rmod = self.R(f"modd_{l}")
        xout = self.dram[xout_name].ap() if xout_name != "y" else self.y
        rout = self.R(xout_name)
        t0 = 0
        it = 0
        wi = 0
        for s, L in enumerate(self.seqs):
            self.bcast_load(self.SH2, self.R("SH2"), modd[s, 3 * D:4 * D], rmod)
            self.bcast_load(self.tmpg, self.R("tmpg"), modd[s, 4 * D:5 * D], rmod)
            self.bcast_load(self.G2, self.R("G2"), W["norm2_g"][l], self.R("norm2_g"))
            fw.op(fw.DVE, lambda e: e.scalar_tensor_tensor(
                out=self.G2[:], in0=self.tmpg[:], scalar=1.0, in1=self.G2[:], op0=ALU.add, op1=ALU.mult),
                [self.R("tmpg"), self.R("G2")], [self.R("G2")])
            self.bcast_load(g2g, self.R("g2g"), modd[s, 5 * D:6 * D], rmod)
            ntile = L // TT
            for tt in range(ntile):
                tok0 = t0 + tt * TT
                b = it % 2
                it += 1
                hT, rhT = hTl[b], self.R(f"hT{b}")
                self.norm_tile(xin, xres, tok0, TT, self.G2, self.SH2, hT, rhT)
                pidx = max(tok0 - 1, t0)
                nidx = min(tok0 + TT, t0 + L - 1)
                self.norm_halo(xin, xres, pidx, nidx, self.G2, self.SH2, hT, rhT, TT)
                has_prev = tt > 0
                has_next = tt < ntile - 1
                for mt in range(22):
                    c4 = mt % 4
                    if c4 == 0:
                        wu_, rwu = wuc[wi % 2], self.R(f"wuc{wi % 2}")
                        wg_, rwg = wgc[wi % 2], self.R(f"wgc{wi % 2}")
                        wi += 1
                        ncol = min(512, DFF - mt * 128)
                        srcu = self.Wb[("up", l)][:, mt * 128: mt * 128 + ncol].rearrange("(kt p) c -> p kt c", p=128)
                        srcg = self.Wb[("gate", l)][:, mt * 128: mt * 128 + ncol].rearrange("(kt p) c -> p kt c", p=128)
                        fw.dma(fw.SP, lambda e, wu_=wu_, srcu=srcu, ncol=ncol: e.dma_start(out=wu_[:, :, 0:ncol], in_=srcu),
                               [self.R(f"wb_up_{l}")], [rwu])
                        fw.dma(fw.SP, lambda e, wg_=wg_, srcg=srcg, ncol=ncol: e.dma_start(out=wg_[:, :, 0:ncol], in_=srcg),
                               [self.R(f"wb_gate_{l}")], [rwg])
                    pa, pra = self.ps()
                    for kt in range(8):
                        fw.op(fw.PE, lambda e, pa=pa, wu_=wu_, kt=kt, c4=c4, hT=hT: e.matmul(
                            pa[:], lhsT=wu_[:, kt, c4 * 128:(c4 + 1) * 128], rhs=hT[:, kt, 0:TT],
                            start=(kt == 0), stop=(kt == 7)), [rwu, rhT], [pra])
                    ph, prh = self.ps()
                    for kt in range(8):
                        fw.op(fw.PE, lambda e, ph=ph, wu_=wu_, kt=kt, c4=c4, hT=hT: e.matmul(
                            ph[:, 0:2], lhsT=wu_[:, kt, c4 * 128:(c4 + 1) * 128], rhs=hT[:, kt, TT:TT + 2],
                            start=(kt == 0), stop=(kt == 7)), [rwu, rhT], [prh])
                    pg, prg = self.ps()
                    for kt in range(8):
                        fw.op(fw.PE, lambda e, pg=pg, wg_=wg_, kt=kt, c4=c4, hT=hT: e.matmul(
                            pg[:], lhsT=wg_[:, kt, c4 * 128:(c4 + 1) * 128], rhs=hT[:, kt, 0:TT],
                            start=(kt == 0), stop=(kt == 7)), [rwg, rhT], [prg])
                    ae, rae = aext[mt % 2], self.R(f"aext{mt % 2}")
                    fw.op(fw.ACT, lambda e, pa=pa, ae=ae: e.activation(out=ae[:, 1:TT + 1], in_=pa[:], func=AF.Copy),
                          [pra], [rae])
                    if has_prev:
                        fw.op(fw.ACT, lambda e, ph=ph, ae=ae: e.activation(out=ae[:, 0:1], in_=ph[:, 0:1], func=AF.Copy),
                              [prh], [rae])
                    else:
                        fw.op(fw.ACT, lambda e, ae=ae: e.memzero(ae[:, 0:1]) if False else e.activation(
                            out=ae[:, 0:1], in_=ae[:, 1:2], func=AF.Copy, scale=0.0), [rae], [rae])
                    if has_next:
                        fw.op(fw.ACT, lambda e, ph=ph, ae=ae: e.activation(
                            out=ae[:, TT + 1:TT + 2], in_=ph[:, 1:2], func=AF.Copy), [prh], [rae])
                    else:
                        fw.op(fw.ACT, lambda e, ae=ae: e.activation(
                            out=ae[:, TT + 1:TT + 2], in_=ae[:, 1:2], func=AF.Copy, scale=0.0), [rae], [rae])
                    cv_, rcv = cv[mt % 2], self.R(f"cv{mt % 2}")
                    fw.op(fw.DVE, lambda e, ae=ae, cv_=cv_, mt=mt: e.tensor_scalar(
                        out=cv_[:], in0=ae[:, 0:TT], scalar1=cw[:, mt, 0:1], scalar2=cb[:, mt:mt + 1],
                        op0=ALU.mult, op1=ALU.add), [rae, self.R("cw"), self.R("cb")], [rcv])
                    fw.op(fw.DVE, lambda e, ae=ae, cv_=cv_, mt=mt: e.scalar_tensor_tensor(
                        out=cv_[:], in0=ae[:, 1:TT + 1], scalar=cw[:, mt, 1:2], in1=cv_[:], op0=ALU.mult, op1=ALU.add),
                        [rae, self.R("cw"), rcv], [rcv])
                    fw.op(fw.DVE, lambda e, ae=ae, cv_=cv_, mt=mt: e.scalar_tensor_tensor(
                        out=cv_[:], in0=ae[:, 2:TT + 2], scalar=cw[:, mt, 2:3], in1=cv_[:], op0=ALU.mult, op1=ALU.add),
                        [rae, self.R("cw"), rcv], [rcv])
                    ge_, rge = ge[mt % 2], self.R(f"ge{mt % 2}")
                    fw.op(fw.ACT, lambda e, cv_=cv_, ge_=ge_: e.activation(out=ge_[:], in_=cv_[:], func=AF.Gelu_apprx_tanh),
                          [rcv], [rge])
                    fw.op(fw.DVE, lambda e, pg=pg, ge_=ge_, mt=mt: e.tensor_tensor(
                        out=uT[:, mt, :], in0=pg[:], in1=ge_[:], op=ALU.mult), [prg, rge], [self.R("uT")])
                for j in range(TT // 128):
                    xr_, rxr = xr[j % 2], self.R(f"xr{j % 2}")
                    src = xin[tok0 + j * 128: tok0 + (j + 1) * 128, :]
                    fw.dma(fw.SP, lambda e, xr_=xr_, src=src: e.dma_start(out=xr_[:], in_=src), [xres], [rxr])
                    yo_, ryo = yo[j % 2], self.R(f"yo{j % 2}")
                    for ch in range(2):
                        po, pro = self.ps()
                        for mt in range(22):
                            fw.op(fw.PE, lambda e, po=po, mt=mt, j=j, ch=ch: e.matmul(
                                po[:], lhsT=uT[:, mt, j * 128:(j + 1) * 128], rhs=wdn[:, mt, ch * 512:(ch + 1) * 512],
                                start=(mt == 0), stop=(mt == 21)), [self.R("uT"), self.R("wdn")], [pro])
                        xo_, rxo = xo[ch], self.R(f"xo{ch}")
                        fw.op(fw.DVE, lambda e, po=po, xo_=xo_, ch=ch: e.tensor_tensor(
                            out=xo_[:], in0=po[:], in1=g2g[:, ch * 512:(ch + 1) * 512], op=ALU.mult),
                            [pro, self.R("g2g")], [rxo])
                        fw.op(fw.POOL, lambda e, xo_=xo_, xr_=xr_, yo_=yo_, ch=ch: e.tensor_tensor(
                            out=yo_[:, ch * 512:(ch + 1) * 512], in0=xo_[:], in1=xr_[:, ch * 512:(ch + 1) * 512], op=ALU.add),
                            [rxo, rxr], [ryo])
                    if final:
                        ss, rss = self.ss[j % 2], self.R(f"ss{j % 2}")
                        rstd, rrs = self.rstd[j % 2], self.R(f"rstd{j % 2}")
                        fw.op(fw.ACT, lambda e, yo_=yo_, ss=ss: e.activation(
                            out=self.sq[:], in_=yo_[:], func=AF.Square, accum_out=ss[:]), [ryo], [self.R("sq"), rss])
                        fw.op(fw.ACT, lambda e, ss=ss, rstd=rstd: e.activation(
                            out=rstd[:], in_=ss[:], func=AF.Sqrt, scale=1.0 / D, bias=self.epsb[:]),
                            [rss, self.R("epsb")], [rrs])
                        fw.op(fw.DVE, lambda e, rstd=rstd: e.reciprocal(out=rstd[:], in_=rstd[:]), [rrs], [rrs])
                        fw.op(fw.DVE, lambda e, yo_=yo_, rstd=rstd: e.scalar_tensor_tensor(
                            out=yo_[:], in0=yo_[:], scalar=rstd[:], in1=fng[:], op0=ALU.mult, op1=ALU.mult),
                            [ryo, rrs, self.R("fng")], [ryo])
                    dst = xout[tok0 + j * 128: tok0 + (j + 1) * 128, :]
                    fw.dma(fw.ACT, lambda e, dst=dst, yo_=yo_: e.dma_start(out=dst, in_=yo_[:]), [ryo], [rout])
            t0 += L

    def norm_halo(self, xsrc, xres, pidx, nidx, G, SH, hT, rhT, TT):
        fw = self.fw
        Gr = self.R("G2")
        SHr = self.R("SH2")
        b = 0
        xt, rx = self.xt[b], self.R(f"xt{b}")
        xn, rxn = self.xn[b], self.R(f"xn{b}")
        hb, rhb = self.hb[b], self.R(f"hb{b}")
        ss, rss = self.ss[b], self.R(f"ss{b}")
        rstd, rrs = self.rstd[b], self.R(f"rstd{b}")
        fw.dma(fw.SP, lambda e: e.dma_start(out=xt[0:1, :], in_=xsrc[pidx:pidx + 1, :]), [xres], [rx])
        fw.dma(fw.SP, lambda e: e.dma_start(out=xt[1:2, :], in_=xsrc[nidx:nidx + 1, :]), [xres], [rx])
        fw.op(fw.ACT, lambda e: e.activation(out=self.sq[0:2, :], in_=xt[0:2, :], func=AF.Square, accum_out=ss[0:2, :]),
              [rx], [self.R("sq"), rss])
        fw.op(fw.ACT, lambda e: e.activation(out=rstd[0:2, :], in_=ss[0:2, :], func=AF.Sqrt, scale=1.0 / D,
                                             bias=self.epsb[0:2, :]), [rss, self.R("epsb")], [rrs])
        fw.op(fw.DVE, lambda e: e.reciprocal(out=rstd[0:2, :], in_=rstd[0:2, :]), [rrs], [rrs])
        fw.op(fw.DVE, lambda e: e.scalar_tensor_tensor(out=xn[0:2, :], in0=xt[0:2, :], scalar=rstd[0:2, :], in1=G[0:2, :],
                                                       op0=ALU.mult, op1=ALU.mult), [rx, rrs, Gr], [rxn])
        fw.op(fw.POOL, lambda e: e.tensor_tensor(out=hb[0:2, :], in0=xn[0:2, :], in1=SH[0:2, :], op=ALU.add),
              [rxn, SHr], [rhb])
        pt, pr = self.ps()
        ptb = pt[:].bitcast(BF16)
        for kt in range(8):
            fw.op(fw.PE, lambda e, kt=kt: e.transpose(out=ptb[:, kt * 2:(kt + 1) * 2], in_=hb[0:2, kt * 128:(kt + 1) * 128],
                                                      identity=self.idb[0:2, 0:2]), [rhb, self.R("idb")], [pr])
        fw.op(fw.ACT, lambda e: e.activation(out=hT[:, :, TT:TT + 2], in_=ptb[:, 0:16].rearrange("p (k t) -> p k t", k=8),
                                             func=AF.Copy), [pr], [rhT])

import math
import numpy as np
import concourse.bass as bass


class K3(K2):
    def extra_inputs(self):
        self.gmask = self.din("gmask", [128, 4])
        self.ret_rel = self.din("ret_rel", [128, 128])
        self.ret_col4 = self.din("ret_col4", [128, 512])
        self.ret_pidx = self.din("ret_pidx", [128, 1])

    def rms_epilogue(self, src_ap, src_res, nrows, scale_col, extra_mul, out_bf, out_res, tag):
        fw = self.fw
        sq, rs = self._sq, self._rs
        pn, prn = self.psum[7], self.psum_res[7]
        fw.op(fw.ACT, lambda e: e.activation(out=sq[0:64, :], in_=src_ap, func=AF.Square), [src_res], [self.R("nsq")])
        fw.op(fw.PE, lambda e: e.matmul(pn[0:64, :], lhsT=self._ones[0:64, 0:64], rhs=sq[0:64, :], start=True, stop=True),
              [self.R("nsq"), self.R("nones")], [prn])
        fw.op(fw.ACT, lambda e: e.activation(out=rs[0:64, :], in_=pn[0:64, :], func=AF.Sqrt, scale=1.0 / 64,
                                             bias=self._eps[0:64, :]), [prn, self.R("neps")], [self.R("nrs")])
        fw.op(fw.DVE, lambda e: e.reciprocal(out=rs[0:64, :], in_=rs[0:64, :]), [self.R("nrs")], [self.R("nrs")])
        if extra_mul is None:
            fw.op(fw.DVE, lambda e: e.scalar_tensor_tensor(out=out_bf, in0=src_ap, scalar=scale_col, in1=rs[0:64, :],
                                                           op0=ALU.mult, op1=ALU.mult), [src_res, self.R("nrs")], [out_res])
        else:
            em, emr = extra_mul
            fw.op(fw.DVE, lambda e: e.scalar_tensor_tensor(out=sq[0:64, :], in0=src_ap, scalar=scale_col, in1=rs[0:64, :],
                                                           op0=ALU.mult, op1=ALU.mult), [src_res, self.R("nrs")], [self.R("nsq")])
            fw.op(fw.DVE, lambda e: e.tensor_tensor(out=out_bf, in0=sq[0:64, :], in1=em, op=ALU.mult),
                  [self.R("nsq"), emr], [out_res])

    def norm_consts(self):
        fw = self.fw
        self._sq = self.sb("nsq", [128, 512], F32)
        self._rs = self.sb("nrs", [128, 512], F32)
        self._ones = self.sb("nones", [128, 64], F32)
        self._eps = self.sb("neps", [128, 1], F32)
        fw.op(fw.DVE, lambda e: e.memset(self._ones[:], 1.0), [], [self.R("nones")])
        fw.op(fw.DVE, lambda e: e.memset(self._eps[:], EPS), [], [self.R("neps")])

    def pA(self, l):
        fw, W = self.fw, self.W
        lam_init = 0.8 - 0.6 * math.exp(-0.3 * l)
        self.norm_consts()
        Lmax = max(self.seqs)
        nkt_max = Lmax // 128
        dl = self.sb("dl", [128, 128], F32)
        dlp = self.sb("dlp", [128, 2, 32], F32)
        s12 = self.sb("s12", [128, 2], F32)
        neglam = self.sb("neglam", [128, 1], F32)
        gsub = self.sb("gsub", [64, 1], F32)
        mk = self.sb("mk", [128, 4], F32)
        KT = self.sb("KT", [128, Lmax], BF16)
        VA = self.sb("VAs", [128, nkt_max, 128], BF16)
        QT = [self.sb(f"QT{i}", [128, 512], BF16) for i in range(2)]
        Qm = [[self.sb(f"Qm{i}_{c}", [128, 512], BF16) for c in range(2)] for i in range(2)]
        PT = [self.sb(f"PT{i}", [128, 1024], BF16) for i in range(2)]
        rec = self.sb("rec", [64, 512], F32)
        oc = [self.sb(f"oc{c}", [64, 512], F32) for c in range(2)]
        od = self.sb("od", [64, 512], F32)
        ob = [self.sb(f"ob{i}", [64, 512], BF16) for i in range(2)]
        fw.dma(fw.SP, lambda e: e.dma_start(out=dl[:], in_=W["da_lambda"][l].rearrange("a b -> (a b)").partition_broadcast(128)),
               [], [self.R("dl")])
        fw.dma(fw.SP, lambda e: e.dma_start(out=mk[:], in_=self.gmask), [], [self.R("mk")])
        fw.dma(fw.SP, lambda e: e.dma_start(out=gsub[:], in_=W["da_subln_g"][l].rearrange("(v o) -> v o", o=1)),
               [], [self.R("gsub")])
        fw.op(fw.DVE, lambda e: e.tensor_scalar(out=gsub[:], in0=gsub[:], scalar1=1.0 - lam_init, scalar2=None, op0=ALU.mult),
              [self.R("gsub")], [self.R("gsub")])
        dv = dl[:].rearrange("p (a b) -> p a b", b=32)
        for i in range(2):
            fw.op(fw.DVE, lambda e, i=i: e.tensor_tensor(out=dlp[:, i, :], in0=dv[:, 2 * i, :], in1=dv[:, 2 * i + 1, :], op=ALU.mult),
                  [self.R("dl")], [self.R("dlp")])
            fw.op(fw.DVE, lambda e, i=i: e.reduce_sum(out=s12[:, i:i + 1], in_=dlp[:, i, :], axis=AX.X),
                  [self.R("dlp")], [self.R("s12")])
        fw.op(fw.ACT, lambda e: e.activation(out=s12[:], in_=s12[:], func=AF.Exp), [self.R("s12")], [self.R("s12")])
        fw.op(fw.DVE, lambda e: e.tensor_tensor(out=neglam[:], in0=s12[:, 1:2], in1=s12[:, 0:1], op=ALU.subtract),
              [self.R("s12")], [self.R("neglam")])
        fw.op(fw.DVE, lambda e: e.tensor_scalar(out=neglam[:], in0=neglam[:], scalar1=-lam_init, scalar2=None, op0=ALU.add),
              [self.R("neglam")], [self.R("neglam")])
        fw.op(fw.POOL, lambda e: e.memset(VA[:, :, 64:128], 1.0), [], [self.R("VAs")])
        scale = 32 ** -0.5
        t0 = 0
        qi = 0
        self._pi = 0
        for s, L in enumerate(self.seqs):
            nkt = L // 128
            for h in range(4):
                th, g0 = h // 2, (h % 2) * 2
                if h % 2 == 0:
                    fw.dma(fw.SP, lambda e, th=th, t0=t0, L=L: e.dma_start(
                        out=KT[:, 0:L], in_=self.dram["KA"].ap()[th * 128:(th + 1) * 128, t0:t0 + L]), [self.R("KA")], [self.R("KT")])
                vsrc = self.dram["VA"].ap()[t0:t0 + L, h * 64:(h + 1) * 64].rearrange("(kt p) v -> p kt v", p=128)
                fw.dma(fw.SP, lambda e, vsrc=vsrc, nkt=nkt: e.dma_start(out=VA[:, 0:nkt, 0:64], in_=vsrc),
                       [self.R("VA")], [self.R("VAs")])
                s1, s2 = [], []
                for qt in range(L // 512):
                    tok0 = t0 + qt * 512
                    qb = qi % 2
                    qi += 1

                    def load_q(qb=qb, tok0=tok0):
                        fw.dma(fw.SP, lambda e: e.dma_start(
                            out=QT[qb][:], in_=self.dram["QA"].ap()[th * 128:(th + 1) * 128, tok0:tok0 + 512]),
                            [self.R("QA")], [self.R(f"QT{qb}")])
                        for c in range(2):
                            eng = fw.POOL if c == 0 else fw.DVE
                            fw.op(eng, lambda e, c=c: e.tensor_scalar(
                                out=Qm[qb][c][:], in0=QT[qb][:], scalar1=mk[:, g0 + c:g0 + c + 1], scalar2=None, op0=ALU.mult),
                                [self.R(f"QT{qb}"), self.R("mk")], [self.R(f"Qm{qb}_{c}")])

                    for c in range(2):
                        for k2 in range(nkt // 2):
                            info = {}

                            def st1(c=c, k2=k2, qb=qb, info=info, first=(c == 0 and k2 == 0), load_q=load_q):
                                if first:
                                    load_q()
                                pi = self._pi
                                self._pi += 1
                                info["sb0"] = (pi % 2) * 2
                                info["pb"] = pi % 2
                                sb0 = info["sb0"]
                                for u in range(2):
                                    kt = 2 * k2 + u
                                    fw.op(fw.PE, lambda e, u=u, kt=kt: e.matmul(
                                        self.psum[sb0 + u][:], lhsT=KT[:, kt * 128:(kt + 1) * 128], rhs=Qm[qb][c][:],
                                        start=True, stop=True), [self.R("KT"), self.R(f"Qm{qb}_{c}")], [self.psum_res[sb0 + u]])

                            def st2(c=c, k2=k2, qb=qb, info=info, tok0=tok0, last=(k2 == nkt // 2 - 1)):
                                sb0, pb = info["sb0"], info["pb"]
                                po, pro = self.psum[4 + c], self.psum_res[4 + c]
                                fw.op(fw.ACT, lambda e: e.activation(
                                    out=PT[pb][:], in_=self.psbig[0][:, sb0 * 512:(sb0 + 2) * 512], func=AF.Exp, scale=scale),
                                    [self.psum_res[sb0], self.psum_res[sb0 + 1]], [self.R(f"PT{pb}")])
                                for u in range(2):
                                    kt = 2 * k2 + u
                                    fw.op(fw.PE, lambda e, kt=kt, u=u: e.matmul(
                                        po[:], lhsT=VA[:, kt, :], rhs=PT[pb][:, u * 512:(u + 1) * 512],
                                        start=(kt == 0), stop=(kt == nkt - 1)), [self.R("VAs"), self.R(f"PT{pb}")], [pro])
                                if last:
                                    fw.op(fw.DVE, lambda e: e.reciprocal(out=rec[:], in_=po[64:128, :]), [pro], [self.R("rec")])
                                    fw.op(fw.DVE, lambda e: e.tensor_tensor(out=oc[c][:], in0=po[0:64, :], in1=rec[:], op=ALU.mult),
                                          [pro, self.R("rec")], [self.R(f"oc{c}")])
                                    if c == 1:
                                        fw.op(fw.DVE, lambda e: e.scalar_tensor_tensor(
                                            out=od[:], in0=oc[1][:], scalar=neglam[0:64, :], in1=oc[0][:], op0=ALU.mult, op1=ALU.add),
                                            [self.R("oc0"), self.R("oc1"), self.R("neglam")], [self.R("od")])
                                        o_, ro = ob[qb], self.R(f"ob{qb}")
                                        self.rms_epilogue(od[:], self.R("od"), 64, gsub[:, 0:1], None, o_[:], ro, "a")
                                        dst = self.dram["OA"].ap()[h * 64:(h + 1) * 64, tok0:tok0 + 512]
                                        fw.dma(fw.ACT, lambda e, dst=dst, o_=o_: e.dma_start(out=dst, in_=o_[:]), [ro], [self.R("OA")])

                            s1.append(st1)
                            s2.append(st2)
                s1[0]()
                for i in range(len(s1)):
                    if i + 1 < len(s1):
                        s1[i + 1]()
                    s2[i]()
            t0 += L

    def pD(self, l):
        fw, W = self.fw, self.W
        self.norm_consts()
        Lmax = max(self.seqs)
        nmax = Lmax // 128
        lg = self.sb("lg", [128, 8], F32)
        gC = self.sb("gC", [128, 8], F32)
        rel = self.sb("rel", [128, 128], F32)
        rp = self.sb("rp", [128, 128], F32)
        rn = self.sb("rn", [128, 128], F32)
        mge = self.sb("mge", [128, 128], F32)
        mlt = self.sb("mlt", [128, 128], F32)
        ef = self.sb("ef", [128, 128], F32)
        eb = self.sb("eb", [128, 128], F32)
        col4 = self.sb("col4", [128, 512], F32)
        cp1 = self.sb("cp1", [128, 512], F32)
        c128 = self.sb("c128", [128, 512], F32)
        pidx = self.sb("pidx", [128, 1], F32)
        c127 = self.sb("c127", [128, 1], F32)
        Dm4 = [self.sb(f"Dm4_{h}", [128, 4, 128], F32) for h in range(4)]
        zf = self.sb("zf", [128, 4], F32)
        zb = self.sb("zb", [128, 4], F32)
        xif = [self.sb(f"xif{h}", [64, 512], F32) for h in range(4)]
        xib = [self.sb(f"xib{h}", [64, 512], F32) for h in range(4)]
        gret = self.sb("gret", [64, 1], F32)
        Ktm = self.sb("Ktm", [128, nmax, 64], BF16)
        Vtm = self.sb("Vtm", [128, nmax, 64], BF16)
        Kzf = self.sb("Kzf", [128, nmax, 64], BF16)
        Kzb = self.sb("Kzb", [128, nmax, 64], BF16)
        Rfb = self.sb("Rfb", [64, nmax, 64], BF16)
        Rbb = self.sb("Rbb", [64, nmax, 64], BF16)
        Rf = [self.sb(f"Rf{i}", [64, 64], F32) for i in range(2)]
        Rb = [self.sb(f"Rb{i}", [64, 64], F32) for i in range(2)]
        qg = [self.sb(f"qg{i}", [64, 512], BF16) for i in range(2)]
        kg = [self.sb(f"kg{i}", [64, 512], BF16) for i in range(2)]
        rgg = [self.sb(f"rgg{i}", [64, 512], BF16) for i in range(2)]
        sil = self.sb("sil", [64, 512], F32)
        Ab = [self.sb(f"Ab{i}", [128, 512], BF16) for i in range(2)]
        qxf = [self.sb(f"qxf{i}", [64, 512], BF16) for i in range(2)]
        qxb = [self.sb(f"qxb{i}", [64, 512], BF16) for i in range(2)]
        ob = [self.sb(f"obd{i}", [64, 512], BF16) for i in range(2)]
        src = W["ret_log_decay"][l].rearrange("a b -> (a b)").partition_broadcast(128)
        fw.dma(fw.SP, lambda e: e.dma_start(out=lg[:], in_=src), [], [self.R("lg")])
        fw.dma(fw.SP, lambda e: e.dma_start(out=rel[:], in_=self.ret_rel), [], [self.R("rel")])
        fw.dma(fw.SP, lambda e: e.dma_start(out=col4[:], in_=self.ret_col4), [], [self.R("col4")])
        fw.dma(fw.SP, lambda e: e.dma_start(out=pidx[:], in_=self.ret_pidx), [], [self.R("pidx")])
        fw.dma(fw.SP, lambda e: e.dma_start(out=gret[:], in_=W["ret_norm_g"][l].rearrange("(v o) -> v o", o=1)),
               [], [self.R("gret")])
        fw.op(fw.ACT, lambda e: e.activation(out=lg[:], in_=lg[:], func=AF.Exp), [self.R("lg")], [self.R("lg")])
        fw.op(fw.DVE, lambda e: e.tensor_scalar(out=lg[:], in0=lg[:], scalar1=-1.0, scalar2=None, op0=ALU.mult),
              [self.R("lg")], [self.R("lg")])
        fw.op(fw.ACT, lambda e: e.activation(out=gC[:], in_=lg[:], func=AF.Exp, scale=128.0), [self.R("lg")], [self.R("gC")])
        fw.op(fw.DVE, lambda e: e.tensor_scalar(out=rp[:], in0=rel[:], scalar1=0.0, scalar2=None, op0=ALU.max),
              [self.R("rel")], [self.R("rp")])
        fw.op(fw.DVE, lambda e: e.tensor_scalar(out=rn[:], in0=rel[:], scalar1=-1.0, scalar2=0.0, op0=ALU.mult, op1=ALU.max),
              [self.R("rel")], [self.R("rn")])
        fw.op(fw.DVE, lambda e: e.tensor_scalar(out=mge[:], in0=rel[:], scalar1=0.0, scalar2=None, op0=ALU.is_ge),
              [self.R("rel")], [self.R("mge")])
        fw.op(fw.DVE, lambda e: e.tensor_scalar(out=mlt[:], in0=rel[:], scalar1=0.0, scalar2=None, op0=ALU.is_lt),
              [self.R("rel")], [self.R("mlt")])
        fw.op(fw.DVE, lambda e: e.tensor_scalar(out=cp1[:], in0=col4[:], scalar1=1.0, scalar2=None, op0=ALU.add),
              [self.R("col4")], [self.R("cp1")])
        fw.op(fw.DVE, lambda e: e.tensor_scalar(out=c128[:], in0=col4[:], scalar1=-1.0, scalar2=128.0, op0=ALU.mult, op1=ALU.add),
              [self.R("col4")], [self.R("c128")])
        fw.op(fw.DVE, lambda e: e.tensor_scalar(out=c127[:], in0=pidx[:], scalar1=-1.0, scalar2=127.0, op0=ALU.mult, op1=ALU.add),
              [self.R("pidx")], [self.R("c127")])
        for h in range(4):
            lf, lb = lg[:, h:h + 1], lg[:, 4 + h:5 + h]
            fw.op(fw.ACT, lambda e, lf=lf: e.activation(out=ef[:], in_=rp[:], func=AF.Exp, scale=lf),
                  [self.R("rp"), self.R("lg")], [self.R("ef")])
            fw.op(fw.ACT, lambda e, lb=lb: e.activation(out=eb[:], in_=rn[:], func=AF.Exp, scale=lb),
                  [self.R("rn"), self.R("lg")], [self.R("eb")])
            fw.op(fw.DVE, lambda e: e.tensor_tensor(out=ef[:], in0=ef[:], in1=mge[:], op=ALU.mult),
                  [self.R("ef"), self.R("mge")], [self.R("ef")])
            fw.op(fw.DVE, lambda e: e.tensor_tensor(out=eb[:], in0=eb[:], in1=mlt[:], op=ALU.mult),
                  [self.R("eb"), self.R("mlt")], [self.R("eb")])
            for r in range(4):
                fw.op(fw.DVE, lambda e, h=h, r=r: e.tensor_tensor(out=Dm4[h][:, r, :], in0=ef[:], in1=eb[:], op=ALU.add),
                      [self.R("ef"), self.R("eb")], [self.R(f"Dm4_{h}")])
            fw.op(fw.ACT, lambda e, h=h, lf=lf: e.activation(out=zf[:, h:h + 1], in_=c127[:], func=AF.Exp, scale=lf),
                  [self.R("c127"), self.R("lg")], [self.R("zf")])
            fw.op(fw.ACT, lambda e, h=h, lb=lb: e.activation(out=zb[:, h:h + 1], in_=pidx[:], func=AF.Exp, scale=lb),
                  [self.R("pidx"), self.R("lg")], [self.R("zb")])
            fw.op(fw.ACT, lambda e, h=h, lf=lf: e.activation(out=xif[h][:], in_=cp1[0:64, :], func=AF.Exp, scale=lf[0:64, :]),
                  [self.R("cp1"), self.R("lg")], [self.R(f"xif{h}")])
            fw.op(fw.ACT, lambda e, h=h, lb=lb: e.activation(out=xib[h][:], in_=c128[0:64, :], func=AF.Exp, scale=lb[0:64, :]),
                  [self.R("c128"), self.R("lg")], [self.R(f"xib{h}")])
        t0 = 0
        self._gi = 0
        for s, L in enumerate(self.seqs):
            n = L // 128
            for h in range(4):
                ksrc = self.dram["RK"].ap()[t0:t0 + L, h * 64:(h + 1) * 64].rearrange("(c p) v -> p c v", p=128)
                vsrc = self.dram["RV"].ap()[t0:t0 + L, h * 64:(h + 1) * 64].rearrange("(c p) v -> p c v", p=128)
                fw.dma(fw.SP, lambda e, ksrc=ksrc, n=n: e.dma_start(out=Ktm[:, 0:n, :], in_=ksrc), [self.R("RK")], [self.R("Ktm")])
                fw.dma(fw.SP, lambda e, vsrc=vsrc, n=n: e.dma_start(out=Vtm[:, 0:n, :], in_=vsrc), [self.R("RV")], [self.R("Vtm")])
                fw.op(fw.DVE, lambda e, h=h, n=n: e.tensor_scalar(out=Kzf[:, 0:n, :], in0=Ktm[:, 0:n, :], scalar1=zf[:, h:h + 1],
                                                                  scalar2=None, op0=ALU.mult), [self.R("Ktm"), self.R("zf")], [self.R("Kzf")])
                fw.op(fw.POOL, lambda e, h=h, n=n: e.tensor_scalar(out=Kzb[:, 0:n, :], in0=Ktm[:, 0:n, :], scalar1=zb[:, h:h + 1],
                                                                   scalar2=None, op0=ALU.mult), [self.R("Ktm"), self.R("zb")], [self.R("Kzb")])
                fw.op(fw.DVE, lambda e: e.memset(Rf[0][:], 0.0), [], [self.R("Rf0")])
                fw.op(fw.DVE, lambda e: e.memset(Rb[0][:], 0.0), [], [self.R("Rb0")])
                for c in range(n):
                    cur, nxt = c % 2, (c + 1) % 2
                    fw.op(fw.POOL, lambda e, c=c, cur=cur: e.tensor_copy(out=Rfb[:, c, :], in_=Rf[cur][:]),
                          [self.R(f"Rf{cur}")], [self.R("Rfb")])
                    if c < n - 1:
                        pk, prk = self.ps()
                        fw.op(fw.PE, lambda e, pk=pk, c=c: e.matmul(pk[0:64, 0:64], lhsT=Kzf[:, c, :], rhs=Vtm[:, c, :],
                                                                    start=True, stop=True), [self.R("Kzf"), self.R("Vtm")], [prk])
                        fw.op(fw.DVE, lambda e, pk=pk, cur=cur, nxt=nxt, h=h: e.scalar_tensor_tensor(
                            out=Rf[nxt][:], in0=Rf[cur][:], scalar=gC[0:64, h:h + 1], in1=pk[0:64, 0:64], op0=ALU.mult, op1=ALU.add),
                            [prk, self.R(f"Rf{cur}"), self.R("gC")], [self.R(f"Rf{nxt}")])
                for i, c in enumerate(range(n - 1, -1, -1)):
                    cur, nxt = i % 2, (i + 1) % 2
                    fw.op(fw.POOL, lambda e, c=c, cur=cur: e.tensor_copy(out=Rbb[:, c, :], in_=Rb[cur][:]),
                          [self.R(f"Rb{cur}")], [self.R("Rbb")])
                    if c > 0:
                        pk, prk = self.ps()
                        fw.op(fw.PE, lambda e, pk=pk, c=c: e.matmul(pk[0:64, 0:64], lhsT=Kzb[:, c, :], rhs=Vtm[:, c, :],
                                                                    start=True, stop=True), [self.R("Kzb"), self.R("Vtm")], [prk])
                        fw.op(fw.DVE, lambda e, pk=pk, cur=cur, nxt=nxt, h=h: e.scalar_tensor_tensor(
                            out=Rb[nxt][:], in0=Rb[cur][:], scalar=gC[0:64, 4 + h:5 + h], in1=pk[0:64, 0:64], op0=ALU.mult, op1=ALU.add),
                            [prk, self.R(f"Rb{cur}"), self.R("gC")], [self.R(f"Rb{nxt}")])
                s1, s2 = [], []
                for g in range(L // 512):
                    info = {}

                    def st1(g=g, h=h, t0=t0, info=info):
                        tok0 = t0 + g * 512
                        b = self._gi % 2
                        self._gi += 1
                        fw.dma(fw.SP, lambda e, b=b, h=h, tok0=tok0: e.dma_start(
                            out=qg[b][:], in_=self.dram["RQ"].ap()[h * 64:(h + 1) * 64, tok0:tok0 + 512]), [self.R("RQ")], [self.R(f"qg{b}")])
                        fw.dma(fw.SP, lambda e, b=b, h=h, tok0=tok0: e.dma_start(
                            out=kg[b][:], in_=self.dram["RKT"].ap()[h * 64:(h + 1) * 64, tok0:tok0 + 512]), [self.R("RKT")], [self.R(f"kg{b}")])
                        fw.dma(fw.SP, lambda e, b=b, h=h, tok0=tok0: e.dma_start(
                            out=rgg[b][:], in_=self.dram["RG"].ap()[h * 64:(h + 1) * 64, tok0:tok0 + 512]), [self.R("RG")], [self.R(f"rgg{b}")])
                        pS, prS = self.ps()
                        for r in range(4):
                            fw.op(fw.PE, lambda e, pS=pS, r=r, b=b: e.matmul(
                                pS[:, r * 128:(r + 1) * 128], lhsT=kg[b][:, r * 128:(r + 1) * 128], rhs=qg[b][:, r * 128:(r + 1) * 128],
                                start=True, stop=True), [self.R(f"kg{b}"), self.R(f"qg{b}")], [prS])
                        info.update(b=b, pS=pS, prS=prS, tok0=tok0)

                    def st2(g=g, h=h, t0=t0, info=info):
                        b, pS, prS, tok0 = info['b'], info['pS'], info['prS'], info['tok0']
                        fw.op(fw.DVE, lambda e, pS=pS, b=b, h=h: e.tensor_tensor(
                            out=Ab[b][:], in0=pS[:], in1=Dm4[h][:].rearrange("p a b -> p (a b)"), op=ALU.mult),
                            [prS, self.R(f"Dm4_{h}")], [self.R(f"Ab{b}")])
                        fw.op(fw.POOL, lambda e, b=b, h=h: e.tensor_tensor(out=qxf[b][:], in0=qg[b][:], in1=xif[h][:], op=ALU.mult),
                              [self.R(f"qg{b}"), self.R(f"xif{h}")], [self.R(f"qxf{b}")])
                        fw.op(fw.POOL, lambda e, b=b, h=h: e.tensor_tensor(out=qxb[b][:], in0=qg[b][:], in1=xib[h][:], op=ALU.mult),
                              [self.R(f"qg{b}"), self.R(f"xib{h}")], [self.R(f"qxb{b}")])
                        pO, prO = self.ps()
                        for r in range(4):
                            c = g * 4 + r
                            cs = slice(r * 128, (r + 1) * 128)
                            fw.op(fw.PE, lambda e, pO=pO, c=c, cs=cs, b=b: e.matmul(
                                pO[0:64, cs], lhsT=Vtm[:, c, :], rhs=Ab[b][:, cs], start=True, stop=False),
                                [self.R("Vtm"), self.R(f"Ab{b}")], [prO])
                            fw.op(fw.PE, lambda e, pO=pO, c=c, cs=cs, b=b: e.matmul(
                                pO[0:64, cs], lhsT=Rfb[:, c, :], rhs=qxf[b][:, cs], start=False, stop=False),
                                [self.R("Rfb"), self.R(f"qxf{b}")], [prO])
                            fw.op(fw.PE, lambda e, pO=pO, c=c, cs=cs, b=b: e.matmul(
                                pO[0:64, cs], lhsT=Rbb[:, c, :], rhs=qxb[b][:, cs], start=False, stop=True),
                                [self.R("Rbb"), self.R(f"qxb{b}")], [prO])
                        fw.op(fw.ACT, lambda e, b=b: e.activation(out=sil[:], in_=rgg[b][:], func=AF.Silu),
                              [self.R(f"rgg{b}")], [self.R("sil")])
                        o_, ro = ob[b], self.R(f"obd{b}")
                        self.rms_epilogue(pO[0:64, :], prO, 64, gret[:, 0:1], (sil[:], self.R("sil")), o_[:], ro, "d")
                        dst = self.dram["OD"].ap()[h * 64:(h + 1) * 64, tok0:tok0 + 512]
                        fw.dma(fw.ACT, lambda e, dst=dst, o_=o_: e.dma_start(out=dst, in_=o_[:]), [ro], [self.R("OD")])

                    s1.append(st1)
                    s2.append(st2)
                s1[0]()
                for i in range(len(s1)):
                    if i + 1 < len(s1):
                        s1[i + 1]()
                    s2[i]()
            t0 += L

import math
import numpy as np
import concourse.bass as bass


NEG = -400.0


def bc_ap(a, pos, count):
    ap = [list(x) for x in a.ap]
    ap.insert(1 + pos, [0, count])
    return bass.AP(tensor=a.tensor, offset=a.offset, ap=ap)


class K4(K3):
    def extra_inputs(self):
        super().extra_inputs()
        self.na_shift = self.din("na_shift", [128, 31, 64])
        self.na_cmask = self.din("na_cmask", [128, 64])

    @staticmethod
    def host_consts_extra():
        p = np.arange(128) % 64
        qc = np.arange(64)
        sh = np.zeros((128, 31, 64), np.float32)
        for d in range(31):
            sh[:, d, :] = (p[:, None] - qc[None, :] + 15 == d)
        start = np.clip(qc - 8, 0, 48)
        inwin = (p[:, None] >= start[None, :]) & (p[:, None] < start[None, :] + 16)
        cm = np.where(inwin, 0.0, NEG).astype(np.float32)
        return dict(na_shift=sh, na_cmask=cm)

    def pB(self, l):
        fw, W = self.fw, self.W
        Lmax = max(self.seqs)
        nkt_max = Lmax // 128
        shift = self.sb("shift", [128, 31, 64], F32)
        cmask = self.sb("cmask", [128, 64], F32)
        rpbb = self.sb("rpbb", [128, 4 * 15 * 31], F32)
        E = [self.sb(f"E{h}", [128, 15, 64], F32) for h in range(4)]
        tmpE = [self.sb(f"tmpE{i}", [128, 15, 64], F32) for i in range(2)]
        qT = self.sb("qTb", [64, Lmax], BF16)
        kT = self.sb("kTb", [64, Lmax], BF16)
        VB = self.sb("VBs", [128, nkt_max, 128], BF16)
        PT = [self.sb(f"PTb{i}", [128, 640], BF16) for i in range(2)]
        rec = [self.sb(f"recb{i}", [64, 128], F32) for i in range(2)]
        stg = [self.sb(f"stgb{i}", [64, 512], BF16) for i in range(2)]
        fw.dma(fw.SP, lambda e: e.dma_start(out=shift[:], in_=self.na_shift), [], [self.R("shift")])
        fw.dma(fw.SP, lambda e: e.dma_start(out=cmask[:], in_=self.na_cmask), [], [self.R("cmask")])
        fw.dma(fw.SP, lambda e: e.dma_start(
            out=rpbb[:], in_=W["na_rpb"][l].rearrange("a b c -> (a b c)").partition_broadcast(128)), [], [self.R("rpbb")])
        fw.op(fw.DVE, lambda e: e.tensor_scalar(out=rpbb[:], in0=rpbb[:], scalar1=8.0, scalar2=None, op0=ALU.mult),
              [self.R("rpbb")], [self.R("rpbb")])
        for h in range(4):
            cm_b = bc_ap(cmask[:], 0, 15)
            fw.op(fw.DVE, lambda e, h=h, cm_b=cm_b: e.tensor_copy(out=E[h][:], in_=cm_b), [self.R("cmask")], [self.R(f"E{h}")])
            for d in range(31):
                tb, rtb = tmpE[d % 2], self.R(f"tmpE{d % 2}")
                sh_b = bc_ap(shift[:, d, :], 0, 15)
                base = h * 465 + d
                rv = rpbb[:, base: base + 15 * 31 - 30: 31] if False else None
                a = rpbb[:, base:base + 1]
                r_b = bass.AP(tensor=a.tensor, offset=a.offset, ap=[list(a.ap[0]), [31, 15], [0, 64]])
                eng = fw.DVE if d % 2 == 0 else fw.POOL
                fw.op(eng, lambda e, tb=tb, sh_b=sh_b, r_b=r_b: e.tensor_tensor(out=tb[:], in0=sh_b, in1=r_b, op=ALU.mult),
                      [self.R("shift"), self.R("rpbb")], [rtb])
                fw.op(eng, lambda e, tb=tb, h=h: e.tensor_tensor(out=E[h][:], in0=E[h][:], in1=tb[:], op=ALU.add),
                      [rtb, self.R(f"E{h}")], [self.R(f"E{h}")])
        fw.op(fw.POOL, lambda e: e.memset(VB[:, :, 64:128], 1.0), [], [self.R("VBs")])
        pbcache = {}

        def get_pb(key):
            if key in pbcache:
                return pbcache[key]
            idx = len(pbcache)
            tiles = []
            for h in range(4):
                t = self.sb(f"PB{idx}_{h}", [128, 128], BF16)
                r = self.R(f"PB{idx}_{h}")
                for kp_ in range(2):
                    for qp_ in range(2):
                        dr = key[kp_][qp_]
                        o = t[kp_ * 64:(kp_ + 1) * 64, qp_ * 64:(qp_ + 1) * 64]
                        if dr is None:
                            fw.op(fw.POOL, lambda e, o=o: e.memset(o, NEG), [], [r])
                        else:
                            fw.op(fw.POOL, lambda e, o=o, h=h, dr=dr, kp_=kp_: e.tensor_copy(
                                out=o, in_=E[h][kp_ * 64:(kp_ + 1) * 64, dr, :]), [self.R(f"E{h}")], [r])
                tiles.append((t, r))
            pbcache[key] = tiles
            return tiles

        t0 = 0
        cnt = 0
        for s, L in enumerate(self.seqs):
            rows = L // 64
            nkt = L // 128
            for h in range(4):
                fw.dma(fw.SP, lambda e, h=h, t0=t0, L=L: e.dma_start(
                    out=qT[:, 0:L], in_=self.dram["QB"].ap()[h * 64:(h + 1) * 64, t0:t0 + L]), [self.R("QB")], [self.R("qTb")])
                fw.dma(fw.SP, lambda e, h=h, t0=t0, L=L: e.dma_start(
                    out=kT[:, 0:L], in_=self.dram["KB"].ap()[h * 64:(h + 1) * 64, t0:t0 + L]), [self.R("KB")], [self.R("kTb")])
                vsrc = self.dram["VB"].ap()[t0:t0 + L, h * 64:(h + 1) * 64].rearrange("(kt p) v -> p kt v", p=128)
                fw.dma(fw.SP, lambda e, vsrc=vsrc, nkt=nkt: e.dma_start(out=VB[:, 0:nkt, 0:64], in_=vsrc),
                       [self.R("VB")], [self.R("VBs")])
                s1, s2 = [], []
                for qi in range(rows // 2):
                    r0 = 2 * qi
                    rs = [min(max(qr - 4, 0), rows - 8) for qr in (r0, r0 + 1)]
                    kp_lo = min(rs) // 2
                    kp_hi = (max(rs) + 7) // 2
                    kps = list(range(kp_lo, kp_hi + 1))
                    assert len(kps) <= 5
                    b = cnt % 2
                    cnt += 1
                    sb0 = b * 2
                    pbs = []
                    for idx, kp in enumerate(kps):
                        key = tuple(tuple((2 * kp + kp_ - (r0 + qp_) + 7) if rs[qp_] <= 2 * kp + kp_ < rs[qp_] + 8 else None
                                          for qp_ in range(2)) for kp_ in range(2))
                        pbs.append(get_pb(key)[h])

                    def st1(kps=kps, pbs=pbs, sb0=sb0, qi=qi):
                        for idx, kp in enumerate(kps):
                            pbt, pbr = pbs[idx]
                            bank = sb0 + idx // 4
                            cs = slice((idx % 4) * 128, (idx % 4 + 1) * 128)
                            fw.op(fw.PE, lambda e, bank=bank, cs=cs, kp=kp: e.matmul(
                                self.psum[bank][:, cs], lhsT=kT[:, kp * 128:(kp + 1) * 128], rhs=qT[:, qi * 128:(qi + 1) * 128],
                                start=True, stop=False), [self.R("kTb"), self.R("qTb")], [self.psum_res[bank]])
                            fw.op(fw.PE, lambda e, bank=bank, cs=cs, pbt=pbt: e.matmul(
                                self.psum[bank][:, cs], lhsT=self.idb[:], rhs=pbt[:], start=False, stop=True),
                                [pbr, self.R("idb")], [self.psum_res[bank]])

                    def st2(kps=kps, sb0=sb0, qi=qi, b=b, h=h, t0=t0):
                        n = len(kps)
                        fw.op(fw.ACT, lambda e: e.activation(
                            out=PT[b][:, 0:n * 128], in_=self.psbig[0][:, sb0 * 512: sb0 * 512 + n * 128], func=AF.Exp, scale=0.125),
                            [self.psum_res[sb0], self.psum_res[sb0 + 1]], [self.R(f"PTb{b}")])
                        po, pro = self.psum[4 + b], self.psum_res[4 + b]
                        for idx, kp in enumerate(kps):
                            fw.op(fw.PE, lambda e, kp=kp, idx=idx: e.matmul(
                                po[:, 0:128], lhsT=VB[:, kp, :], rhs=PT[b][:, idx * 128:(idx + 1) * 128],
                                start=(idx == 0), stop=(idx == n - 1)), [self.R("VBs"), self.R(f"PTb{b}")], [pro])
                        fw.op(fw.DVE, lambda e: e.reciprocal(out=rec[b][:], in_=po[64:128, 0:128]), [pro], [self.R(f"recb{b}")])
                        sgi = (qi // 4) % 2
                        sg, rsg = stg[sgi], self.R(f"stgb{sgi}")
                        q4 = qi % 4
                        fw.op(fw.DVE, lambda e: e.tensor_tensor(
                            out=sg[:, q4 * 128:(q4 + 1) * 128], in0=po[0:64, 0:128], in1=rec[b][:], op=ALU.mult),
                            [pro, self.R(f"recb{b}")], [rsg])
                        if q4 == 3:
                            tok0 = t0 + (qi - 3) * 128
                            dst = self.dram["OB"].ap()[h * 64:(h + 1) * 64, tok0:tok0 + 512]
                            fw.dma(fw.ACT, lambda e, dst=dst, sg=sg: e.dma_start(out=dst, in_=sg[:]), [rsg], [self.R("OB")])

                    s1.append(st1)
                    s2.append(st2)
                s1[0]()
                for i in range(len(s1)):
                    if i + 1 < len(s1):
                        s1[i + 1]()
                    s2[i]()
            t0 += L

import math
import numpy as np
import concourse.bass as bass
import concourse.mybir as mybir


I32 = mybir.dt.int32
TWO_PI_HI = 6.28125
TWO_PI_LO = 2.0 * math.pi - 6.28125


class K5(K4):
    def extra_inputs(self):
        super().extra_inputs()
        self.s5_maskg = self.din("s5_maskg", [128, 2])

    @staticmethod
    def host_consts_extra():
        d = K4.host_consts_extra()
        p = np.arange(128)
        d["s5_maskg"] = (p[:, None] // 64 == np.arange(2)[None, :]).astype(np.float32)
        return d

    def pC(self, l):
        fw, W = self.fw, self.W
        nc = self.nc
        TB = 512

        def T(name, shape, dt=F32):
            return self.sb(name, shape, dt), self.R(name)

        def tt(eng, out, in0, in1, op, reads, writes):
            fw.op(eng, lambda e: e.tensor_tensor(out=out, in0=in0, in1=in1, op=op), reads, writes)

        are, r_are = T("c_are", [128, 2, 8])
        aim, r_aim = T("c_aim", [128, 2, 8])
        ldt, r_ldt = T("c_ldt", [128, 2, 8])
        names = ["dt", "mag", "th", "kf", "half", "ah", "sh", "ch", "t1", "t2", "cs", "sn", "ar", "ai", "den", "am1",
                 "fre", "fim", "t3"]
        P = {}
        for n_ in names:
            P[n_] = T("c_" + n_, [128, 16])
        ki, r_ki = T("c_ki", [128, 16], I32)
        hpi, r_hpi = T("c_hpi", [128, 1])
        mg, r_mg = T("c_mg", [128, 2])
        fw.op(fw.DVE, lambda e: e.memset(hpi[:], math.pi / 2), [], [r_hpi])
        fw.dma(fw.SP, lambda e: e.dma_start(out=mg[:], in_=self.s5_maskg), [], [r_mg])
        for dr in range(2):
            for (tile_, rr, nm) in ((are, r_are, "s5_a_re"), (aim, r_aim, "s5_a_im")):
                a_ = W[nm][l, dr]
                src = bass.AP(tensor=a_.tensor, offset=a_.offset, ap=[[1, 128], [128, 8]])
                fw.dma(fw.SP, lambda e, tile_=tile_, dr=dr, src=src: e.dma_start(
                    out=tile_[:, dr, :], in_=src, allow_slow_non_contiguous=True), [], [rr])
            for gh in range(2):
                a = W["s5_log_dt"][l, dr]
                src = bass.AP(tensor=a.tensor, offset=a.offset + gh, ap=[[0, 64], [2, 8]])
                fw.dma(fw.SP, lambda e, dr=dr, gh=gh, src=src: e.dma_start(
                    out=ldt[gh * 64:(gh + 1) * 64, dr, :], in_=src, allow_slow_non_contiguous=True), [], [r_ldt])
        lr = are[:].rearrange("p a b -> p (a b)")
        li = aim[:].rearrange("p a b -> p (a b)")
        ld = ldt[:].rearrange("p a b -> p (a b)")
        V = {k_: v[0][:] for k_, v in P.items()}
        Rr = {k_: v[1] for k_, v in P.items()}
        dv = fw.DVE
        fw.op(dv, lambda e: e.tensor_scalar(out=lr, in0=lr, scalar1=-1e-4, scalar2=None, op0=ALU.min), [r_are], [r_are])
        fw.op(fw.ACT, lambda e: e.activation(out=V["dt"], in_=ld, func=AF.Exp), [r_ldt], [Rr["dt"]])
        tt(dv, V["t1"], lr, V["dt"], ALU.mult, [r_are, Rr["dt"]], [Rr["t1"]])
        fw.op(fw.ACT, lambda e: e.activation(out=V["mag"], in_=V["t1"], func=AF.Exp), [Rr["t1"]], [Rr["mag"]])
        tt(dv, V["th"], li, V["dt"], ALU.mult, [r_aim, Rr["dt"]], [Rr["th"]])
        fw.op(dv, lambda e: e.tensor_scalar(out=V["t2"], in0=V["th"], scalar1=1.0 / (2 * math.pi), scalar2=None, op0=ALU.mult),
              [Rr["th"]], [Rr["t2"]])
        fw.op(dv, lambda e: e.tensor_copy(out=ki[:], in_=V["t2"]), [Rr["t2"]], [r_ki])
        fw.op(dv, lambda e: e.tensor_copy(out=V["kf"], in_=ki[:]), [r_ki], [Rr["kf"]])
        fw.op(dv, lambda e: e.scalar_tensor_tensor(out=V["t3"], in0=V["kf"], scalar=-TWO_PI_HI, in1=V["th"], op0=ALU.mult, op1=ALU.add),
              [Rr["kf"], Rr["th"]], [Rr["t3"]])
        fw.op(dv, lambda e: e.scalar_tensor_tensor(out=V["half"], in0=V["kf"], scalar=-TWO_PI_LO, in1=V["t3"], op0=ALU.mult, op1=ALU.add),
              [Rr["kf"], Rr["t3"]], [Rr["half"]])
        fw.op(dv, lambda e: e.tensor_scalar(out=V["half"], in0=V["half"], scalar1=0.5, scalar2=None, op0=ALU.mult),
              [Rr["half"]], [Rr["half"]])
        fw.op(dv, lambda e: e.scalar_tensor_tensor(out=V["ah"], in0=V["half"], scalar=-1.0, in1=V["half"], op0=ALU.mult, op1=ALU.max),
              [Rr["half"]], [Rr["ah"]])
        fw.op(fw.ACT, lambda e: e.activation(out=V["sh"], in_=V["half"], func=AF.Sin), [Rr["half"]], [Rr["sh"]])
        fw.op(fw.ACT, lambda e: e.activation(out=V["ch"], in_=V["ah"], func=AF.Sin, scale=-1.0, bias=hpi[:]),
              [Rr["ah"], r_hpi], [Rr["ch"]])
        tt(dv, V["t1"], V["ch"], V["ch"], ALU.mult, [Rr["ch"]], [Rr["t1"]])
        tt(dv, V["t2"], V["sh"], V["sh"], ALU.mult, [Rr["sh"]], [Rr["t2"]])
        tt(dv, V["cs"], V["t1"], V["t2"], ALU.subtract, [Rr["t1"], Rr["t2"]], [Rr["cs"]])
        fw.op(dv, lambda e: e.scalar_tensor_tensor(out=V["sn"], in0=V["sh"], scalar=2.0, in1=V["ch"], op0=ALU.mult, op1=ALU.mult),
              [Rr["sh"], Rr["ch"]], [Rr["sn"]])
        tt(dv, V["ar"], V["mag"], V["cs"], ALU.mult, [Rr["mag"], Rr["cs"]], [Rr["ar"]])
        tt(dv, V["ai"], V["mag"], V["sn"], ALU.mult, [Rr["mag"], Rr["sn"]], [Rr["ai"]])
        tt(dv, V["t1"], lr, lr, ALU.mult, [r_are], [Rr["t1"]])
        tt(dv, V["t2"], li, li, ALU.mult, [r_aim], [Rr["t2"]])
        tt(dv, V["den"], V["t1"], V["t2"], ALU.add, [Rr["t1"], Rr["t2"]], [Rr["den"]])
        fw.op(dv, lambda e: e.reciprocal(out=V["den"], in_=V["den"]), [Rr["den"]], [Rr["den"]])
        fw.op(dv, lambda e: e.tensor_scalar(out=V["am1"], in0=V["ar"], scalar1=-1.0, scalar2=None, op0=ALU.add), [Rr["ar"]], [Rr["am1"]])
        tt(dv, V["t1"], V["am1"], lr, ALU.mult, [Rr["am1"], r_are], [Rr["t1"]])
        tt(dv, V["t2"], V["ai"], li, ALU.mult, [Rr["ai"], r_aim], [Rr["t2"]])
        tt(dv, V["t3"], V["t1"], V["t2"], ALU.add, [Rr["t1"], Rr["t2"]], [Rr["t3"]])
        tt(dv, V["fre"], V["t3"], V["den"], ALU.mult, [Rr["t3"], Rr["den"]], [Rr["fre"]])
        tt(dv, V["t1"], V["ai"], lr, ALU.mult, [Rr["ai"], r_are], [Rr["t1"]])
        tt(dv, V["t2"], V["am1"], li, ALU.mult, [Rr["am1"], r_aim], [Rr["t2"]])
        tt(dv, V["t3"], V["t1"], V["t2"], ALU.subtract, [Rr["t1"], Rr["t2"]], [Rr["t3"]])
        tt(dv, V["fim"], V["t3"], V["den"], ALU.mult, [Rr["t3"], Rr["den"]], [Rr["fim"]])
        Bt = {}
        for nm in ("b_re", "b_im", "c_re", "c_im"):
            Bt[nm] = T("c_" + nm, [128, 2, 8, 16])
            for dr in range(2):
                a_ = W["s5_" + nm][l, dr]
                if nm[0] == "b":
                    src = bass.AP(tensor=a_.tensor, offset=a_.offset, ap=[[16, 128], [2048, 8], [1, 16]])
                    fw.dma(fw.SP, lambda e, nm=nm, dr=dr, src=src: e.dma_start(
                        out=Bt[nm][0][:, dr, :, :], in_=src, allow_slow_non_contiguous=True), [], [Bt[nm][1]])
                else:
                    for gh in range(2):
                        for pt_ in range(8):
                            src = bass.AP(tensor=a_.tensor, offset=a_.offset + gh * 1024 + pt_ * 2048, ap=[[1, 64], [64, 16]])
                            fw.dma(fw.SP, lambda e, nm=nm, dr=dr, src=src, gh=gh, pt_=pt_: e.dma_start(
                                out=Bt[nm][0][gh * 64:(gh + 1) * 64, dr, pt_, :], in_=src, allow_slow_non_contiguous=True), [], [Bt[nm][1]])
        bt_re, r_btre = T("c_btre", [128, 2, 8, 16])
        bt_im, r_btim = T("c_btim", [128, 2, 8, 16])
        tmpb = [T(f"c_tmpb{i}", [128, 8, 16]) for i in range(2)]
        for dr in range(2):
            fr_b = bc_ap(V["fre"][:, dr * 8:(dr + 1) * 8], 1, 16)
            fi_b = bc_ap(V["fim"][:, dr * 8:(dr + 1) * 8], 1, 16)
            bre, bim = Bt["b_re"][0][:, dr, :, :], Bt["b_im"][0][:, dr, :, :]
            rb = [Bt["b_re"][1], Bt["b_im"][1], Rr["fre"], Rr["fim"]]
            tt(dv, tmpb[0][0][:], bre, fr_b, ALU.mult, rb, [tmpb[0][1]])
            tt(dv, tmpb[1][0][:], bim, fi_b, ALU.mult, rb, [tmpb[1][1]])
            tt(dv, bt_re[:, dr, :, :], tmpb[0][0][:], tmpb[1][0][:], ALU.subtract, [tmpb[0][1], tmpb[1][1]], [r_btre])
            tt(dv, tmpb[0][0][:], bim, fr_b, ALU.mult, rb, [tmpb[0][1]])
            tt(dv, tmpb[1][0][:], bre, fi_b, ALU.mult, rb, [tmpb[1][1]])
            tt(dv, bt_im[:, dr, :, :], tmpb[0][0][:], tmpb[1][0][:], ALU.add, [tmpb[0][1], tmpb[1][1]], [r_btim])
        BT = {}
        CT = {}
        xm, r_xm = T("c_xm", [128, 2, 16])
        for dr in range(2):
            for pt_ in range(8):
                for ri, (src_t, rs_) in enumerate(((bt_re, r_btre), (bt_im, r_btim))):
                    t_, r_ = T(f"c_BT{dr}_{pt_}_{ri}", [128, 128], BF16)
                    BT[(dr, pt_, ri)] = (t_, r_)
                    fw.op(fw.POOL, lambda e, t_=t_: e.memset(t_[:], 0.0), [], [r_])
                    for g2 in range(2):
                        fw.op(dv, lambda e, src_t=src_t, dr=dr, pt_=pt_, g2=g2: e.tensor_scalar(
                            out=xm[:, g2, :], in0=src_t[:, dr, pt_, :], scalar1=mg[:, g2:g2 + 1], scalar2=None, op0=ALU.mult),
                            [rs_, r_mg], [r_xm])
                    pp, prp = self.ps()
                    fw.op(fw.PE, lambda e, pp=pp: e.transpose(out=pp[0:32, 0:128], in_=xm[:].rearrange("p a b -> p (a b)"),
                                                              identity=self.idf[:]), [r_xm, self.R("idf")], [prp])
                    r0 = (pt_ % 4) * 32
                    fw.op(fw.ACT, lambda e, pp=pp, t_=t_, r0=r0: e.activation(out=t_[r0:r0 + 32, :], in_=pp[0:32, 0:128], func=AF.Copy),
                          [prp], [r_])
                for ri, (nm, sgn) in enumerate((("c_re", 1.0), ("c_im", -1.0))):
                    t_, r_ = T(f"c_CT{dr}_{pt_}_{ri}", [128, 128], BF16)
                    CT[(dr, pt_, ri)] = (t_, r_)
                    fw.op(fw.POOL, lambda e, t_=t_: e.memset(t_[:], 0.0), [], [r_])
                    for g2 in range(2):
                        c0 = (pt_ % 4) * 32 + g2 * 16
                        fw.op(dv, lambda e, t_=t_, nm=nm, dr=dr, pt_=pt_, g2=g2, c0=c0, sgn=sgn: e.tensor_scalar(
                            out=t_[:, c0:c0 + 16], in0=Bt[nm][0][:, dr, pt_, :], scalar1=mg[:, g2:g2 + 1], scalar2=sgn,
                            op0=ALU.mult, op1=ALU.mult), [Bt[nm][1], r_mg], [r_])
        return self.pC_main(l, dict(locals()))

    def pC_main(self, l, ctx):
        fw, W = self.fw, self.W
        globals_needed = None
        T, tt, V, Rr, dv, BT, CT, TB = ctx['T'], ctx['tt'], ctx['V'], ctx['Rr'], ctx['dv'], ctx['BT'], ctx['CT'], ctx['TB']
        uT = [T(f"c_uT{i}", [128, 2, TB], BF16) for i in range(2)]
        bu = [T(f"c_bu{ri}", [128, 8, TB]) for ri in range(2)]
        H = [T(f"c_H{ri}", [128, 8, TB + 1]) for ri in range(2)]
        Hb = [T(f"c_Hb{ri}", [128, 8, TB], BF16) for ri in range(2)]
        sc = {k_: T("c_s" + k_, [128, 8]) for k_ in ("a", "b", "c", "d", "e", "f")}
        yst = [T(f"c_yst{i}", [128, TB]) for i in range(2)]
        yc = [T(f"c_yc{i}", [128, TB]) for i in range(2)]
        gfp = [T(f"c_g{i}", [128, TB]) for i in range(2)]
        gbf = [T(f"c_gb{i}", [128, TB], BF16) for i in range(2)]
        sg = [T(f"c_sg{i}", [128, TB]) for i in range(2)]
        ocb = [T(f"c_oc{i}", [128, TB], BF16) for i in range(2)]
        dcol, r_dcol = T("c_dcol", [128, 2])
        gbcol, r_gbcol = T("c_gbcol", [128, 2])
        glu, r_glu = T("c_glu", [128, 2, 256], BF16)
        fw.dma(fw.SP, lambda e: e.dma_start(out=dcol[:], in_=W["s5_d"][l].rearrange("(hf p) -> p hf", p=128),
                                            allow_slow_non_contiguous=True), [], [r_dcol])
        fw.dma(fw.SP, lambda e: e.dma_start(out=gbcol[:], in_=W["s5_glu_b"][l].rearrange("(hf p) -> p hf", p=128),
                                            allow_slow_non_contiguous=True), [], [r_gbcol])
        fw.dma(fw.SP, lambda e: e.dma_start(out=glu[:], in_=self.Wb[("glu", l)].rearrange("(kt p) c -> p kt c", p=128)),
               [self.R(f"wb_glu_{l}")], [r_glu])
        if "YC" not in self.dram:
            self.dscratch("YC", [256, self.T], F32)
        YC = self.dram["YC"].ap()
        t0 = 0
        ui = 0
        for s, L in enumerate(self.seqs):
            nb = L // TB
            for dr in range(2):
                arv = V["ar"][:, dr * 8:(dr + 1) * 8]
                aiv = V["ai"][:, dr * 8:(dr + 1) * 8]
                for ri in range(2):
                    cin = 0 if dr == 0 else TB
                    fw.op(dv, lambda e, ri=ri, cin=cin: e.memset(H[ri][0][:, :, cin:cin + 1], 0.0), [], [H[ri][1]])
                blocks = range(nb) if dr == 0 else range(nb - 1, -1, -1)
                for bi, b in enumerate(blocks):
                    tok0 = t0 + b * TB
                    u_, ru = uT[ui % 2]
                    ui += 1
                    fw.dma(fw.SP, lambda e, u_=u_, tok0=tok0: e.dma_start(
                        out=u_[:], in_=self.dram["U"].ap()[:, tok0:tok0 + TB].rearrange("(hf p) t -> p hf t", p=128)),
                        [self.R("U")], [ru])
                    for pt_ in range(8):
                        for ri in range(2):
                            pp, prp = self.ps()
                            bt_, rbt = BT[(dr, pt_, ri)]
                            fw.op(fw.PE, lambda e, pp=pp, bt_=bt_, u_=u_, pt_=pt_: e.matmul(
                                pp[:], lhsT=bt_[:], rhs=u_[:, pt_ // 4, :], start=True, stop=True), [rbt, ru], [prp])
                            fw.op(fw.ACT, lambda e, pp=pp, ri=ri, pt_=pt_: e.activation(out=bu[ri][0][:, pt_, :], in_=pp[:], func=AF.Copy),
                                  [prp], [bu[ri][1]])
                    if bi > 0:
                        for ri in range(2):
                            if dr == 0:
                                fw.op(dv, lambda e, ri=ri: e.tensor_copy(out=H[ri][0][:, :, 0:1], in_=H[ri][0][:, :, TB:TB + 1]),
                                      [H[ri][1]], [H[ri][1]])
                            else:
                                fw.op(dv, lambda e, ri=ri: e.tensor_copy(out=H[ri][0][:, :, TB:TB + 1], in_=H[ri][0][:, :, 0:1]),
                                      [H[ri][1]], [H[ri][1]])
                    Hre, rHre = H[0]
                    Him, rHim = H[1]
                    trange = range(TB) if dr == 0 else range(TB - 1, -1, -1)
                    for t in trange:
                        pc = t if dr == 0 else t + 1
                        oc_ = t + 1 if dr == 0 else t
                        pre, pim = Hre[:, :, pc], Him[:, :, pc]
                        tt(dv, sc["a"][0][:], pre, arv, ALU.mult, [rHre, Rr["ar"]], [sc["a"][1]])
                        tt(dv, sc["b"][0][:], pim, aiv, ALU.mult, [rHim, Rr["ai"]], [sc["b"][1]])
                        tt(dv, sc["c"][0][:], sc["a"][0][:], sc["b"][0][:], ALU.subtract, [sc["a"][1], sc["b"][1]], [sc["c"][1]])
                        tt(fw.POOL, sc["d"][0][:], pim, arv, ALU.mult, [rHim, Rr["ar"]], [sc["d"][1]])
                        tt(fw.POOL, sc["e"][0][:], pre, aiv, ALU.mult, [rHre, Rr["ai"]], [sc["e"][1]])
                        tt(fw.POOL, sc["f"][0][:], sc["d"][0][:], sc["e"][0][:], ALU.add, [sc["d"][1], sc["e"][1]], [sc["f"][1]])
                        tt(dv, Hre[:, :, oc_], sc["c"][0][:], bu[0][0][:, :, t], ALU.add, [sc["c"][1], bu[0][1]], [rHre])
                        tt(fw.POOL, Him[:, :, oc_], sc["f"][0][:], bu[1][0][:, :, t], ALU.add, [sc["f"][1], bu[1][1]], [rHim])
                    off = 1 if dr == 0 else 0
                    for ri in range(2):
                        fw.op(fw.ACT, lambda e, ri=ri, off=off: e.activation(out=Hb[ri][0][:], in_=H[ri][0][:, :, off:off + TB], func=AF.Copy),
                              [H[ri][1]], [Hb[ri][1]])
                    for hf in range(2):
                        py, pry = self.ps()
                        k_ = 0
                        for pt_ in range(hf * 4, hf * 4 + 4):
                            for ri in range(2):
                                ct_, rct = CT[(dr, pt_, ri)]
                                fw.op(fw.PE, lambda e, py=py, ct_=ct_, ri=ri, pt_=pt_, k_=k_: e.matmul(
                                    py[:], lhsT=ct_[:], rhs=Hb[ri][0][:, pt_, :], start=(k_ == 0), stop=(k_ == 7)),
                                    [rct, Hb[ri][1]], [pry])
                                k_ += 1
                        ycd = YC[hf * 128:(hf + 1) * 128, tok0:tok0 + TB]
                        if dr == 0:
                            ys, rys = yst[hf]
                            fw.op(fw.ACT, lambda e, py=py, ys=ys: e.activation(out=ys[:], in_=py[:], func=AF.Copy), [pry], [rys])
                            fw.dma(fw.ACT, lambda e, ycd=ycd, ys=ys: e.dma_start(out=ycd, in_=ys[:]), [rys], [self.R("YC")])
                        else:
                            y_, ry = yc[hf]
                            fw.dma(fw.SP, lambda e, ycd=ycd, y_=y_: e.dma_start(out=y_[:], in_=ycd), [self.R("YC")], [ry])
                            ys, rys = yst[hf]
                            tt(dv, ys[:], py[:], y_[:], ALU.add, [pry, ry], [rys])
                            fw.op(dv, lambda e, ys=ys, u_=u_, hf=hf: e.scalar_tensor_tensor(
                                out=ys[:], in0=u_[:, hf, :], scalar=dcol[:, hf:hf + 1], in1=ys[:], op0=ALU.mult, op1=ALU.add),
                                [ru, rys, r_dcol], [rys])
                            fw.op(fw.ACT, lambda e, ys=ys, hf=hf: e.activation(out=gfp[hf][0][:], in_=ys[:], func=AF.Gelu_apprx_tanh),
                                  [rys], [gfp[hf][1]])
                            fw.op(fw.ACT, lambda e, hf=hf: e.activation(out=gbf[hf][0][:], in_=gfp[hf][0][:], func=AF.Copy),
                                  [gfp[hf][1]], [gbf[hf][1]])
                    if dr == 1:
                        for hf in range(2):
                            pz, prz = self.ps()
                            for k2 in range(2):
                                fw.op(fw.PE, lambda e, pz=pz, k2=k2, hf=hf: e.matmul(
                                    pz[:], lhsT=glu[:, k2, hf * 128:(hf + 1) * 128], rhs=gbf[k2][0][:], start=(k2 == 0), stop=(k2 == 1)),
                                    [r_glu, gbf[0][1], gbf[1][1]], [prz])
                            fw.op(fw.ACT, lambda e, pz=pz, hf=hf: e.activation(out=sg[hf][0][:], in_=pz[:], func=AF.Sigmoid,
                                                                              bias=gbcol[:, hf:hf + 1]), [prz, r_gbcol], [sg[hf][1]])
                            tt(dv, ocb[hf][0][:], gfp[hf][0][:], sg[hf][0][:], ALU.mult, [gfp[hf][1], sg[hf][1]], [ocb[hf][1]])
                            dst = self.dram["OC"].ap()[hf * 128:(hf + 1) * 128, tok0:tok0 + TB]
                            fw.dma(fw.ACT, lambda e, dst=dst, hf=hf: e.dma_start(out=dst, in_=ocb[hf][0][:]), [ocb[hf][1]], [self.R("OC")])
            t0 += L

import math
import numpy as np
import concourse.bass as bass
import concourse.mybir as mybir


def rev_ap(a):
    ap = [list(x) for x in a.ap]
    step, cnt = ap[-1]
    ap[-1] = [-step, cnt]
    return bass.AP(tensor=a.tensor, offset=a.offset + step * (cnt - 1), ap=ap)


class K6(K5):
    def pC_main(self, l, ctx):
        fw, W = self.fw, self.W
        T, tt, V, Rr, dv, BT, CT, TB = (ctx[k_] for k_ in ("T", "tt", "V", "Rr", "dv", "BT", "CT", "TB"))
        pl = fw.POOL
        ck = [(V["cs"], Rr["cs"])]
        sk = [(V["sn"], Rr["sn"])]
        for k_ in range(1, 10):
            c_t, c_r = T(f"c_ck{k_}", [128, 16])
            s_t, s_r = T(f"c_sk{k_}", [128, 16])
            pc, pcr = ck[-1]
            ps_, psr = sk[-1]
            tt(dv, V["t1"], pc, pc, ALU.mult, [pcr], [Rr["t1"]])
            tt(dv, V["t2"], ps_, ps_, ALU.mult, [psr], [Rr["t2"]])
            tt(dv, c_t[:], V["t1"], V["t2"], ALU.subtract, [Rr["t1"], Rr["t2"]], [c_r])
            fw.op(dv, lambda e, s_t=s_t, pc=pc, ps_=ps_: e.scalar_tensor_tensor(
                out=s_t[:], in0=ps_, scalar=2.0, in1=pc, op0=ALU.mult, op1=ALU.mult), [pcr, psr], [s_r])
            ck.append((c_t[:], c_r))
            sk.append((s_t[:], s_r))
        Ere, rEre = T("c_Ere", [128, 8, TB])
        Eim, rEim = T("c_Eim", [128, 8, TB])
        et = [T(f"c_et{i}", [128, 8, TB // 2]) for i in range(2)]
        uT = [T(f"c_uT{i}", [128, 2, TB], BF16) for i in range(2)]
        shr, r_shr = T("c_shr", [128, 8, TB])
        shi, r_shi = T("c_shi", [128, 8, TB])
        Hb = [T(f"c_Hb{ri}", [128, 8, TB], BF16) for ri in range(2)]
        tm = [[T(f"c_tm{j}_{i}", [128, TB]) for i in range(2)] for j in range(4)]
        vr = [T(f"c_vr{i}", [128, TB]) for i in range(2)]
        vi = [T(f"c_vi{i}", [128, TB]) for i in range(2)]
        un = [[T(f"c_un{j}_{i}", [128, TB]) for i in range(2)] for j in range(4)]
        ini = [T(f"c_ini{ri}", [128, 8]) for ri in range(2)]
        it_ = [T(f"c_it{j}", [128, 8]) for j in range(4)]
        yst = [T(f"c_yst{i}", [128, TB]) for i in range(2)]
        yc = [T(f"c_yc{i}", [128, TB]) for i in range(2)]
        gfp = [T(f"c_g{i}", [128, TB]) for i in range(2)]
        gbf = [T(f"c_gb{i}", [128, TB], BF16) for i in range(2)]
        sg = [T(f"c_sg{i}", [128, TB]) for i in range(2)]
        ocb = [T(f"c_oc{i}", [128, TB], BF16) for i in range(2)]
        dcol, r_dcol = T("c_dcol", [128, 2])
        gbcol, r_gbcol = T("c_gbcol", [128, 2])
        glu, r_glu = T("c_glu", [128, 2, 256], BF16)
        fw.dma(fw.SP, lambda e: e.dma_start(out=dcol[:], in_=W["s5_d"][l].rearrange("(hf p) -> p hf", p=128),
                                            allow_slow_non_contiguous=True), [], [r_dcol])
        fw.dma(fw.SP, lambda e: e.dma_start(out=gbcol[:], in_=W["s5_glu_b"][l].rearrange("(hf p) -> p hf", p=128),
                                            allow_slow_non_contiguous=True), [], [r_gbcol])
        fw.dma(fw.SP, lambda e: e.dma_start(out=glu[:], in_=self.Wb[("glu", l)].rearrange("(kt p) c -> p kt c", p=128)),
               [self.R(f"wb_glu_{l}")], [r_glu])
        if "YC" not in self.dram:
            self.dscratch("YC", [256, self.T], F32)
        YC = self.dram["YC"].ap()
        ui = 0
        cnt = 0
        for dr in range(2):
            sl = slice(dr * 8, (dr + 1) * 8)
            fw.op(dv, lambda e: e.memset(Ere[:, :, 0:1], 1.0), [], [rEre])
            fw.op(dv, lambda e: e.memset(Eim[:, :, 0:1], 0.0), [], [rEim])
            for k_ in range(9):
                n = 1 << k_
                cb = bc_ap(ck[k_][0][:, sl], 1, n)
                sb_ = bc_ap(sk[k_][0][:, sl], 1, n)
                rd = [rEre, rEim, ck[k_][1], sk[k_][1]]
                e0, r0 = et[0]
                e1, r1 = et[1]
                tt(dv, e0[:, :, 0:n], Ere[:, :, 0:n], cb, ALU.mult, rd, [r0])
                tt(dv, e1[:, :, 0:n], Eim[:, :, 0:n], sb_, ALU.mult, rd, [r1])
                tt(dv, Ere[:, :, n:2 * n], e0[:, :, 0:n], e1[:, :, 0:n], ALU.subtract, [r0, r1], [rEre])
                tt(dv, e0[:, :, 0:n], Ere[:, :, 0:n], sb_, ALU.mult, rd, [r0])
                tt(dv, e1[:, :, 0:n], Eim[:, :, 0:n], cb, ALU.mult, rd, [r1])
                tt(dv, Eim[:, :, n:2 * n], e0[:, :, 0:n], e1[:, :, 0:n], ALU.add, [r0, r1], [rEim])
            c9 = ck[9][0][:, sl]
            s9 = sk[9][0][:, sl]
            t0 = 0
            for s, L in enumerate(self.seqs):
                nb = L // TB
                for bi in range(nb):
                    b = bi if dr == 0 else nb - 1 - bi
                    tok0 = t0 + b * TB
                    u_, ru = uT[ui % 2]
                    ui += 1
                    fw.dma(fw.SP, lambda e, u_=u_, tok0=tok0: e.dma_start(
                        out=u_[:], in_=self.dram["U"].ap()[:, tok0:tok0 + TB].rearrange("(hf p) t -> p hf t", p=128)),
                        [self.R("U")], [ru])
                    if bi == 0:
                        for ri in range(2):
                            fw.op(dv, lambda e, ri=ri: e.memset(ini[ri][0][:], 0.0), [], [ini[ri][1]])
                    else:
                        lre, lim = shr[:, :, TB - 1], shi[:, :, TB - 1]
                        rdd = [r_shr, r_shi, ck[9][1], sk[9][1]]
                        tt(dv, it_[0][0][:], lre, c9, ALU.mult, rdd, [it_[0][1]])
                        tt(dv, it_[1][0][:], lim, s9, ALU.mult, rdd, [it_[1][1]])
                        tt(dv, it_[2][0][:], lim, c9, ALU.mult, rdd, [it_[2][1]])
                        tt(dv, it_[3][0][:], lre, s9, ALU.mult, rdd, [it_[3][1]])
                        tt(dv, ini[0][0][:], it_[0][0][:], it_[1][0][:], ALU.subtract, [it_[0][1], it_[1][1]], [ini[0][1]])
                        tt(dv, ini[1][0][:], it_[2][0][:], it_[3][0][:], ALU.add, [it_[2][1], it_[3][1]], [ini[1][1]])
                    for pt_ in range(8):
                        cb2 = cnt % 2
                        cnt += 1
                        pre, prre = self.ps()
                        pim, prim = self.ps()
                        for ri, (pp, prp) in enumerate(((pre, prre), (pim, prim))):
                            bt_, rbt = BT[(dr, pt_, ri)]
                            fw.op(fw.PE, lambda e, pp=pp, bt_=bt_, u_=u_, pt_=pt_: e.matmul(
                                pp[:], lhsT=bt_[:], rhs=u_[:, pt_ // 4, :], start=True, stop=True), [rbt, ru], [prp])
                        sre = pre[:] if dr == 0 else rev_ap(pre[:])
                        sim = pim[:] if dr == 0 else rev_ap(pim[:])
                        er, ei = Ere[:, pt_, :], Eim[:, pt_, :]
                        t1, t2, t3, t4 = (tm[j][cb2] for j in range(4))
                        tt(dv, t1[0][:], sre, er, ALU.mult, [prre, rEre], [t1[1]])
                        tt(dv, t2[0][:], sim, ei, ALU.mult, [prim, rEim], [t2[1]])
                        tt(dv, t3[0][:], sim, er, ALU.mult, [prim, rEre], [t3[1]])
                        tt(dv, t4[0][:], sre, ei, ALU.mult, [prre, rEim], [t4[1]])
                        vre, rvre = vr[cb2]
                        vim, rvim = vi[cb2]
                        tt(pl, vre[:], t1[0][:], t2[0][:], ALU.add, [t1[1], t2[1]], [rvre])
                        tt(pl, vim[:], t3[0][:], t4[0][:], ALU.subtract, [t3[1], t4[1]], [rvim])
                        a_ = V["mag"][:, dr * 8 + pt_: dr * 8 + pt_ + 1]
                        rbc = bass.AP(tensor=a_.tensor, offset=a_.offset, ap=[list(a_.ap[0]), [0, TB]])
                        fw.op(dv, lambda e, pt_=pt_, vre=vre, rbc=rbc: e.tensor_tensor_scan(
                            out=shr[:, pt_, :], data0=rbc, data1=vre[:], initial=ini[0][0][:, pt_:pt_ + 1], op0=ALU.mult, op1=ALU.add),
                            [rvre, Rr["mag"], ini[0][1]], [r_shr])
                        fw.op(dv, lambda e, pt_=pt_, vim=vim, rbc=rbc: e.tensor_tensor_scan(
                            out=shi[:, pt_, :], data0=rbc, data1=vim[:], initial=ini[1][0][:, pt_:pt_ + 1], op0=ALU.mult, op1=ALU.add),
                            [rvim, Rr["mag"], ini[1][1]], [r_shi])
                        u1, u2, u3, u4 = (un[j][cb2] for j in range(4))
                        tt(dv, u1[0][:], shr[:, pt_, :], er, ALU.mult, [r_shr, rEre], [u1[1]])
                        tt(dv, u2[0][:], shi[:, pt_, :], ei, ALU.mult, [r_shi, rEim], [u2[1]])
                        tt(dv, Hb[0][0][:, pt_, :], u1[0][:], u2[0][:], ALU.subtract, [u1[1], u2[1]], [Hb[0][1]])
                        tt(pl, u3[0][:], shi[:, pt_, :], er, ALU.mult, [r_shi, rEre], [u3[1]])
                        tt(pl, u4[0][:], shr[:, pt_, :], ei, ALU.mult, [r_shr, rEim], [u4[1]])
                        tt(pl, Hb[1][0][:, pt_, :], u3[0][:], u4[0][:], ALU.add, [u3[1], u4[1]], [Hb[1][1]])
                    for hf in range(2):
                        py, pry = self.ps()
                        k_ = 0
                        for pt_ in range(hf * 4, hf * 4 + 4):
                            for ri in range(2):
                                ct_, rct = CT[(dr, pt_, ri)]
                                fw.op(fw.PE, lambda e, py=py, ct_=ct_, ri=ri, pt_=pt_, k_=k_: e.matmul(
                                    py[:], lhsT=ct_[:], rhs=Hb[ri][0][:, pt_, :], start=(k_ == 0), stop=(k_ == 7)),
                                    [rct, Hb[ri][1]], [pry])
                                k_ += 1
                        ycd = YC[hf * 128:(hf + 1) * 128, tok0:tok0 + TB]
                        if dr == 0:
                            ys, rys = yst[hf]
                            fw.op(fw.ACT, lambda e, py=py, ys=ys: e.activation(out=ys[:], in_=py[:], func=AF.Copy), [pry], [rys])
                            fw.dma(fw.ACT, lambda e, ycd=ycd, ys=ys: e.dma_start(out=ycd, in_=ys[:]), [rys], [self.R("YC")])
                        else:
                            y_, ry = yc[hf]
                            fw.dma(fw.SP, lambda e, ycd=ycd, y_=y_: e.dma_start(out=y_[:], in_=ycd), [self.R("YC")], [ry])
                            ys, rys = yst[hf]
                            tt(dv, ys[:], rev_ap(py[:]), y_[:], ALU.add, [pry, ry], [rys])
                            fw.op(dv, lambda e, ys=ys, u_=u_, hf=hf: e.scalar_tensor_tensor(
                                out=ys[:], in0=u_[:, hf, :], scalar=dcol[:, hf:hf + 1], in1=ys[:], op0=ALU.mult, op1=ALU.add),
                                [ru, rys, r_dcol], [rys])
                            fw.op(fw.ACT, lambda e, ys=ys, hf=hf: e.activation(out=gfp[hf][0][:], in_=ys[:], func=AF.Gelu_apprx_tanh),
                                  [rys], [gfp[hf][1]])
                            fw.op(fw.ACT, lambda e, hf=hf: e.activation(out=gbf[hf][0][:], in_=gfp[hf][0][:], func=AF.Copy),
                                  [gfp[hf][1]], [gbf[hf][1]])
                    if dr == 1:
                        for hf in range(2):
                            pz, prz = self.ps()
                            for k2 in range(2):
                                fw.op(fw.PE, lambda e, pz=pz, k2=k2, hf=hf: e.matmul(
                                    pz[:], lhsT=glu[:, k2, hf * 128:(hf + 1) * 128], rhs=gbf[k2][0][:], start=(k2 == 0), stop=(k2 == 1)),
                                    [r_glu, gbf[0][1], gbf[1][1]], [prz])
                            fw.op(fw.ACT, lambda e, pz=pz, hf=hf: e.activation(out=sg[hf][0][:], in_=pz[:], func=AF.Sigmoid,
                                                                              bias=gbcol[:, hf:hf + 1]), [prz, r_gbcol], [sg[hf][1]])
                            tt(dv, ocb[hf][0][:], gfp[hf][0][:], sg[hf][0][:], ALU.mult, [gfp[hf][1], sg[hf][1]], [ocb[hf][1]])
                            dst = self.dram["OC"].ap()[hf * 128:(hf + 1) * 128, tok0:tok0 + TB]
                            fw.dma(fw.ACT, lambda e, dst=dst, hf=hf: e.dma_start(out=dst, in_=ocb[hf][0][:]), [ocb[hf][1]], [self.R("OC")])
                t0 += L

import ml_dtypes
ROPE_THETA = 500000.0


def _host_consts(Lmax):
    half = 4
    inv = np.power(np.float32(ROPE_THETA), -np.arange(half, dtype=np.float32) / half).astype(np.float32)
    pos = np.arange(Lmax, dtype=np.float32)
    ang = pos[None, :] * inv[:, None]
    cos = np.ones((32, Lmax), np.float32); sin = np.zeros((32, Lmax), np.float32)
    cos[0:4] = np.cos(ang); cos[4:8] = np.cos(ang); sin[0:4] = np.sin(ang); sin[4:8] = np.sin(ang)
    p = np.arange(128)
    d = dict(gmask=(p[:, None] // 32 == np.arange(4)[None, :]).astype(np.float32),
             ret_rel=(p[None, :] - p[:, None]).astype(np.float32),
             ret_col4=np.tile((np.arange(512) % 128).astype(np.float32)[None, :], (128, 1)),
             ret_pidx=p.astype(np.float32)[:, None],
             rope_cos=cos, rope_sin=sin, ident_f=np.eye(128, dtype=np.float32),
             ident_b=np.eye(128).astype(ml_dtypes.bfloat16))
    d.update(K6.host_consts_extra())
    return d


def kernel(**inputs):
    xp = np.asarray(inputs["x_prompt"]); xs = np.asarray(inputs["x_sample"])
    cp = np.asarray(inputs["c_prompt"]); cs = np.asarray(inputs["c_sample"])
    Lp, Ls = xp.shape[1], xs.shape[1]
    seqs = [Lp, Ls, Ls]
    k = K6(seqs, phases=("p0", "p1", "pA", "pB", "pC", "pD", "p3", "p4"))
    nc = k.build()
    hc = _host_consts(max(seqs))
    in_maps = []
    for c in range(8):
        m = {}
        m["x"] = np.ascontiguousarray(np.concatenate([xp[c // 4], xs[2 * c], xs[2 * c + 1]], axis=0))
        cc = np.stack([cp[c // 4], cs[2 * c], cs[2 * c + 1]], axis=0)
        m["cT"] = np.ascontiguousarray(cc.reshape(3, 8, 128).transpose(2, 1, 0))
        for kk in k.W:
            m[kk] = np.ascontiguousarray(np.asarray(inputs[kk]))
        for kk, v in hc.items():
            if kk in k.dram:
                m[kk] = v
        in_maps.append(m)
    res = run_bass_kernel_spmd(nc, in_maps, core_ids=list(range(8)))
    yp = np.zeros_like(xp); ys = np.zeros_like(xs)
    q = Lp // 4
    for c in range(8):
        y = res.results[c]["y"]
        r = c % 4
        yp[c // 4, r * q:(r + 1) * q] = y[r * q:(r + 1) * q]
        ys[2 * c] = y[Lp:Lp + Ls]
        ys[2 * c + 1] = y[Lp + Ls:Lp + 2 * Ls]
    return (yp, ys)
```

```python
import numpy as np
import concourse.bass as bass
import concourse.mybir as mybir

F32 = mybir.dt.float32
BF16 = mybir.dt.bfloat16
AF = mybir.ActivationFunctionType
ALU = mybir.AluOpType
AX = mybir.AxisListType

NDMA_SLOTS = 8


class _Rec:
    def __init__(self):
        self.call = None

    def __getattr__(self, name):
        def f(*a, **kw):
            self.call = (name, a, kw)
            return self
        return f


def freeze(fn):
    r = _Rec()
    fn(r)
    assert r.call is not None
    name, a, kw = r.call
    return lambda e: getattr(e, name)(*a, **kw)


class Res:
    __slots__ = ("w", "r", "name")

    def __init__(self, name=""):
        self.w = {}
        self.r = {}
        self.name = name


class Q:
    def __init__(self, fw, name, self_sync=True, dma=False):
        self.fw = fw
        self.name = name
        self.ops = []
        self.count = 0
        self.semkey = fw.new_sem(name)
        self.waited = {}
        self.self_sync = self_sync
        self.dma_count = 0
        self.dma_keys = [fw.new_sem(f"{name}_d{i}") for i in range(NDMA_SLOTS)] if dma else []


class FW:
    def __init__(self, nc):
        self.nc = nc
        self.sem_names = []
        self.PE = Q(self, "pe", self_sync=False)
        self.DVE = Q(self, "dve")
        self.ACT = Q(self, "act", dma=True)
        self.POOL = Q(self, "pool", dma=True)
        self.SP = Q(self, "sp", dma=True)
        self.queues = [self.PE, self.DVE, self.ACT, self.POOL, self.SP]

    def new_sem(self, name):
        self.sem_names.append(name)
        return len(self.sem_names) - 1

    def _deps(self, q, reads, writes):
        deps = {}
        for r in reads:
            for k, v in r.w.items():
                if deps.get(k, 0) < v:
                    deps[k] = v
        for w in writes:
            for k, v in w.w.items():
                if deps.get(k, 0) < v:
                    deps[k] = v
            for k, v in w.r.items():
                if deps.get(k, 0) < v:
                    deps[k] = v
        for k, v in deps.items():
            if k == q.semkey and not q.self_sync:
                continue
            if q.waited.get(k, 0) < v:
                q.ops.append(("wait", k, v))
                q.waited[k] = v

    def op(self, q, fn, reads=(), writes=()):
        self._deps(q, reads, writes)
        q.count += 1
        q.ops.append(("op", freeze(fn), q.semkey, 1))
        for w in writes:
            w.w[q.semkey] = q.count
        for r in reads:
            r.r[q.semkey] = q.count

    def dma(self, q, fn, reads=(), writes=()):
        i = q.dma_count
        q.dma_count += 1
        slot = i % NDMA_SLOTS
        key = q.dma_keys[slot]
        prev = 16 * (i // NDMA_SLOTS)
        if prev > 0 and q.waited.get(key, 0) < prev:
            q.ops.append(("wait", key, prev))
            q.waited[key] = prev
        self._deps(q, reads, writes)
        val = prev + 16
        q.ops.append(("op", freeze(fn), key, 16))
        for w in writes:
            w.w[key] = val
        for r in reads:
            r.r[key] = val

    def barrier(self):
        targets = {}
        for qq in self.queues:
            if qq.count > 0:
                targets[qq.semkey] = qq.count
            for slot, key in enumerate(qq.dma_keys):
                n = (qq.dma_count - slot + NDMA_SLOTS - 1) // NDMA_SLOTS if qq.dma_count > slot else 0
                if n > 0:
                    targets[key] = 16 * n
        for q in self.queues:
            for k, v in targets.items():
                if q.waited.get(k, 0) < v:
                    q.ops.append(("wait", k, v))
                    q.waited[k] = v

    def finish(self):
        q = self.SP
        for qq in self.queues:
            for slot, key in enumerate(qq.dma_keys):
                n = (qq.dma_count - slot + NDMA_SLOTS - 1) // NDMA_SLOTS if qq.dma_count > slot else 0
                if n > 0 and q.waited.get(key, 0) < 16 * n:
                    q.ops.append(("wait", key, 16 * n))
                    q.waited[key] = 16 * n
            if qq is not q and qq.count > 0 and q.waited.get(qq.semkey, 0) < qq.count:
                q.ops.append(("wait", qq.semkey, qq.count))

    def emit(self, block, sems):
        def run(q, eng):
            for o in q.ops:
                if o[0] == "wait":
                    eng.wait_ge(sems[o[1]], o[2])
                else:
                    o[1](eng).then_inc(sems[o[2]], o[3])

        @block.tensor
        def _(e):
            run(self.PE, e)

        @block.vector
        def _(e):
            run(self.DVE, e)

        @block.scalar
        def _(e):
            run(self.ACT, e)

        @block.gpsimd
        def _(e):
            run(self.POOL, e)

        @block.sync
        def _(e):
            run(self.SP, e)

    def n_instr(self):
        return sum(len(q.ops) for q in self.queues)

import contextlib
import numpy as np
import concourse.bass as bass
import concourse.mybir as mybir
from concourse.bass_utils import run_bass_kernel_spmd


D = 1024
DEPTH = 2
NSEG = 11
DFF = 2816
EPS = 1e-6


def AP(t, offset, ap):
    return bass.AP(tensor=t.tensor if hasattr(t, "tensor") else t, offset=offset, ap=[list(a) for a in ap])


class K:
    def __init__(self, seqs, debug=(), depth=DEPTH, phases=("p0", "p1")):
        self.seqs = list(seqs)
        self.T = sum(seqs)
        self.debug = set(debug)
        self.depth = depth
        self.phases = phases
        self.nc = bass.Bass("TRN2", target_bir_lowering=False)
        self.fw = FW(self.nc)
        self.es = contextlib.ExitStack()
        self.res = {}
        self.dram = {}

    def din(self, name, shape, dt=F32):
        t = self.nc.dram_tensor(name, list(shape), dt, kind="ExternalInput")
        self.dram[name] = t
        self.res[name] = Res(name)
        return t.ap()

    def dscratch(self, name, shape, dt=BF16):
        kind = "ExternalOutput" if name in self.debug else "Internal"
        if name in getattr(self, "ext_in", ()):
            kind = "ExternalInput"
        t = self.nc.dram_tensor(name, list(shape), dt, kind=kind)
        self.dram[name] = t
        self.res[name] = Res(name)
        return t.ap()

    @contextlib.contextmanager
    def phase(self):
        self.pes = contextlib.ExitStack()
        try:
            yield
        finally:
            self.fw.barrier()
            self.pes.close()
            self.pes = None

    def sb(self, name, shape, dt=F32):
        st = self.pes if getattr(self, "pes", None) is not None else self.es
        self._uid = getattr(self, "_uid", 0) + 1
        t = st.enter_context(self.nc.sbuf_tensor(f"{name}_u{self._uid}", list(shape), dt))
        self.res[name] = Res(name)
        return t

    def R(self, name):
        return self.res[name]

    def new_psum(self):
        self.psum = []
        self.psum_res = []
        self.psbig = [self.es.enter_context(self.nc.psum_tensor(f"psbig{i}", [128, 2048], F32)) for i in range(2)]
        for i in range(8):
            t = self.psbig[i // 4][:, (i % 4) * 512:(i % 4 + 1) * 512]
            self.psum.append(t)
            self.psum_res.append(Res(f"ps{i}"))
        self.psum_i = 0

    def ps(self):
        i = self.psum_i % 7
        self.psum_i += 1
        return self.psum[i], self.psum_res[i]

    def build(self):
        nc, fw = self.nc, self.fw
        T = self.T
        dep = self.depth
        self.x = self.din("x", [T, D])
        self.cT = self.din("cT", [128, 8, 3])
        self.y = self.nc.dram_tensor("y", [T, D], F32, kind="ExternalOutput").ap()
        self.res["y"] = Res("y")
        W = {}
        wshapes = {
            "norm1_g": [dep, D], "norm2_g": [dep, D], "ada_w": [dep, D, 6 * D], "ada_b": [dep, 6 * D],
            "w_in": [dep, D, 2816], "da_lambda": [dep, 4, 32], "da_subln_g": [dep, 64],
            "na_rpb": [dep, 4, 15, 31], "s5_a_re": [dep, 2, 16, 64], "s5_a_im": [dep, 2, 16, 64],
            "s5_log_dt": [dep, 2, 16], "s5_b_re": [dep, 2, 16, 64, 16], "s5_b_im": [dep, 2, 16, 64, 16],
            "s5_c_re": [dep, 2, 16, 16, 64], "s5_c_im": [dep, 2, 16, 16, 64], "s5_d": [dep, 256],
            "s5_glu_w": [dep, 256, 256], "s5_glu_b": [dep, 256], "ret_log_decay": [dep, 2, 4],
            "ret_norm_g": [dep, 64], "w_branch": [dep, 4, 256, D], "w_branch_gate": [dep, 4, D, D],
            "w_out": [dep, D, D], "ffn_w_up": [dep, D, DFF], "ffn_w_gate": [dep, D, DFF],
            "ffn_conv_w": [dep, 3, 1, DFF], "ffn_conv_b": [dep, DFF], "ffn_w_down": [dep, DFF, D],
            "final_norm_g": [D],
        }
        for k, s in wshapes.items():
            W[k] = self.din(k, s)
        self.W = W
        Lmax = max(self.seqs)
        self.rope_cos = self.din("rope_cos", [32, Lmax])
        self.rope_sin = self.din("rope_sin", [32, Lmax])
        self.ident_f = self.din("ident_f", [128, 128])
        self.ident_b = self.din("ident_b", [128, 128], BF16)
        self.Wb = {}
        for k, shp in [("w_in", [D, 2816]), ("wg", [4 * D, D]), ("wb", [4 * 256, D]), ("w_out", [D, D]),
                       ("up", [D, DFF]), ("gate", [D, DFF]), ("down", [DFF, D]), ("glu", [256, 256])]:
            for l in range(dep):
                self.Wb[(k, l)] = self.dscratch(f"wb_{k}_{l}", shp, BF16)
        for l in range(dep):
            self.dscratch(f"modd_{l}", [3, 6 * D], F32)
        for n in ["QA", "KA", "QB", "KB", "U", "RQ", "RKT", "RG"]:
            self.dscratch(n, [256, T], BF16)
        for n in ["VA", "VB", "RK", "RV"]:
            self.dscratch(n, [T, 256], BF16)
        for n in ["OA", "OB", "OC", "OD"]:
            self.dscratch(n, [256, T], BF16)
        self.dscratch("XMID", [T, D], F32)
        self.dscratch("XR", [T, D], F32)
        if hasattr(self, "extra_inputs"):
            self.extra_inputs()
        self.new_psum()
        self.idb = self.sb("idb", [128, 128], BF16)
        self.idf = self.sb("idf", [128, 128], F32)
        fw.dma(fw.SP, lambda e: e.dma_start(out=self.idb[:], in_=self.ident_b), [self.R("ident_b")], [self.R("idb")])
        fw.dma(fw.SP, lambda e: e.dma_start(out=self.idf[:], in_=self.ident_f), [self.R("ident_f")], [self.R("idf")])

        for l in range(dep):
            if "p0" in self.phases:
                with self.phase():
                    self.p0(l)
            xin_name = "x" if l == 0 else "XR"
            if "p1" in self.phases:
                with self.phase():
                    self.p1(l, xin_name)
            for ph in ("pA", "pB", "pC", "pD"):
                if ph in self.phases:
                    with self.phase():
                        getattr(self, ph)(l)
            if "p3" in self.phases:
                with self.phase():
                    self.p3(l, xin_name)
            if "p4" in self.phases:
                with self.phase():
                    self.p4(l, "y" if l == dep - 1 else "XR", l == dep - 1)
        fw.finish()
        sems = [self.es.enter_context(nc.semaphore(n)) for n in fw.sem_names]
        block = self.es.enter_context(nc.Block())
        fw.emit(block, sems)
        self.es.close()
        return nc

    def convert(self, src_ap2d, dst_name, l, bufs):
        fw = self.fw
        Rr, C = src_ap2d.shape
        dst = self.Wb[(dst_name, l)]
        rd = self.R(f"wb_{dst_name}_{l}")
        CH = 2048
        for rt in range(Rr // 128):
            for c0 in range(0, C, CH):
                cw = min(CH, C - c0)
                i = self.cv_i
                self.cv_i += 1
                fb, bb = bufs[0][i % 3], bufs[1][i % 3]
                rf, rb = self.R(f"cvf{i % 3}"), self.R(f"cvb{i % 3}")
                src = src_ap2d[rt * 128:(rt + 1) * 128, c0:c0 + cw]
                fw.dma(fw.SP, lambda e, fb=fb, src=src, cw=cw: e.dma_start(out=fb[:, 0:cw], in_=src), [], [rf])
                q = fw.DVE if i % 2 == 0 else fw.POOL
                fw.op(q, lambda e, fb=fb, bb=bb, cw=cw: e.tensor_copy(out=bb[:, 0:cw], in_=fb[:, 0:cw]), [rf], [rb])
                d = dst[rt * 128:(rt + 1) * 128, c0:c0 + cw]
                fw.dma(fw.ACT, lambda e, bb=bb, d=d, cw=cw: e.dma_start(out=d, in_=bb[:, 0:cw]), [rb], [rd])

    def p0(self, l):
        fw, W = self.fw, self.W
        if True:
            self.cvf = [self.sb(f"cvf{i}", [128, 2048], F32) for i in range(3)]
            self.cvb = [self.sb(f"cvb{i}", [128, 2048], BF16) for i in range(3)]
            self.cv_i = 0
            self.csil = self.sb("csil", [128, 8, 3], F32)
            self.adab = self.sb("adab", [3, 6 * D], F32)
            self.adaw = [self.sb(f"adaw{i}", [128, 8, 512], F32) for i in range(2)]
            self.modsb = self.sb("modsb", [3, 512], F32)
            fw.dma(fw.SP, lambda e: e.dma_start(out=self.csil[:], in_=self.cT), [self.R("cT")], [self.R("csil")])
            fw.op(fw.ACT, lambda e: e.activation(out=self.csil[:], in_=self.csil[:], func=AF.Silu),
                  [self.R("csil")], [self.R("csil")])
        bufs = (self.cvf, self.cvb)
        self.convert(W["w_in"][l], "w_in", l, bufs)
        self.convert(W["w_branch_gate"][l].rearrange("a r c -> (a r) c"), "wg", l, bufs)
        self.convert(W["w_branch"][l].rearrange("a r c -> (a r) c"), "wb", l, bufs)
        self.convert(W["w_out"][l], "w_out", l, bufs)
        self.convert(W["ffn_w_up"][l], "up", l, bufs)
        self.convert(W["ffn_w_gate"][l], "gate", l, bufs)
        self.convert(W["ffn_w_down"][l], "down", l, bufs)
        self.convert(W["s5_glu_w"][l], "glu", l, bufs)
        ab = W["ada_b"][l].partition_broadcast(3)
        fw.dma(fw.SP, lambda e: e.dma_start(out=self.adab[:], in_=ab), [], [self.R("adab")])
        modd = self.dram[f"modd_{l}"].ap()
        for cc in range(12):
            wbuf = self.adaw[cc % 2]
            rw = self.R(f"adaw{cc % 2}")
            src = W["ada_w"][l][:, cc * 512:(cc + 1) * 512].rearrange("(kt p) c -> p kt c", p=128)
            fw.dma(fw.SP, lambda e, wbuf=wbuf, src=src: e.dma_start(out=wbuf[:], in_=src), [], [rw])
            pt, pr = self.ps()
            for kt in range(8):
                fw.op(fw.PE, lambda e, pt=pt, wbuf=wbuf, kt=kt: e.matmul(
                    pt[0:3, :], lhsT=self.csil[:, kt, :], rhs=wbuf[:, kt, :], start=(kt == 0), stop=(kt == 7)),
                    [self.R("csil"), rw], [pr])
            fw.op(fw.DVE, lambda e, pt=pt, cc=cc: e.tensor_tensor(
                out=self.modsb[:], in0=pt[0:3, :], in1=self.adab[:, cc * 512:(cc + 1) * 512], op=ALU.add),
                [pr, self.R("adab")], [self.R("modsb")])
            dst = modd[:, cc * 512:(cc + 1) * 512]
            fw.dma(fw.SP, lambda e, dst=dst: e.dma_start(out=dst, in_=self.modsb[:]),
                   [self.R("modsb")], [self.R(f"modd_{l}")])

    def bcast_load(self, dst_tile, dst_res, src_row_ap, src_res):
        self.fw.dma(self.fw.SP, lambda e: e.dma_start(out=dst_tile[:], in_=src_row_ap.partition_broadcast(128)),
                    [src_res], [dst_res])

    def p1(self, l, xin_name):
        fw, W = self.fw, self.W
        TT = 512
        if True:
            self.w_in_sb = self.sb("w_in_sb", [128, 8, 2816], BF16)
            self.w_rot = self.sb("w_rot", [128, 8, 512], BF16)
            self.G1 = self.sb("G1", [128, D], F32)
            self.SH1 = self.sb("SH1", [128, D], F32)
            self.tmpg = self.sb("tmpg", [128, D], F32)
            self.xt = [self.sb(f"xt{i}", [128, D], F32) for i in range(2)]
            self.xn = [self.sb(f"xn{i}", [128, D], F32) for i in range(2)]
            self.hb = [self.sb(f"hb{i}", [128, D], BF16) for i in range(2)]
            self.sq = self.sb("sq", [128, D], F32)
            self.ss = [self.sb(f"ss{i}", [128, 1], F32) for i in range(2)]
            self.rstd = [self.sb(f"rstd{i}", [128, 1], F32) for i in range(2)]
            self.hT = [self.sb(f"hT{i}", [128, 8, TT], BF16) for i in range(2)]
            self.cos_t = [self.sb(f"cos{i}", [128, TT], F32) for i in range(2)]
            self.sin_t = [self.sb(f"sin{i}", [128, TT], F32) for i in range(2)]
            self.stg = [self.sb(f"stg{i}", [128, TT], BF16) for i in range(4)]
            self.rt1 = [self.sb(f"rt1_{i}", [128, TT], F32) for i in range(2)]
            self.rt2 = [self.sb(f"rt2_{i}", [128, TT], F32) for i in range(2)]
            self.stk = [self.sb(f"stk{i}", [128, 4, 256], BF16) for i in range(2)]
            self.stg_i = 0
            self.epsb = self.sb("epsb", [128, 1], F32)
            fw.op(fw.DVE, lambda e: e.memset(self.epsb[:], EPS), [], [self.R("epsb")])
        src = self.Wb[("w_in", l)].rearrange("(kt p) c -> p kt c", p=128)
        fw.dma(fw.SP, lambda e: e.dma_start(out=self.w_in_sb[:], in_=src), [self.R(f"wb_w_in_{l}")], [self.R("w_in_sb")])
        fw.op(fw.POOL, lambda e: e.memset(self.w_rot[:], 0.0), [], [self.R("w_rot")])
        for kt in range(8):
            sv = self.w_in_sb[:, kt, 0:512].rearrange("p (b d) -> p b d", d=32)
            dv = self.w_rot[:, kt, :].rearrange("p (b d) -> p b d", d=32)
            fw.op(fw.DVE, lambda e, sv=sv, dv=dv: e.tensor_scalar(
                out=dv[:, :, 0:4], in0=sv[:, :, 4:8], scalar1=-1.0, scalar2=None, op0=ALU.mult),
                [self.R("w_in_sb")], [self.R("w_rot")])
            fw.op(fw.DVE, lambda e, sv=sv, dv=dv: e.tensor_copy(out=dv[:, :, 4:8], in_=sv[:, :, 0:4]),
                  [self.R("w_in_sb")], [self.R("w_rot")])
        modd = self.dram[f"modd_{l}"].ap()
        rmod = self.R(f"modd_{l}")
        t0 = 0
        it = 0
        for s, L in enumerate(self.seqs):
            self.bcast_load(self.SH1, self.R("SH1"), modd[s, 0:D], rmod)
            self.bcast_load(self.tmpg, self.R("tmpg"), modd[s, D:2 * D], rmod)
            self.bcast_load(self.G1, self.R("G1"), W["norm1_g"][l], self.R("norm1_g"))
            fw.op(fw.DVE, lambda e: e.scalar_tensor_tensor(
                out=self.G1[:], in0=self.tmpg[:], scalar=1.0, in1=self.G1[:], op0=ALU.add, op1=ALU.mult),
                [self.R("tmpg"), self.R("G1")], [self.R("G1")])
            for tt in range(L // TT):
                tok0 = t0 + tt * TT
                pos0 = tt * TT
                b = it % 2
                it += 1
                hT, rhT = self.hT[b], self.R(f"hT{b}")
                self.norm_tile(self.dram[xin_name].ap(), self.R(xin_name), tok0, TT, self.G1, self.SH1, hT, rhT)
                ct, st = self.cos_t[b], self.sin_t[b]
                for blk in range(4):
                    fw.dma(fw.SP, lambda e, ct=ct, blk=blk, pos0=pos0: e.dma_start(
                        out=ct[blk * 32:(blk + 1) * 32, :], in_=self.rope_cos[:, pos0:pos0 + TT]), [], [self.R(f"cos{b}")])
                    fw.dma(fw.SP, lambda e, st=st, blk=blk, pos0=pos0: e.dma_start(
                        out=st[blk * 32:(blk + 1) * 32, :], in_=self.rope_sin[:, pos0:pos0 + TT]), [], [self.R(f"sin{b}")])
                fm = [("QA", 0, "rot"), ("KA", 256, "rot"), ("QB", 768, None), ("KB", 1024, None),
                      ("U", 1536, None), ("RQ", 1792, None), ("RKT", 2048, 0.125), ("RG", 2560, None)]
                for name, c0, mode in fm:
                    for mt in range(2):
                        cc = c0 + mt * 128
                        pt, pr = self.ps()
                        for kt in range(8):
                            fw.op(fw.PE, lambda e, pt=pt, kt=kt, cc=cc, hT=hT: e.matmul(
                                pt[:], lhsT=self.w_in_sb[:, kt, cc:cc + 128], rhs=hT[:, kt, :],
                                start=(kt == 0), stop=(kt == 7)), [self.R("w_in_sb"), rhT], [pr])
                        si = self.stg_i % 4
                        self.stg_i += 1
                        stg, rs = self.stg[si], self.R(f"stg{si}")
                        if mode == "rot":
                            pt2, pr2 = self.ps()
                            for kt in range(8):
                                fw.op(fw.PE, lambda e, pt2=pt2, kt=kt, cc=cc, hT=hT: e.matmul(
                                    pt2[:], lhsT=self.w_rot[:, kt, cc:cc + 128], rhs=hT[:, kt, :],
                                    start=(kt == 0), stop=(kt == 7)), [self.R("w_rot"), rhT], [pr2])
                            r1, r2 = self.rt1[si % 2], self.rt2[si % 2]
                            rr1, rr2 = self.R(f"rt1_{si % 2}"), self.R(f"rt2_{si % 2}")
                            fw.op(fw.DVE, lambda e, pt=pt, r1=r1, ct=ct: e.tensor_tensor(
                                out=r1[:], in0=pt[:], in1=ct[:], op=ALU.mult), [pr, self.R(f"cos{b}")], [rr1])
                            fw.op(fw.DVE, lambda e, pt2=pt2, r2=r2, st=st: e.tensor_tensor(
                                out=r2[:], in0=pt2[:], in1=st[:], op=ALU.mult), [pr2, self.R(f"sin{b}")], [rr2])
                            fw.op(fw.DVE, lambda e, r1=r1, r2=r2, stg=stg: e.tensor_tensor(
                                out=stg[:], in0=r1[:], in1=r2[:], op=ALU.add), [rr1, rr2], [rs])
                        else:
                            sc = 1.0 if mode is None else mode
                            fw.op(fw.ACT, lambda e, pt=pt, stg=stg, sc=sc: e.activation(
                                out=stg[:], in_=pt[:], func=AF.Copy, scale=sc), [pr], [rs])
                        dst = self.dram[name].ap()[mt * 128:(mt + 1) * 128, tok0:tok0 + TT]
                        fw.dma(fw.ACT, lambda e, dst=dst, stg=stg: e.dma_start(out=dst, in_=stg[:]),
                               [rs], [self.R(name)])
                tmn = [("VA", 512, 1.0), ("VB", 1280, 1.0), ("RK", 2048, 0.125), ("RV", 2304, 1.0)]
                for j in range(TT // 128):
                    sk = self.stk[j % 2]
                    rsk = self.R(f"stk{j % 2}")
                    for half in range(2):
                        pt, pr = self.ps()
                        for q2 in range(2):
                            name, c0, sc = tmn[half * 2 + q2]
                            for kt in range(8):
                                fw.op(fw.PE, lambda e, pt=pt, kt=kt, c0=c0, hT=hT, j=j, q2=q2: e.matmul(
                                    pt[:, q2 * 256:(q2 + 1) * 256], lhsT=hT[:, kt, j * 128:(j + 1) * 128],
                                    rhs=self.w_in_sb[:, kt, c0:c0 + 256], start=(kt == 0), stop=(kt == 7)),
                                    [self.R("w_in_sb"), rhT], [pr])
                        for q2 in range(2):
                            name, c0, sc = tmn[half * 2 + q2]
                            fw.op(fw.DVE, lambda e, pt=pt, sk=sk, q2=q2, half=half, sc=sc: e.tensor_scalar(
                                out=sk[:, half * 2 + q2, :], in0=pt[:, q2 * 256:(q2 + 1) * 256], scalar1=sc,
                                scalar2=None, op0=ALU.mult), [pr], [rsk])
                    for qi, (name, c0, sc) in enumerate(tmn):
                        dst = self.dram[name].ap()[tok0 + j * 128: tok0 + (j + 1) * 128, :]
                        fw.dma(fw.SP, lambda e, dst=dst, sk=sk, qi=qi: e.dma_start(out=dst, in_=sk[:, qi, :]),
                               [rsk], [self.R(name)])
            t0 += L

    def norm_tile(self, xsrc, xres, tok0, TT, G, SH, hT, rhT):
        fw = self.fw
        Gr = self.R("G1") if G is self.G1 else self.R("G2")
        SHr = self.R("SH1") if SH is self.SH1 else self.R("SH2")
        for j in range(TT // 128):
            b = j % 2
            xt, rx = self.xt[b], self.R(f"xt{b}")
            xn, rxn = self.xn[b], self.R(f"xn{b}")
            hb, rhb = self.hb[b], self.R(f"hb{b}")
            ss, rss = self.ss[b], self.R(f"ss{b}")
            rstd, rrs = self.rstd[b], self.R(f"rstd{b}")
            src = xsrc[tok0 + j * 128: tok0 + (j + 1) * 128, :]
            fw.dma(fw.SP, lambda e, xt=xt, src=src: e.dma_start(out=xt[:], in_=src), [xres], [rx])
            fw.op(fw.ACT, lambda e, xt=xt, ss=ss: e.activation(
                out=self.sq[:], in_=xt[:], func=AF.Square, accum_out=ss[:]), [rx], [self.R("sq"), rss])
            fw.op(fw.ACT, lambda e, ss=ss, rstd=rstd: e.activation(
                out=rstd[:], in_=ss[:], func=AF.Sqrt, scale=1.0 / D, bias=self.epsb[:]), [rss, self.R("epsb")], [rrs])
            fw.op(fw.DVE, lambda e, rstd=rstd: e.reciprocal(out=rstd[:], in_=rstd[:]), [rrs], [rrs])
            fw.op(fw.DVE, lambda e, xt=xt, xn=xn, rstd=rstd: e.scalar_tensor_tensor(
                out=xn[:], in0=xt[:], scalar=rstd[:], in1=G[:], op0=ALU.mult, op1=ALU.mult), [rx, rrs, Gr], [rxn])
            fw.op(fw.POOL, lambda e, xn=xn, hb=hb: e.tensor_tensor(
                out=hb[:], in0=xn[:], in1=SH[:], op=ALU.add), [rxn, SHr], [rhb])
            pt, pr = self.ps()
            ptb = pt[:].bitcast(BF16)
            for kt in range(8):
                fw.op(fw.PE, lambda e, ptb=ptb, hb=hb, kt=kt: e.transpose(
                    out=ptb[:, kt * 128:(kt + 1) * 128], in_=hb[:, kt * 128:(kt + 1) * 128], identity=self.idb[:]),
                    [rhb, self.R("idb")], [pr])
            fw.op(fw.ACT, lambda e, ptb=ptb, hT=hT, j=j: e.activation(
                out=hT[:, :, j * 128:(j + 1) * 128], in_=ptb.rearrange("p (k t) -> p k t", k=8), func=AF.Copy),
                [pr], [rhT])

import math
import numpy as np
import concourse.bass as bass


GRID_W = 64


class K2(K):
    def p3(self, l, xin_name):
        fw, W = self.fw, self.W
        TT = 512
        xin = self.dram[xin_name].ap()
        xres = self.R(xin_name)
        self.G1 = self.sb("G1", [128, D], F32)
        self.SH1 = self.sb("SH1", [128, D], F32)
        self.tmpg = self.sb("tmpg", [128, D], F32)
        g1g = self.sb("g1g", [128, D], F32)
        self.xt = [self.sb(f"xt{i}", [128, D], F32) for i in range(2)]
        self.xn = [self.sb(f"xn{i}", [128, D], F32) for i in range(2)]
        self.hb = [self.sb(f"hb{i}", [128, D], BF16) for i in range(2)]
        self.sq = self.sb("sq", [128, D], F32)
        self.ss = [self.sb(f"ss{i}", [128, 1], F32) for i in range(2)]
        self.rstd = [self.sb(f"rstd{i}", [128, 1], F32) for i in range(2)]
        self.hT = [self.sb(f"hT{i}", [128, 8, TT], BF16) for i in range(2)]
        self.epsb = self.sb("epsb", [128, 1], F32)
        fw.op(fw.DVE, lambda e: e.memset(self.epsb[:], EPS), [], [self.R("epsb")])
        wout = self.sb("wout", [128, 8, D], BF16)
        wgc = [self.sb(f"wgc{i}", [128, 8, 512], BF16) for i in range(3)]
        wbc = [self.sb(f"wbc{i}", [128, 2, D], BF16) for i in range(2)]
        oT = [self.sb(f"oT{i}", [128, 2, TT], BF16) for i in range(2)]
        gt = [self.sb(f"gt{i}", [128, TT], BF16) for i in range(2)]
        acc = self.sb("acc", [128, 8, TT], F32)
        tmpm = [self.sb(f"tmpm{i}", [128, TT], F32) for i in range(2)]
        mT = self.sb("mT", [128, 8, TT], BF16)
        xo = [self.sb(f"xo{i}", [128, 512], F32) for i in range(2)]
        xr = [self.sb(f"xr{i}", [128, D], F32) for i in range(2)]
        fw.dma(fw.SP, lambda e: e.dma_start(out=wout[:], in_=self.Wb[("w_out", l)].rearrange("(kt p) c -> p kt c", p=128)),
               [self.R(f"wb_w_out_{l}")], [self.R("wout")])
        modd = self.dram[f"modd_{l}"].ap()
        rmod = self.R(f"modd_{l}")
        onames = ["OA", "OB", "OC", "OD"]
        t0 = 0
        it = 0
        wg_i = 0
        for s, L in enumerate(self.seqs):
            self.bcast_load(self.SH1, self.R("SH1"), modd[s, 0:D], rmod)
            self.bcast_load(self.tmpg, self.R("tmpg"), modd[s, D:2 * D], rmod)
            self.bcast_load(self.G1, self.R("G1"), W["norm1_g"][l], self.R("norm1_g"))
            fw.op(fw.DVE, lambda e: e.scalar_tensor_tensor(
                out=self.G1[:], in0=self.tmpg[:], scalar=1.0, in1=self.G1[:], op0=ALU.add, op1=ALU.mult),
                [self.R("tmpg"), self.R("G1")], [self.R("G1")])
            self.bcast_load(g1g, self.R("g1g"), modd[s, 2 * D:3 * D], rmod)
            for tt in range(L // TT):
                tok0 = t0 + tt * TT
                b = it % 2
                it += 1
                hT, rhT = self.hT[b], self.R(f"hT{b}")
                self.norm_tile(xin, xres, tok0, TT, self.G1, self.SH1, hT, rhT)
                for i in range(4):
                    wb_, rwb = wbc[i % 2], self.R(f"wbc{i % 2}")
                    src = self.Wb[("wb", l)][i * 256:(i + 1) * 256, :].rearrange("(kt p) c -> p kt c", p=128)
                    fw.dma(fw.SP, lambda e, wb_=wb_, src=src: e.dma_start(out=wb_[:], in_=src),
                           [self.R(f"wb_wb_{l}")], [rwb])
                    o_, ro = oT[i % 2], self.R(f"oT{i % 2}")
                    osrc = self.dram[onames[i]].ap()[:, tok0:tok0 + TT].rearrange("(kt p) t -> p kt t", p=128)
                    fw.dma(fw.SP, lambda e, o_=o_, osrc=osrc: e.dma_start(out=o_[:], in_=osrc),
                           [self.R(onames[i])], [ro])
                    for half in range(2):
                        wg_, rwg = wgc[wg_i % 3], self.R(f"wgc{wg_i % 3}")
                        wg_i += 1
                        src = self.Wb[("wg", l)][i * D:(i + 1) * D, half * 512:(half + 1) * 512].rearrange(
                            "(kt p) c -> p kt c", p=128)
                        fw.dma(fw.SP, lambda e, wg_=wg_, src=src: e.dma_start(out=wg_[:], in_=src),
                               [self.R(f"wb_wg_{l}")], [rwg])
                        for m4 in range(4):
                            mt = half * 4 + m4
                            pg, prg = self.ps()
                            for kt in range(8):
                                fw.op(fw.PE, lambda e, pg=pg, wg_=wg_, kt=kt, m4=m4, hT=hT: e.matmul(
                                    pg[:], lhsT=wg_[:, kt, m4 * 128:(m4 + 1) * 128], rhs=hT[:, kt, :],
                                    start=(kt == 0), stop=(kt == 7)), [rwg, rhT], [prg])
                            pb, prb = self.ps()
                            for k2 in range(2):
                                fw.op(fw.PE, lambda e, pb=pb, wb_=wb_, k2=k2, mt=mt, o_=o_: e.matmul(
                                    pb[:], lhsT=wb_[:, k2, mt * 128:(mt + 1) * 128], rhs=o_[:, k2, :],
                                    start=(k2 == 0), stop=(k2 == 1)), [rwb, ro], [prb])
                            g_, rg_ = gt[mt % 2], self.R(f"gt{mt % 2}")
                            fw.op(fw.ACT, lambda e, pg=pg, g_=g_: e.activation(out=g_[:], in_=pg[:], func=AF.Sigmoid),
                                  [prg], [rg_])
                            if i == 0:
                                fw.op(fw.DVE, lambda e, pb=pb, g_=g_, mt=mt: e.tensor_tensor(
                                    out=acc[:, mt, :], in0=pb[:], in1=g_[:], op=ALU.mult), [prb, rg_], [self.R("acc")])
                            else:
                                tm, rtm = tmpm[mt % 2], self.R(f"tmpm{mt % 2}")
                                fw.op(fw.DVE, lambda e, pb=pb, g_=g_, tm=tm: e.tensor_tensor(
                                    out=tm[:], in0=pb[:], in1=g_[:], op=ALU.mult), [prb, rg_], [rtm])
                                if i < 3:
                                    fw.op(fw.POOL, lambda e, tm=tm, mt=mt: e.tensor_tensor(
                                        out=acc[:, mt, :], in0=acc[:, mt, :], in1=tm[:], op=ALU.add),
                                        [rtm, self.R("acc")], [self.R("acc")])
                                else:
                                    fw.op(fw.POOL, lambda e, tm=tm, mt=mt: e.tensor_tensor(
                                        out=mT[:, mt, :], in0=acc[:, mt, :], in1=tm[:], op=ALU.add),
                                        [rtm, self.R("acc")], [self.R("mT")])
                for j in range(TT // 128):
                    xr_, rxr = xr[j % 2], self.R(f"xr{j % 2}")
                    src = xin[tok0 + j * 128: tok0 + (j + 1) * 128, :]
                    fw.dma(fw.SP, lambda e, xr_=xr_, src=src: e.dma_start(out=xr_[:], in_=src), [xres], [rxr])
                    for ch in range(2):
                        po, pro = self.ps()
                        for kt in range(8):
                            fw.op(fw.PE, lambda e, po=po, kt=kt, j=j, ch=ch: e.matmul(
                                po[:], lhsT=mT[:, kt, j * 128:(j + 1) * 128], rhs=wout[:, kt, ch * 512:(ch + 1) * 512],
                                start=(kt == 0), stop=(kt == 7)), [self.R("mT"), self.R("wout")], [pro])
                        xo_, rxo = xo[ch], self.R(f"xo{ch}")
                        fw.op(fw.DVE, lambda e, po=po, xo_=xo_, ch=ch: e.tensor_tensor(
                            out=xo_[:], in0=po[:], in1=g1g[:, ch * 512:(ch + 1) * 512], op=ALU.mult),
                            [pro, self.R("g1g")], [rxo])
                        fw.op(fw.POOL, lambda e, xo_=xo_, xr_=xr_, ch=ch: e.tensor_tensor(
                            out=xo_[:], in0=xo_[:], in1=xr_[:, ch * 512:(ch + 1) * 512], op=ALU.add), [rxo, rxr], [rxo])
                        dst = self.dram["XMID"].ap()[tok0 + j * 128: tok0 + (j + 1) * 128, ch * 512:(ch + 1) * 512]
                        fw.dma(fw.ACT, lambda e, dst=dst, xo_=xo_: e.dma_start(out=dst, in_=xo_[:]), [rxo], [self.R("XMID")])
            t0 += L

    def p4(self, l, xout_name, final):
        fw, W = self.fw, self.W
        TT = 512
        xin = self.dram["XMID"].ap()
        xres = self.R("XMID")
        self.G2 = self.sb("G2", [128, D], F32)
        self.SH2 = self.sb("SH2", [128, D], F32)
        self.G1, self.SH1 = None, None
        self.tmpg = self.sb("tmpg", [128, D], F32)
        g2g = self.sb("g2g", [128, D], F32)
        fng = self.sb("fng", [128, D], F32)
        self.xt = [self.sb(f"xt{i}", [128, D], F32) for i in range(2)]
        self.xn = [self.sb(f"xn{i}", [128, D], F32) for i in range(2)]
        self.hb = [self.sb(f"hb{i}", [128, D], BF16) for i in range(2)]
        self.sq = self.sb("sq", [128, D], F32)
        self.ss = [self.sb(f"ss{i}", [128, 1], F32) for i in range(2)]
        self.rstd = [self.sb(f"rstd{i}", [128, 1], F32) for i in range(2)]
        hTl = [self.sb(f"hT{i}", [128, 8, TT + 2], BF16) for i in range(2)]
        self.epsb = self.sb("epsb", [128, 1], F32)
        fw.op(fw.DVE, lambda e: e.memset(self.epsb[:], EPS), [], [self.R("epsb")])
        wdn = self.sb("wdn", [128, 22, D], BF16)
        wuc = [self.sb(f"wuc{i}", [128, 8, 512], BF16) for i in range(2)]
        wgc = [self.sb(f"wgc{i}", [128, 8, 512], BF16) for i in range(2)]
        cw = self.sb("cw", [128, 22, 3], F32)
        cb = self.sb("cb", [128, 22], F32)
        aext = [self.sb(f"aext{i}", [128, TT + 2], F32) for i in range(2)]
        cv = [self.sb(f"cv{i}", [128, TT], F32) for i in range(2)]
        ge = [self.sb(f"ge{i}", [128, TT], F32) for i in range(2)]
        uT = self.sb("uT", [128, 22, TT], BF16)
        xo = [self.sb(f"xo{i}", [128, 512], F32) for i in range(2)]
        xr = [self.sb(f"xr{i}", [128, D], F32) for i in range(2)]
        yo = [self.sb(f"yo{i}", [128, D], F32) for i in range(2)]
        fw.dma(fw.SP, lambda e: e.dma_start(out=wdn[:], in_=self.Wb[("down", l)].rearrange("(kt p) c -> p kt c", p=128)),
               [self.R(f"wb_down_{l}")], [self.R("wdn")])
        with self.nc.allow_non_contiguous_dma(reason="small conv params"):
            pass
        for k3 in range(3):
            src = W["ffn_conv_w"][l][k3, 0, :].rearrange("(mt p) -> p mt", p=128)
            fw.dma(fw.SP, lambda e, src=src, k3=k3: e.dma_start(out=cw[:, :, k3], in_=src, allow_slow_non_contiguous=True), [], [self.R("cw")])
        src = W["ffn_conv_b"][l].rearrange("(mt p) -> p mt", p=128)
        fw.dma(fw.SP, lambda e, src=src: e.dma_start(out=cb[:], in_=src, allow_slow_non_contiguous=True), [], [self.R("cb")])
        if final:
            self.bcast_load(fng, self.R("fng"), W["final_norm_g"], self.R("final_norm_g"))
        modd = self.dram[f"modd_{l}"].ap()
        rmod = self.R(f"modd_{l}")
        xout = self.dram[xout_name].ap() if xout_name != "y" else self.y
        rout = self.R(xout_name)
        t0 = 0
        it = 0
        wi = 0
        for s, L in enumerate(self.seqs):
            self.bcast_load(self.SH2, self.R("SH2"), modd[s, 3 * D:4 * D], rmod)
            self.bcast_load(self.tmpg, self.R("tmpg"), modd[s, 4 * D:5 * D], rmod)
            self.bcast_load(self.G2, self.R("G2"), W["norm2_g"][l], self.R("norm2_g"))
            fw.op(fw.DVE, lambda e: e.scalar_tensor_tensor(
                out=self.G2[:], in0=self.tmpg[:], scalar=1.0, in1=self.G2[:], op0=ALU.add, op1=ALU.mult),
                [self.R("tmpg"), self.R("G2")], [self.R("G2")])
            self.bcast_load(g2g, self.R("g2g"), modd[s, 5 * D:6 * D], rmod)
            ntile = L // TT
            for tt in range(ntile):
                tok0 = t0 + tt * TT
                b = it % 2
                it += 1
                hT, rhT = hTl[b], self.R(f"hT{b}")
                self.norm_tile(xin, xres, tok0, TT, self.G2, self.SH2, hT, rhT)
                pidx = max(tok0 - 1, t0)
                nidx = min(tok0 + TT, t0 + L - 1)
                self.norm_halo(xin, xres, pidx, nidx, self.G2, self.SH2, hT, rhT, TT)
                has_prev = tt > 0
                has_next = tt < ntile - 1
                for mt in range(22):
                    c4 = mt % 4
                    if c4 == 0:
                        wu_, rwu = wuc[wi % 2], self.R(f"wuc{wi % 2}")
                        wg_, rwg = wgc[wi % 2], self.R(f"wgc{wi % 2}")
                        wi += 1
                        ncol = min(512, DFF - mt * 128)
                        srcu = self.Wb[("up", l)][:, mt * 128: mt * 128 + ncol].rearrange("(kt p) c -> p kt c", p=128)
                        srcg = self.Wb[("gate", l)][:, mt * 128: mt * 128 + ncol].rearrange("(kt p) c -> p kt c", p=128)
                        fw.dma(fw.SP, lambda e, wu_=wu_, srcu=srcu, ncol=ncol: e.dma_start(out=wu_[:, :, 0:ncol], in_=srcu),
                               [self.R(f"wb_up_{l}")], [rwu])
                        fw.dma(fw.SP, lambda e, wg_=wg_, srcg=srcg, ncol=ncol: e.dma_start(out=wg_[:, :, 0:ncol], in_=srcg),
                               [self.R(f"wb_gate_{l}")], [rwg])
                    pa, pra = self.ps()
                    for kt in range(8):
                        fw.op(fw.PE, lambda e, pa=pa, wu_=wu_, kt=kt, c4=c4, hT=hT: e.matmul(
                            pa[:], lhsT=wu_[:, kt, c4 * 128:(c4 + 1) * 128], rhs=hT[:, kt, 0:TT],
                            start=(kt == 0), stop=(kt == 7)), [rwu, rhT], [pra])
                    ph, prh = self.ps()
                    for kt in range(8):
                        fw.op(fw.PE, lambda e, ph=ph, wu_=wu_, kt=kt, c4=c4, hT=hT: e.matmul(
                            ph[:, 0:2], lhsT=wu_[:, kt, c4 * 128:(c4 + 1) * 128], rhs=hT[:, kt, TT:TT + 2],
                            start=(kt == 0), stop=(kt == 7)), [rwu, rhT], [prh])
                    pg, prg = self.ps()
                    for kt in range(8):
                        fw.op(fw.PE, lambda e, pg=pg, wg_=wg_, kt=kt, c4=c4, hT=hT: e.matmul(
                            pg[:], lhsT=wg_[:, kt, c4 * 128:(c4 + 1) * 128], rhs=hT[:, kt, 0:TT],
                            start=(kt == 0), stop=(kt == 7)), [rwg, rhT], [prg])
                    ae, rae = aext[mt % 2], self.R(f"aext{mt % 2}")
                    fw.op(fw.ACT, lambda e, pa=pa, ae=ae: e.activation(out=ae[:, 1:TT + 1], in_=pa[:], func=AF.Copy),
                          [pra], [rae])
                    if has_prev:
                        fw.op(fw.ACT, lambda e, ph=ph, ae=ae: e.activation(out=ae[:, 0:1], in_=ph[:, 0:1], func=AF.Copy),
                              [prh], [rae])
                    else:
                        fw.op(fw.ACT, lambda e, ae=ae: e.memzero(ae[:, 0:1]) if False else e.activation(
                            out=ae[:, 0:1], in_=ae[:, 1:2], func=AF.Copy, scale=0.0), [rae], [rae])
                    if has_next:
                        fw.op(fw.ACT, lambda e, ph=ph, ae=ae: e.activation(
                            out=ae[:, TT + 1:TT + 2], in_=ph[:, 1:2], func=AF.Copy), [prh], [rae])
                    else:
                        fw.op(fw.ACT, lambda e, ae=ae: e.activation(
                            out=ae[:, TT + 1:TT + 2], in_=ae[:, 1:2], func=AF.Copy, scale=0.0), [rae], [rae])
                    cv_, rcv = cv[mt % 2], self.R(f"cv{mt % 2}")
                    fw.op(fw.DVE, lambda e, ae=ae, cv_=cv_, mt=mt: e.tensor_scalar(
                        out=cv_[:], in0=ae[:, 0:TT], scalar1=cw[:, mt, 0:1], scalar2=cb[:, mt:mt + 1],
                        op0=ALU.mult, op1=ALU.add), [rae, self.R("cw"), self.R("cb")], [rcv])
                    fw.op(fw.DVE, lambda e, ae=ae, cv_=cv_, mt=mt: e.scalar_tensor_tensor(
                        out=cv_[:], in0=ae[:, 1:TT + 1], scalar=cw[:, mt, 1:2], in1=cv_[:], op0=ALU.mult, op1=ALU.add),
                        [rae, self.R("cw"), rcv], [rcv])
                    fw.op(fw.DVE, lambda e, ae=ae, cv_=cv_, mt=mt: e.scalar_tensor_tensor(
                        out=cv_[:], in0=ae[:, 2:TT + 2], scalar=cw[:, mt, 2:3], in1=cv_[:], op0=ALU.mult, op1=ALU.add),
                        [rae, self.R("cw"), rcv], [rcv])
                    ge_, rge = ge[mt % 2], self.R(f"ge{mt % 2}")
                    fw.op(fw.ACT, lambda e, cv_=cv_, ge_=ge_: e.activation(out=ge_[:], in_=cv_[:], func=AF.Gelu_apprx_tanh),
                          [rcv], [rge])
                    fw.op(fw.DVE, lambda e, pg=pg, ge_=ge_, mt=mt: e.tensor_tensor(
                        out=uT[:, mt, :], in0=pg[:], in1=ge_[:], op=ALU.mult), [prg, rge], [self.R("uT")])
                for j in range(TT // 128):
                    xr_, rxr = xr[j % 2], self.R(f"xr{j % 2}")
                    src = xin[tok0 + j * 128: tok0 + (j + 1) * 128, :]
                    fw.dma(fw.SP, lambda e, xr_=xr_, src=src: e.dma_start(out=xr_[:], in_=src), [xres], [rxr])
                    yo_, ryo = yo[j % 2], self.R(f"yo{j % 2}")
                    for ch in range(2):
                        po, pro = self.ps()
                        for mt in range(22):
                            fw.op(fw.PE, lambda e, po=po, mt=mt, j=j, ch=ch: e.matmul(
                                po[:], lhsT=uT[:, mt, j * 128:(j + 1) * 128], rhs=wdn[:, mt, ch * 512:(ch + 1) * 512],
                                start=(mt == 0), stop=(mt == 21)), [self.R("uT"), self.R("wdn")], [pro])
                        xo_, rxo = xo[ch], self.R(f"xo{ch}")
                        fw.op(fw.DVE, lambda e, po=po, xo_=xo_, ch=ch: e.tensor_tensor(
                            out=xo_[:], in0=po[:], in1=g2g[:, ch * 512:(ch + 1) * 512], op=ALU.mult),
                            [pro, self.R("g2g")], [rxo])
                        fw.op(fw.POOL, lambda e, xo_=xo_, xr_=xr_, yo_=yo_, ch=ch: e.tensor_tensor(
                            out=yo_[:, ch * 512:(ch + 1) * 512], in0=xo_[:], in1=xr_[:, ch * 512:(ch + 1) * 512], op=ALU.add),
                            [rxo, rxr], [ryo])
                    if final:
                        ss, rss = self.ss[j % 2], self.R(f"ss{j % 2}")
                        rstd, rrs = self.rstd[j % 2], self.R(f"rstd{j % 2}")
                        fw.op(fw.ACT, lambda e, yo_=yo_, ss=ss: e.activation(
                            out=self.sq[:], in_=yo_[:], func=AF.Square, accum_out=ss[:]), [ryo], [self.R("sq"), rss])
                        fw.op(fw.ACT, lambda e, ss=ss, rstd=rstd: e.activation(
                            out=rstd[:], in_=ss[:], func=AF.Sqrt, scale=1.0 / D, bias=self.epsb[:]),
                            [rss, self.R("epsb")], [rrs])
                        fw.op(fw.DVE, lambda e, rstd=rstd: e.reciprocal(out=rstd[:], in_=rstd[:]), [rrs], [rrs])
                        fw.op(fw.DVE, lambda e, yo_=yo_, rstd=rstd: e.scalar_tensor_tensor(
                            out=yo_[:], in0=yo_[:], scalar=rstd[:], in1=fng[:], op0=ALU.mult, op1=ALU.mult),
                            [ryo, rrs, self.R("fng")], [ryo])
                    dst = xout[tok0 + j * 128: tok0 + (j + 1) * 128, :]
                    fw.dma(fw.ACT, lambda e, dst=dst, yo_=yo_: e.dma_start(out=dst, in_=yo_[:]), [ryo], [rout])
            t0 += L

    def norm_halo(self, xsrc, xres, pidx, nidx, G, SH, hT, rhT, TT):
        fw = self.fw
        Gr = self.R("G2")
        SHr = self.R("SH2")
        b = 0
        xt, rx = self.xt[b], self.R(f"xt{b}")
        xn, rxn = self.xn[b], self.R(f"xn{b}")
        hb, rhb = self.hb[b], self.R(f"hb{b}")
        ss, rss = self.ss[b], self.R(f"ss{b}")
        rstd, rrs = self.rstd[b], self.R(f"rstd{b}")
        fw.dma(fw.SP, lambda e: e.dma_start(out=xt[0:1, :], in_=xsrc[pidx:pidx + 1, :]), [xres], [rx])
        fw.dma(fw.SP, lambda e: e.dma_start(out=xt[1:2, :], in_=xsrc[nidx:nidx + 1, :]), [xres], [rx])
        fw.op(fw.ACT, lambda e: e.activation(out=self.sq[0:2, :], in_=xt[0:2, :], func=AF.Square, accum_out=ss[0:2, :]),
              [rx], [self.R("sq"), rss])
        fw.op(fw.ACT, lambda e: e.activation(out=rstd[0:2, :], in_=ss[0:2, :], func=AF.Sqrt, scale=1.0 / D,
                                             bias=self.epsb[0:2, :]), [rss, self.R("epsb")], [rrs])
        fw.op(fw.DVE, lambda e: e.reciprocal(out=rstd[0:2, :], in_=rstd[0:2, :]), [rrs], [rrs])
        fw.op(fw.DVE, lambda e: e.scalar_tensor_tensor(out=xn[0:2, :], in0=xt[0:2, :], scalar=rstd[0:2, :], in1=G[0:2, :],
                                                       op0=ALU.mult, op1=ALU.mult), [rx, rrs, Gr], [rxn])
        fw.op(fw.POOL, lambda e: e.tensor_tensor(out=hb[0:2, :], in0=xn[0:2, :], in1=SH[0:2, :], op=ALU.add),
              [rxn, SHr], [rhb])
        pt, pr = self.ps()
        ptb = pt[:].bitcast(BF16)
        for kt in range(8):
            fw.op(fw.PE, lambda e, kt=kt: e.transpose(out=ptb[:, kt * 2:(kt + 1) * 2], in_=hb[0:2, kt * 128:(kt + 1) * 128],
                                                      identity=self.idb[0:2, 0:2]), [rhb, self.R("idb")], [pr])
        fw.op(fw.ACT, lambda e: e.activation(out=hT[:, :, TT:TT + 2], in_=ptb[:, 0:16].rearrange("p (k t) -> p k t", k=8),
                                             func=AF.Copy), [pr], [rhT])

import math
import numpy as np
import concourse.bass as bass


class K3(K2):
    def extra_inputs(self):
        self.gmask = self.din("gmask", [128, 4])
        self.ret_rel = self.din("ret_rel", [128, 128])
        self.ret_col4 = self.din("ret_col4", [128, 512])
        self.ret_pidx = self.din("ret_pidx", [128, 1])

    def rms_epilogue(self, src_ap, src_res, nrows, scale_col, extra_mul, out_bf, out_res, tag):
        fw = self.fw
        sq, rs = self._sq, self._rs
        pn, prn = self.psum[7], self.psum_res[7]
        fw.op(fw.ACT, lambda e: e.activation(out=sq[0:64, :], in_=src_ap, func=AF.Square), [src_res], [self.R("nsq")])
        fw.op(fw.PE, lambda e: e.matmul(pn[0:64, :], lhsT=self._ones[0:64, 0:64], rhs=sq[0:64, :], start=True, stop=True),
              [self.R("nsq"), self.R("nones")], [prn])
        fw.op(fw.ACT, lambda e: e.activation(out=rs[0:64, :], in_=pn[0:64, :], func=AF.Sqrt, scale=1.0 / 64,
                                             bias=self._eps[0:64, :]), [prn, self.R("neps")], [self.R("nrs")])
        fw.op(fw.DVE, lambda e: e.reciprocal(out=rs[0:64, :], in_=rs[0:64, :]), [self.R("nrs")], [self.R("nrs")])
        if extra_mul is None:
            fw.op(fw.DVE, lambda e: e.scalar_tensor_tensor(out=out_bf, in0=src_ap, scalar=scale_col, in1=rs[0:64, :],
                                                           op0=ALU.mult, op1=ALU.mult), [src_res, self.R("nrs")], [out_res])
        else:
            em, emr = extra_mul
            fw.op(fw.DVE, lambda e: e.scalar_tensor_tensor(out=sq[0:64, :], in0=src_ap, scalar=scale_col, in1=rs[0:64, :],
                                                           op0=ALU.mult, op1=ALU.mult), [src_res, self.R("nrs")], [self.R("nsq")])
            fw.op(fw.DVE, lambda e: e.tensor_tensor(out=out_bf, in0=sq[0:64, :], in1=em, op=ALU.mult),
                  [self.R("nsq"), emr], [out_res])

    def norm_consts(self):
        fw = self.fw
        self._sq = self.sb("nsq", [128, 512], F32)
        self._rs = self.sb("nrs", [128, 512], F32)
        self._ones = self.sb("nones", [128, 64], F32)
        self._eps = self.sb("neps", [128, 1], F32)
        fw.op(fw.DVE, lambda e: e.memset(self._ones[:], 1.0), [], [self.R("nones")])
        fw.op(fw.DVE, lambda e: e.memset(self._eps[:], EPS), [], [self.R("neps")])

    def pA(self, l):
        fw, W = self.fw, self.W
        lam_init = 0.8 - 0.6 * math.exp(-0.3 * l)
        self.norm_consts()
        Lmax = max(self.seqs)
        nkt_max = Lmax // 128
        dl = self.sb("dl", [128, 128], F32)
        dlp = self.sb("dlp", [128, 2, 32], F32)
        s12 = self.sb("s12", [128, 2], F32)
        neglam = self.sb("neglam", [128, 1], F32)
        gsub = self.sb("gsub", [64, 1], F32)
        mk = self.sb("mk", [128, 4], F32)
        KT = self.sb("KT", [128, Lmax], BF16)
        VA = self.sb("VAs", [128, nkt_max, 128], BF16)
        QT = [self.sb(f"QT{i}", [128, 512], BF16) for i in range(2)]
        Qm = [[self.sb(f"Qm{i}_{c}", [128, 512], BF16) for c in range(2)] for i in range(2)]
        PT = [self.sb(f"PT{i}", [128, 1024], BF16) for i in range(3)]
        rec = self.sb("rec", [64, 512], F32)
        oc = [self.sb(f"oc{c}", [64, 512], F32) for c in range(2)]
        od = self.sb("od", [64, 512], F32)
        ob = [self.sb(f"ob{i}", [64, 512], BF16) for i in range(2)]
        fw.dma(fw.SP, lambda e: e.dma_start(out=dl[:], in_=W["da_lambda"][l].rearrange("a b -> (a b)").partition_broadcast(128)),
               [], [self.R("dl")])
        fw.dma(fw.SP, lambda e: e.dma_start(out=mk[:], in_=self.gmask), [], [self.R("mk")])
        fw.dma(fw.SP, lambda e: e.dma_start(out=gsub[:], in_=W["da_subln_g"][l].rearrange("(v o) -> v o", o=1)),
               [], [self.R("gsub")])
        fw.op(fw.DVE, lambda e: e.tensor_scalar(out=gsub[:], in0=gsub[:], scalar1=1.0 - lam_init, scalar2=None, op0=ALU.mult),
              [self.R("gsub")], [self.R("gsub")])
        dv = dl[:].rearrange("p (a b) -> p a b", b=32)
        for i in range(2):
            fw.op(fw.DVE, lambda e, i=i: e.tensor_tensor(out=dlp[:, i, :], in0=dv[:, 2 * i, :], in1=dv[:, 2 * i + 1, :], op=ALU.mult),
                  [self.R("dl")], [self.R("dlp")])
            fw.op(fw.DVE, lambda e, i=i: e.reduce_sum(out=s12[:, i:i + 1], in_=dlp[:, i, :], axis=AX.X),
                  [self.R("dlp")], [self.R("s12")])
        fw.op(fw.ACT, lambda e: e.activation(out=s12[:], in_=s12[:], func=AF.Exp), [self.R("s12")], [self.R("s12")])
        fw.op(fw.DVE, lambda e: e.tensor_tensor(out=neglam[:], in0=s12[:, 1:2], in1=s12[:, 0:1], op=ALU.subtract),
              [self.R("s12")], [self.R("neglam")])
        fw.op(fw.DVE, lambda e: e.tensor_scalar(out=neglam[:], in0=neglam[:], scalar1=-lam_init, scalar2=None, op0=ALU.add),
              [self.R("neglam")], [self.R("neglam")])
        fw.op(fw.POOL, lambda e: e.memset(VA[:, :, 64:128], 1.0), [], [self.R("VAs")])
        scale = 32 ** -0.5
        t0 = 0
        qi = 0
        self._pi = 0
        for s, L in enumerate(self.seqs):
            nkt = L // 128
            for h in range(4):
                th, g0 = h // 2, (h % 2) * 2
                if h % 2 == 0:
                    fw.dma(fw.SP, lambda e, th=th, t0=t0, L=L: e.dma_start(
                        out=KT[:, 0:L], in_=self.dram["KA"].ap()[th * 128:(th + 1) * 128, t0:t0 + L]), [self.R("KA")], [self.R("KT")])
                vsrc = self.dram["VA"].ap()[t0:t0 + L, h * 64:(h + 1) * 64].rearrange("(kt p) v -> p kt v", p=128)
                fw.dma(fw.SP, lambda e, vsrc=vsrc, nkt=nkt: e.dma_start(out=VA[:, 0:nkt, 0:64], in_=vsrc),
                       [self.R("VA")], [self.R("VAs")])
                s1, s2 = [], []
                for qt in range(L // 512):
                    tok0 = t0 + qt * 512
                    qb = qi % 2
                    qi += 1

                    def load_q(qb=qb, tok0=tok0):
                        fw.dma(fw.SP, lambda e: e.dma_start(
                            out=QT[qb][:], in_=self.dram["QA"].ap()[th * 128:(th + 1) * 128, tok0:tok0 + 512]),
                            [self.R("QA")], [self.R(f"QT{qb}")])
                        for c in range(2):
                            eng = fw.POOL if c == 0 else fw.DVE
                            fw.op(eng, lambda e, c=c: e.tensor_scalar(
                                out=Qm[qb][c][:], in0=QT[qb][:], scalar1=mk[:, g0 + c:g0 + c + 1], scalar2=None, op0=ALU.mult),
                                [self.R(f"QT{qb}"), self.R("mk")], [self.R(f"Qm{qb}_{c}")])

                    for c in range(2):
                        for k2 in range(nkt // 2):
                            info = {}

                            def st1(c=c, k2=k2, qb=qb, info=info, first=(c == 0 and k2 == 0), load_q=load_q):
                                if first:
                                    load_q()
                                pi = self._pi
                                self._pi += 1
                                info["sb0"] = (pi % 3) * 2
                                info["pb"] = pi % 3
                                sb0 = info["sb0"]
                                for u in range(2):
                                    kt = 2 * k2 + u
                                    fw.op(fw.PE, lambda e, u=u, kt=kt: e.matmul(
                                        self.psum[sb0 + u][:], lhsT=KT[:, kt * 128:(kt + 1) * 128], rhs=Qm[qb][c][:],
                                        start=True, stop=True), [self.R("KT"), self.R(f"Qm{qb}_{c}")], [self.psum_res[sb0 + u]])

                            def st2(c=c, k2=k2, qb=qb, info=info, tok0=tok0, last=(k2 == nkt // 2 - 1)):
                                sb0, pb = info["sb0"], info["pb"]
                                po, pro = self.psum[6 + c], self.psum_res[6 + c]
                                sbig = self.psbig[sb0 // 4]
                                so = (sb0 % 4) * 512
                                fw.op(fw.ACT, lambda e: e.activation(
                                    out=PT[pb][:], in_=sbig[:, so:so + 1024], func=AF.Exp, scale=scale),
                                    [self.psum_res[sb0], self.psum_res[sb0 + 1]], [self.R(f"PT{pb}")])
                                for u in range(2):
                                    kt = 2 * k2 + u
                                    fw.op(fw.PE, lambda e, kt=kt, u=u: e.matmul(
                                        po[:], lhsT=VA[:, kt, :], rhs=PT[pb][:, u * 512:(u + 1) * 512],
                                        start=(kt == 0), stop=(kt == nkt - 1)), [self.R("VAs"), self.R(f"PT{pb}")], [pro])
                                if last:
                                    fw.op(fw.DVE, lambda e: e.reciprocal(out=rec[:], in_=po[64:128, :]), [pro], [self.R("rec")])
                                    fw.op(fw.DVE, lambda e: e.tensor_tensor(out=oc[c][:], in0=po[0:64, :], in1=rec[:], op=ALU.mult),
                                          [pro, self.R("rec")], [self.R(f"oc{c}")])
                                    if c == 1:
                                        fw.op(fw.DVE, lambda e: e.scalar_tensor_tensor(
                                            out=od[:], in0=oc[1][:], scalar=neglam[0:64, :], in1=oc[0][:], op0=ALU.mult, op1=ALU.add),
                                            [self.R("oc0"), self.R("oc1"), self.R("neglam")], [self.R("od")])
                                        o_, ro = ob[qb], self.R(f"ob{qb}")
                                        self.rms_epilogue(od[:], self.R("od"), 64, gsub[:, 0:1], None, o_[:], ro, "a")
                                        dst = self.dram["OA"].ap()[h * 64:(h + 1) * 64, tok0:tok0 + 512]
                                        fw.dma(fw.ACT, lambda e, dst=dst, o_=o_: e.dma_start(out=dst, in_=o_[:]), [ro], [self.R("OA")])

                            s1.append(st1)
                            s2.append(st2)
                s1[0]()
                if len(s1) > 1:
                    s1[1]()
                for i in range(len(s1)):
                    if i + 2 < len(s1):
                        s1[i + 2]()
                    s2[i]()
            t0 += L

    def pD(self, l):
        fw, W = self.fw, self.W
        self.norm_consts()
        Lmax = max(self.seqs)
        nmax = Lmax // 128
        lg = self.sb("lg", [128, 8], F32)
        gC = self.sb("gC", [128, 8], F32)
        rel = self.sb("rel", [128, 128], F32)
        rp = self.sb("rp", [128, 128], F32)
        rn = self.sb("rn", [128, 128], F32)
        mge = self.sb("mge", [128, 128], F32)
        mlt = self.sb("mlt", [128, 128], F32)
        ef = self.sb("ef", [128, 128], F32)
        eb = self.sb("eb", [128, 128], F32)
        col4 = self.sb("col4", [128, 512], F32)
        cp1 = self.sb("cp1", [128, 512], F32)
        c128 = self.sb("c128", [128, 512], F32)
        pidx = self.sb("pidx", [128, 1], F32)
        c127 = self.sb("c127", [128, 1], F32)
        Dm4 = [self.sb(f"Dm4_{h}", [128, 4, 128], F32) for h in range(4)]
        zf = self.sb("zf", [128, 4], F32)
        zb = self.sb("zb", [128, 4], F32)
        xif = [self.sb(f"xif{h}", [64, 512], F32) for h in range(4)]
        xib = [self.sb(f"xib{h}", [64, 512], F32) for h in range(4)]
        gret = self.sb("gret", [64, 1], F32)
        Ktm = self.sb("Ktm", [128, nmax, 64], BF16)
        Vtm = self.sb("Vtm", [128, nmax, 64], BF16)
        Kzf = self.sb("Kzf", [128, nmax, 64], BF16)
        Kzb = self.sb("Kzb", [128, nmax, 64], BF16)
        Rfb = self.sb("Rfb", [64, nmax, 64], BF16)
        Rbb = self.sb("Rbb", [64, nmax, 64], BF16)
        Rf = [self.sb(f"Rf{i}", [64, 64], F32) for i in range(2)]
        Rb = [self.sb(f"Rb{i}", [64, 64], F32) for i in range(2)]
        qg = [self.sb(f"qg{i}", [64, 512], BF16) for i in range(2)]
        kg = [self.sb(f"kg{i}", [64, 512], BF16) for i in range(2)]
        rgg = [self.sb(f"rgg{i}", [64, 512], BF16) for i in range(2)]
        sil = self.sb("sil", [64, 512], F32)
        Ab = [self.sb(f"Ab{i}", [128, 512], BF16) for i in range(2)]
        qxf = [self.sb(f"qxf{i}", [64, 512], BF16) for i in range(2)]
        qxb = [self.sb(f"qxb{i}", [64, 512], BF16) for i in range(2)]
        ob = [self.sb(f"obd{i}", [64, 512], BF16) for i in range(2)]
        src = W["ret_log_decay"][l].rearrange("a b -> (a b)").partition_broadcast(128)
        fw.dma(fw.SP, lambda e: e.dma_start(out=lg[:], in_=src), [], [self.R("lg")])
        fw.dma(fw.SP, lambda e: e.dma_start(out=rel[:], in_=self.ret_rel), [], [self.R("rel")])
        fw.dma(fw.SP, lambda e: e.dma_start(out=col4[:], in_=self.ret_col4), [], [self.R("col4")])
        fw.dma(fw.SP, lambda e: e.dma_start(out=pidx[:], in_=self.ret_pidx), [], [self.R("pidx")])
        fw.dma(fw.SP, lambda e: e.dma_start(out=gret[:], in_=W["ret_norm_g"][l].rearrange("(v o) -> v o", o=1)),
               [], [self.R("gret")])
        fw.op(fw.ACT, lambda e: e.activation(out=lg[:], in_=lg[:], func=AF.Exp), [self.R("lg")], [self.R("lg")])
        fw.op(fw.DVE, lambda e: e.tensor_scalar(out=lg[:], in0=lg[:], scalar1=-1.0, scalar2=None, op0=ALU.mult),
              [self.R("lg")], [self.R("lg")])
        fw.op(fw.ACT, lambda e: e.activation(out=gC[:], in_=lg[:], func=AF.Exp, scale=128.0), [self.R("lg")], [self.R("gC")])
        fw.op(fw.DVE, lambda e: e.tensor_scalar(out=rp[:], in0=rel[:], scalar1=0.0, scalar2=None, op0=ALU.max),
              [self.R("rel")], [self.R("rp")])
        fw.op(fw.DVE, lambda e: e.tensor_scalar(out=rn[:], in0=rel[:], scalar1=-1.0, scalar2=0.0, op0=ALU.mult, op1=ALU.max),
              [self.R("rel")], [self.R("rn")])
        fw.op(fw.DVE, lambda e: e.tensor_scalar(out=mge[:], in0=rel[:], scalar1=0.0, scalar2=None, op0=ALU.is_ge),
              [self.R("rel")], [self.R("mge")])
        fw.op(fw.DVE, lambda e: e.tensor_scalar(out=mlt[:], in0=rel[:], scalar1=0.0, scalar2=None, op0=ALU.is_lt),
              [self.R("rel")], [self.R("mlt")])
        fw.op(fw.DVE, lambda e: e.tensor_scalar(out=cp1[:], in0=col4[:], scalar1=1.0, scalar2=None, op0=ALU.add),
              [self.R("col4")], [self.R("cp1")])
        fw.op(fw.DVE, lambda e: e.tensor_scalar(out=c128[:], in0=col4[:], scalar1=-1.0, scalar2=128.0, op0=ALU.mult, op1=ALU.add),
              [self.R("col4")], [self.R("c128")])
        fw.op(fw.DVE, lambda e: e.tensor_scalar(out=c127[:], in0=pidx[:], scalar1=-1.0, scalar2=127.0, op0=ALU.mult, op1=ALU.add),
              [self.R("pidx")], [self.R("c127")])
        for h in range(4):
            lf, lb = lg[:, h:h + 1], lg[:, 4 + h:5 + h]
            fw.op(fw.ACT, lambda e, lf=lf: e.activation(out=ef[:], in_=rp[:], func=AF.Exp, scale=lf),
                  [self.R("rp"), self.R("lg")], [self.R("ef")])
            fw.op(fw.ACT, lambda e, lb=lb: e.activation(out=eb[:], in_=rn[:], func=AF.Exp, scale=lb),
                  [self.R("rn"), self.R("lg")], [self.R("eb")])
            fw.op(fw.DVE, lambda e: e.tensor_tensor(out=ef[:], in0=ef[:], in1=mge[:], op=ALU.mult),
                  [self.R("ef"), self.R("mge")], [self.R("ef")])
            fw.op(fw.DVE, lambda e: e.tensor_tensor(out=eb[:], in0=eb[:], in1=mlt[:], op=ALU.mult),
                  [self.R("eb"), self.R("mlt")], [self.R("eb")])
            for r in range(4):
                fw.op(fw.DVE, lambda e, h=h, r=r: e.tensor_tensor(out=Dm4[h][:, r, :], in0=ef[:], in1=eb[:], op=ALU.add),
                      [self.R("ef"), self.R("eb")], [self.R(f"Dm4_{h}")])
            fw.op(fw.ACT, lambda e, h=h, lf=lf: e.activation(out=zf[:, h:h + 1], in_=c127[:], func=AF.Exp, scale=lf),
                  [self.R("c127"), self.R("lg")], [self.R("zf")])
            fw.op(fw.ACT, lambda e, h=h, lb=lb: e.activation(out=zb[:, h:h + 1], in_=pidx[:], func=AF.Exp, scale=lb),
                  [self.R("pidx"), self.R("lg")], [self.R("zb")])
            fw.op(fw.ACT, lambda e, h=h, lf=lf: e.activation(out=xif[h][:], in_=cp1[0:64, :], func=AF.Exp, scale=lf[0:64, :]),
                  [self.R("cp1"), self.R("lg")], [self.R(f"xif{h}")])
            fw.op(fw.ACT, lambda e, h=h, lb=lb: e.activation(out=xib[h][:], in_=c128[0:64, :], func=AF.Exp, scale=lb[0:64, :]),
                  [self.R("c128"), self.R("lg")], [self.R(f"xib{h}")])
        t0 = 0
        self._gi = 0
        for s, L in enumerate(self.seqs):
            n = L // 128
            for h in range(4):
                ksrc = self.dram["RK"].ap()[t0:t0 + L, h * 64:(h + 1) * 64].rearrange("(c p) v -> p c v", p=128)
                vsrc = self.dram["RV"].ap()[t0:t0 + L, h * 64:(h + 1) * 64].rearrange("(c p) v -> p c v", p=128)
                fw.dma(fw.SP, lambda e, ksrc=ksrc, n=n: e.dma_start(out=Ktm[:, 0:n, :], in_=ksrc), [self.R("RK")], [self.R("Ktm")])
                fw.dma(fw.SP, lambda e, vsrc=vsrc, n=n: e.dma_start(out=Vtm[:, 0:n, :], in_=vsrc), [self.R("RV")], [self.R("Vtm")])
                fw.op(fw.DVE, lambda e, h=h, n=n: e.tensor_scalar(out=Kzf[:, 0:n, :], in0=Ktm[:, 0:n, :], scalar1=zf[:, h:h + 1],
                                                                  scalar2=None, op0=ALU.mult), [self.R("Ktm"), self.R("zf")], [self.R("Kzf")])
                fw.op(fw.POOL, lambda e, h=h, n=n: e.tensor_scalar(out=Kzb[:, 0:n, :], in0=Ktm[:, 0:n, :], scalar1=zb[:, h:h + 1],
                                                                   scalar2=None, op0=ALU.mult), [self.R("Ktm"), self.R("zb")], [self.R("Kzb")])
                fw.op(fw.DVE, lambda e: e.memset(Rf[0][:], 0.0), [], [self.R("Rf0")])
                fw.op(fw.DVE, lambda e: e.memset(Rb[0][:], 0.0), [], [self.R("Rb0")])
                for c in range(n):
                    cur, nxt = c % 2, (c + 1) % 2
                    fw.op(fw.POOL, lambda e, c=c, cur=cur: e.tensor_copy(out=Rfb[:, c, :], in_=Rf[cur][:]),
                          [self.R(f"Rf{cur}")], [self.R("Rfb")])
                    if c < n - 1:
                        pk, prk = self.ps()
                        fw.op(fw.PE, lambda e, pk=pk, c=c: e.matmul(pk[0:64, 0:64], lhsT=Kzf[:, c, :], rhs=Vtm[:, c, :],
                                                                    start=True, stop=True), [self.R("Kzf"), self.R("Vtm")], [prk])
                        fw.op(fw.DVE, lambda e, pk=pk, cur=cur, nxt=nxt, h=h: e.scalar_tensor_tensor(
                            out=Rf[nxt][:], in0=Rf[cur][:], scalar=gC[0:64, h:h + 1], in1=pk[0:64, 0:64], op0=ALU.mult, op1=ALU.add),
                            [prk, self.R(f"Rf{cur}"), self.R("gC")], [self.R(f"Rf{nxt}")])
                for i, c in enumerate(range(n - 1, -1, -1)):
                    cur, nxt = i % 2, (i + 1) % 2
                    fw.op(fw.POOL, lambda e, c=c, cur=cur: e.tensor_copy(out=Rbb[:, c, :], in_=Rb[cur][:]),
                          [self.R(f"Rb{cur}")], [self.R("Rbb")])
                    if c > 0:
                        pk, prk = self.ps()
                        fw.op(fw.PE, lambda e, pk=pk, c=c: e.matmul(pk[0:64, 0:64], lhsT=Kzb[:, c, :], rhs=Vtm[:, c, :],
                                                                    start=True, stop=True), [self.R("Kzb"), self.R("Vtm")], [prk])
                        fw.op(fw.DVE, lambda e, pk=pk, cur=cur, nxt=nxt, h=h: e.scalar_tensor_tensor(
                            out=Rb[nxt][:], in0=Rb[cur][:], scalar=gC[0:64, 4 + h:5 + h], in1=pk[0:64, 0:64], op0=ALU.mult, op1=ALU.add),
                            [prk, self.R(f"Rb{cur}"), self.R("gC")], [self.R(f"Rb{nxt}")])
                s1, s2 = [], []
                for g in range(L // 512):
                    info = {}

                    def st1(g=g, h=h, t0=t0, info=info):
                        tok0 = t0 + g * 512
                        b = self._gi % 2
                        self._gi += 1
                        fw.dma(fw.SP, lambda e, b=b, h=h, tok0=tok0: e.dma_start(
                            out=qg[b][:], in_=self.dram["RQ"].ap()[h * 64:(h + 1) * 64, tok0:tok0 + 512]), [self.R("RQ")], [self.R(f"qg{b}")])
                        fw.dma(fw.SP, lambda e, b=b, h=h, tok0=tok0: e.dma_start(
                            out=kg[b][:], in_=self.dram["RKT"].ap()[h * 64:(h + 1) * 64, tok0:tok0 + 512]), [self.R("RKT")], [self.R(f"kg{b}")])
                        fw.dma(fw.SP, lambda e, b=b, h=h, tok0=tok0: e.dma_start(
                            out=rgg[b][:], in_=self.dram["RG"].ap()[h * 64:(h + 1) * 64, tok0:tok0 + 512]), [self.R("RG")], [self.R(f"rgg{b}")])
                        pS, prS = self.ps()
                        for r in range(4):
                            fw.op(fw.PE, lambda e, pS=pS, r=r, b=b: e.matmul(
                                pS[:, r * 128:(r + 1) * 128], lhsT=kg[b][:, r * 128:(r + 1) * 128], rhs=qg[b][:, r * 128:(r + 1) * 128],
                                start=True, stop=True), [self.R(f"kg{b}"), self.R(f"qg{b}")], [prS])
                        info.update(b=b, pS=pS, prS=prS, tok0=tok0)

                    def st2(g=g, h=h, t0=t0, info=info):
                        b, pS, prS, tok0 = info['b'], info['pS'], info['prS'], info['tok0']
                        fw.op(fw.DVE, lambda e, pS=pS, b=b, h=h: e.tensor_tensor(
                            out=Ab[b][:], in0=pS[:], in1=Dm4[h][:].rearrange("p a b -> p (a b)"), op=ALU.mult),
                            [prS, self.R(f"Dm4_{h}")], [self.R(f"Ab{b}")])
                        fw.op(fw.POOL, lambda e, b=b, h=h: e.tensor_tensor(out=qxf[b][:], in0=qg[b][:], in1=xif[h][:], op=ALU.mult),
                              [self.R(f"qg{b}"), self.R(f"xif{h}")], [self.R(f"qxf{b}")])
                        fw.op(fw.POOL, lambda e, b=b, h=h: e.tensor_tensor(out=qxb[b][:], in0=qg[b][:], in1=xib[h][:], op=ALU.mult),
                              [self.R(f"qg{b}"), self.R(f"xib{h}")], [self.R(f"qxb{b}")])
                        pO, prO = self.ps()
                        for r in range(4):
                            c = g * 4 + r
                            cs = slice(r * 128, (r + 1) * 128)
                            fw.op(fw.PE, lambda e, pO=pO, c=c, cs=cs, b=b: e.matmul(
                                pO[0:64, cs], lhsT=Vtm[:, c, :], rhs=Ab[b][:, cs], start=True, stop=False),
                                [self.R("Vtm"), self.R(f"Ab{b}")], [prO])
                            fw.op(fw.PE, lambda e, pO=pO, c=c, cs=cs, b=b: e.matmul(
                                pO[0:64, cs], lhsT=Rfb[:, c, :], rhs=qxf[b][:, cs], start=False, stop=False),
                                [self.R("Rfb"), self.R(f"qxf{b}")], [prO])
                            fw.op(fw.PE, lambda e, pO=pO, c=c, cs=cs, b=b: e.matmul(
                                pO[0:64, cs], lhsT=Rbb[:, c, :], rhs=qxb[b][:, cs], start=False, stop=True),
                                [self.R("Rbb"), self.R(f"qxb{b}")], [prO])
                        fw.op(fw.ACT, lambda e, b=b: e.activation(out=sil[:], in_=rgg[b][:], func=AF.Silu),
                              [self.R(f"rgg{b}")], [self.R("sil")])
                        o_, ro = ob[b], self.R(f"obd{b}")
                        self.rms_epilogue(pO[0:64, :], prO, 64, gret[:, 0:1], (sil[:], self.R("sil")), o_[:], ro, "d")
                        dst = self.dram["OD"].ap()[h * 64:(h + 1) * 64, tok0:tok0 + 512]
                        fw.dma(fw.ACT, lambda e, dst=dst, o_=o_: e.dma_start(out=dst, in_=o_[:]), [ro], [self.R("OD")])

                    s1.append(st1)
                    s2.append(st2)
                s1[0]()
                for i in range(len(s1)):
                    if i + 1 < len(s1):
                        s1[i + 1]()
                    s2[i]()
            t0 += L

import math
import numpy as np
import concourse.bass as bass


NEG = -400.0


def bc_ap(a, pos, count):
    ap = [list(x) for x in a.ap]
    ap.insert(1 + pos, [0, count])
    return bass.AP(tensor=a.tensor, offset=a.offset, ap=ap)


class K4(K3):
    def extra_inputs(self):
        super().extra_inputs()
        self.na_shift = self.din("na_shift", [128, 31, 64])
        self.na_cmask = self.din("na_cmask", [128, 64])

    @staticmethod
    def host_consts_extra():
        p = np.arange(128) % 64
        qc = np.arange(64)
        sh = np.zeros((128, 31, 64), np.float32)
        for d in range(31):
            sh[:, d, :] = (p[:, None] - qc[None, :] + 15 == d)
        start = np.clip(qc - 8, 0, 48)
        inwin = (p[:, None] >= start[None, :]) & (p[:, None] < start[None, :] + 16)
        cm = np.where(inwin, 0.0, NEG).astype(np.float32)
        return dict(na_shift=sh, na_cmask=cm)

    def pB(self, l):
        fw, W = self.fw, self.W
        Lmax = max(self.seqs)
        nkt_max = Lmax // 128
        shift = self.sb("shift", [128, 31, 64], F32)
        cmask = self.sb("cmask", [128, 64], F32)
        rpbb = self.sb("rpbb", [128, 4 * 15 * 31], F32)
        E = [self.sb(f"E{h}", [128, 15, 64], F32) for h in range(4)]
        tmpE = [self.sb(f"tmpE{i}", [128, 15, 64], F32) for i in range(2)]
        qT = self.sb("qTb", [64, Lmax], BF16)
        kT = self.sb("kTb", [64, Lmax], BF16)
        VB = self.sb("VBs", [128, nkt_max, 128], BF16)
        PT = [self.sb(f"PTb{i}", [128, 640], BF16) for i in range(2)]
        rec = [self.sb(f"recb{i}", [64, 128], F32) for i in range(2)]
        stg = [self.sb(f"stgb{i}", [64, 512], BF16) for i in range(2)]
        fw.dma(fw.SP, lambda e: e.dma_start(out=shift[:], in_=self.na_shift), [], [self.R("shift")])
        fw.dma(fw.SP, lambda e: e.dma_start(out=cmask[:], in_=self.na_cmask), [], [self.R("cmask")])
        fw.dma(fw.SP, lambda e: e.dma_start(
            out=rpbb[:], in_=W["na_rpb"][l].rearrange("a b c -> (a b c)").partition_broadcast(128)), [], [self.R("rpbb")])
        fw.op(fw.DVE, lambda e: e.tensor_scalar(out=rpbb[:], in0=rpbb[:], scalar1=8.0, scalar2=None, op0=ALU.mult),
              [self.R("rpbb")], [self.R("rpbb")])
        for h in range(4):
            cm_b = bc_ap(cmask[:], 0, 15)
            fw.op(fw.DVE, lambda e, h=h, cm_b=cm_b: e.tensor_copy(out=E[h][:], in_=cm_b), [self.R("cmask")], [self.R(f"E{h}")])
            for d in range(31):
                tb, rtb = tmpE[d % 2], self.R(f"tmpE{d % 2}")
                sh_b = bc_ap(shift[:, d, :], 0, 15)
                base = h * 465 + d
                rv = rpbb[:, base: base + 15 * 31 - 30: 31] if False else None
                a = rpbb[:, base:base + 1]
                r_b = bass.AP(tensor=a.tensor, offset=a.offset, ap=[list(a.ap[0]), [31, 15], [0, 64]])
                eng = fw.DVE if d % 2 == 0 else fw.POOL
                fw.op(eng, lambda e, tb=tb, sh_b=sh_b, r_b=r_b: e.tensor_tensor(out=tb[:], in0=sh_b, in1=r_b, op=ALU.mult),
                      [self.R("shift"), self.R("rpbb")], [rtb])
                fw.op(eng, lambda e, tb=tb, h=h: e.tensor_tensor(out=E[h][:], in0=E[h][:], in1=tb[:], op=ALU.add),
                      [rtb, self.R(f"E{h}")], [self.R(f"E{h}")])
        fw.op(fw.POOL, lambda e: e.memset(VB[:, :, 64:128], 1.0), [], [self.R("VBs")])
        pbcache = {}

        def get_pb(key):
            if key in pbcache:
                return pbcache[key]
            idx = len(pbcache)
            tiles = []
            for h in range(4):
                t = self.sb(f"PB{idx}_{h}", [128, 128], BF16)
                r = self.R(f"PB{idx}_{h}")
                for kp_ in range(2):
                    for qp_ in range(2):
                        dr = key[kp_][qp_]
                        o = t[kp_ * 64:(kp_ + 1) * 64, qp_ * 64:(qp_ + 1) * 64]
                        if dr is None:
                            fw.op(fw.POOL, lambda e, o=o: e.memset(o, NEG), [], [r])
                        else:
                            fw.op(fw.POOL, lambda e, o=o, h=h, dr=dr, kp_=kp_: e.tensor_copy(
                                out=o, in_=E[h][kp_ * 64:(kp_ + 1) * 64, dr, :]), [self.R(f"E{h}")], [r])
                tiles.append((t, r))
            pbcache[key] = tiles
            return tiles

        t0 = 0
        cnt = 0
        for s, L in enumerate(self.seqs):
            rows = L // 64
            nkt = L // 128
            for h in range(4):
                fw.dma(fw.SP, lambda e, h=h, t0=t0, L=L: e.dma_start(
                    out=qT[:, 0:L], in_=self.dram["QB"].ap()[h * 64:(h + 1) * 64, t0:t0 + L]), [self.R("QB")], [self.R("qTb")])
                fw.dma(fw.SP, lambda e, h=h, t0=t0, L=L: e.dma_start(
                    out=kT[:, 0:L], in_=self.dram["KB"].ap()[h * 64:(h + 1) * 64, t0:t0 + L]), [self.R("KB")], [self.R("kTb")])
                vsrc = self.dram["VB"].ap()[t0:t0 + L, h * 64:(h + 1) * 64].rearrange("(kt p) v -> p kt v", p=128)
                fw.dma(fw.SP, lambda e, vsrc=vsrc, nkt=nkt: e.dma_start(out=VB[:, 0:nkt, 0:64], in_=vsrc),
                       [self.R("VB")], [self.R("VBs")])
                s1, s2 = [], []
                for qi in range(rows // 2):
                    r0 = 2 * qi
                    rs = [min(max(qr - 4, 0), rows - 8) for qr in (r0, r0 + 1)]
                    kp_lo = min(rs) // 2
                    kp_hi = (max(rs) + 7) // 2
                    kps = list(range(kp_lo, kp_hi + 1))
                    assert len(kps) <= 5
                    b = cnt % 2
                    cnt += 1
                    sb0 = b * 2
                    pbs = []
                    for idx, kp in enumerate(kps):
                        key = tuple(tuple((2 * kp + kp_ - (r0 + qp_) + 7) if rs[qp_] <= 2 * kp + kp_ < rs[qp_] + 8 else None
                                          for qp_ in range(2)) for kp_ in range(2))
                        pbs.append(get_pb(key)[h])

                    def st1(kps=kps, pbs=pbs, sb0=sb0, qi=qi):
                        for idx, kp in enumerate(kps):
                            pbt, pbr = pbs[idx]
                            bank = sb0 + idx // 4
                            cs = slice((idx % 4) * 128, (idx % 4 + 1) * 128)
                            fw.op(fw.PE, lambda e, bank=bank, cs=cs, kp=kp: e.matmul(
                                self.psum[bank][:, cs], lhsT=kT[:, kp * 128:(kp + 1) * 128], rhs=qT[:, qi * 128:(qi + 1) * 128],
                                start=True, stop=False), [self.R("kTb"), self.R("qTb")], [self.psum_res[bank]])
                            fw.op(fw.PE, lambda e, bank=bank, cs=cs, pbt=pbt: e.matmul(
                                self.psum[bank][:, cs], lhsT=self.idb[:], rhs=pbt[:], start=False, stop=True),
                                [pbr, self.R("idb")], [self.psum_res[bank]])

                    def st2(kps=kps, sb0=sb0, qi=qi, b=b, h=h, t0=t0):
                        n = len(kps)
                        fw.op(fw.ACT, lambda e: e.activation(
                            out=PT[b][:, 0:n * 128], in_=self.psbig[0][:, sb0 * 512: sb0 * 512 + n * 128], func=AF.Exp, scale=0.125),
                            [self.psum_res[sb0], self.psum_res[sb0 + 1]], [self.R(f"PTb{b}")])
                        po, pro = self.psum[4 + b], self.psum_res[4 + b]
                        for idx, kp in enumerate(kps):
                            fw.op(fw.PE, lambda e, kp=kp, idx=idx: e.matmul(
                                po[:, 0:128], lhsT=VB[:, kp, :], rhs=PT[b][:, idx * 128:(idx + 1) * 128],
                                start=(idx == 0), stop=(idx == n - 1)), [self.R("VBs"), self.R(f"PTb{b}")], [pro])
                        fw.op(fw.DVE, lambda e: e.reciprocal(out=rec[b][:], in_=po[64:128, 0:128]), [pro], [self.R(f"recb{b}")])
                        sgi = (qi // 4) % 2
                        sg, rsg = stg[sgi], self.R(f"stgb{sgi}")
                        q4 = qi % 4
                        fw.op(fw.DVE, lambda e: e.tensor_tensor(
                            out=sg[:, q4 * 128:(q4 + 1) * 128], in0=po[0:64, 0:128], in1=rec[b][:], op=ALU.mult),
                            [pro, self.R(f"recb{b}")], [rsg])
                        if q4 == 3:
                            tok0 = t0 + (qi - 3) * 128
                            dst = self.dram["OB"].ap()[h * 64:(h + 1) * 64, tok0:tok0 + 512]
                            fw.dma(fw.ACT, lambda e, dst=dst, sg=sg: e.dma_start(out=dst, in_=sg[:]), [rsg], [self.R("OB")])

                    s1.append(st1)
                    s2.append(st2)
                s1[0]()
                for i in range(len(s1)):
                    if i + 1 < len(s1):
                        s1[i + 1]()
                    s2[i]()
            t0 += L

import math
import numpy as np
import concourse.bass as bass
import concourse.mybir as mybir


I32 = mybir.dt.int32
TWO_PI_HI = 6.28125
TWO_PI_LO = 2.0 * math.pi - 6.28125


class K5(K4):
    def extra_inputs(self):
        super().extra_inputs()
        self.s5_maskg = self.din("s5_maskg", [128, 2])

    @staticmethod
    def host_consts_extra():
        d = K4.host_consts_extra()
        p = np.arange(128)
        d["s5_maskg"] = (p[:, None] // 64 == np.arange(2)[None, :]).astype(np.float32)
        return d

    def pC(self, l):
        fw, W = self.fw, self.W
        nc = self.nc
        TB = 512

        def T(name, shape, dt=F32):
            return self.sb(name, shape, dt), self.R(name)

        def tt(eng, out, in0, in1, op, reads, writes):
            fw.op(eng, lambda e: e.tensor_tensor(out=out, in0=in0, in1=in1, op=op), reads, writes)

        are, r_are = T("c_are", [128, 2, 8])
        aim, r_aim = T("c_aim", [128, 2, 8])
        ldt, r_ldt = T("c_ldt", [128, 2, 8])
        names = ["dt", "mag", "th", "kf", "half", "ah", "sh", "ch", "t1", "t2", "cs", "sn", "ar", "ai", "den", "am1",
                 "fre", "fim", "t3"]
        P = {}
        for n_ in names:
            P[n_] = T("c_" + n_, [128, 16])
        ki, r_ki = T("c_ki", [128, 16], I32)
        hpi, r_hpi = T("c_hpi", [128, 1])
        mg, r_mg = T("c_mg", [128, 2])
        fw.op(fw.DVE, lambda e: e.memset(hpi[:], math.pi / 2), [], [r_hpi])
        fw.dma(fw.SP, lambda e: e.dma_start(out=mg[:], in_=self.s5_maskg), [], [r_mg])
        for dr in range(2):
            for (tile_, rr, nm) in ((are, r_are, "s5_a_re"), (aim, r_aim, "s5_a_im")):
                a_ = W[nm][l, dr]
                src = bass.AP(tensor=a_.tensor, offset=a_.offset, ap=[[1, 128], [128, 8]])
                fw.dma(fw.SP, lambda e, tile_=tile_, dr=dr, src=src: e.dma_start(
                    out=tile_[:, dr, :], in_=src, allow_slow_non_contiguous=True), [], [rr])
            for gh in range(2):
                a = W["s5_log_dt"][l, dr]
                src = bass.AP(tensor=a.tensor, offset=a.offset + gh, ap=[[0, 64], [2, 8]])
                fw.dma(fw.SP, lambda e, dr=dr, gh=gh, src=src: e.dma_start(
                    out=ldt[gh * 64:(gh + 1) * 64, dr, :], in_=src, allow_slow_non_contiguous=True), [], [r_ldt])
        lr = are[:].rearrange("p a b -> p (a b)")
        li = aim[:].rearrange("p a b -> p (a b)")
        ld = ldt[:].rearrange("p a b -> p (a b)")
        V = {k_: v[0][:] for k_, v in P.items()}
        Rr = {k_: v[1] for k_, v in P.items()}
        dv = fw.DVE
        fw.op(dv, lambda e: e.tensor_scalar(out=lr, in0=lr, scalar1=-1e-4, scalar2=None, op0=ALU.min), [r_are], [r_are])
        fw.op(fw.ACT, lambda e: e.activation(out=V["dt"], in_=ld, func=AF.Exp), [r_ldt], [Rr["dt"]])
        tt(dv, V["t1"], lr, V["dt"], ALU.mult, [r_are, Rr["dt"]], [Rr["t1"]])
        fw.op(fw.ACT, lambda e: e.activation(out=V["mag"], in_=V["t1"], func=AF.Exp), [Rr["t1"]], [Rr["mag"]])
        tt(dv, V["th"], li, V["dt"], ALU.mult, [r_aim, Rr["dt"]], [Rr["th"]])
        fw.op(dv, lambda e: e.tensor_scalar(out=V["t2"], in0=V["th"], scalar1=1.0 / (2 * math.pi), scalar2=None, op0=ALU.mult),
              [Rr["th"]], [Rr["t2"]])
        fw.op(dv, lambda e: e.tensor_copy(out=ki[:], in_=V["t2"]), [Rr["t2"]], [r_ki])
        fw.op(dv, lambda e: e.tensor_copy(out=V["kf"], in_=ki[:]), [r_ki], [Rr["kf"]])
        fw.op(dv, lambda e: e.scalar_tensor_tensor(out=V["t3"], in0=V["kf"], scalar=-TWO_PI_HI, in1=V["th"], op0=ALU.mult, op1=ALU.add),
              [Rr["kf"], Rr["th"]], [Rr["t3"]])
        fw.op(dv, lambda e: e.scalar_tensor_tensor(out=V["half"], in0=V["kf"], scalar=-TWO_PI_LO, in1=V["t3"], op0=ALU.mult, op1=ALU.add),
              [Rr["kf"], Rr["t3"]], [Rr["half"]])
        fw.op(dv, lambda e: e.tensor_scalar(out=V["half"], in0=V["half"], scalar1=0.5, scalar2=None, op0=ALU.mult),
              [Rr["half"]], [Rr["half"]])
        fw.op(dv, lambda e: e.scalar_tensor_tensor(out=V["ah"], in0=V["half"], scalar=-1.0, in1=V["half"], op0=ALU.mult, op1=ALU.max),
              [Rr["half"]], [Rr["ah"]])
        fw.op(fw.ACT, lambda e: e.activation(out=V["sh"], in_=V["half"], func=AF.Sin), [Rr["half"]], [Rr["sh"]])
        fw.op(fw.ACT, lambda e: e.activation(out=V["ch"], in_=V["ah"], func=AF.Sin, scale=-1.0, bias=hpi[:]),
              [Rr["ah"], r_hpi], [Rr["ch"]])
        tt(dv, V["t1"], V["ch"], V["ch"], ALU.mult, [Rr["ch"]], [Rr["t1"]])
        tt(dv, V["t2"], V["sh"], V["sh"], ALU.mult, [Rr["sh"]], [Rr["t2"]])
        tt(dv, V["cs"], V["t1"], V["t2"], ALU.subtract, [Rr["t1"], Rr["t2"]], [Rr["cs"]])
        fw.op(dv, lambda e: e.scalar_tensor_tensor(out=V["sn"], in0=V["sh"], scalar=2.0, in1=V["ch"], op0=ALU.mult, op1=ALU.mult),
              [Rr["sh"], Rr["ch"]], [Rr["sn"]])
        tt(dv, V["ar"], V["mag"], V["cs"], ALU.mult, [Rr["mag"], Rr["cs"]], [Rr["ar"]])
        tt(dv, V["ai"], V["mag"], V["sn"], ALU.mult, [Rr["mag"], Rr["sn"]], [Rr["ai"]])
        tt(dv, V["t1"], lr, lr, ALU.mult, [r_are], [Rr["t1"]])
        tt(dv, V["t2"], li, li, ALU.mult, [r_aim], [Rr["t2"]])
        tt(dv, V["den"], V["t1"], V["t2"], ALU.add, [Rr["t1"], Rr["t2"]], [Rr["den"]])
        fw.op(dv, lambda e: e.reciprocal(out=V["den"], in_=V["den"]), [Rr["den"]], [Rr["den"]])
        fw.op(dv, lambda e: e.tensor_scalar(out=V["am1"], in0=V["ar"], scalar1=-1.0, scalar2=None, op0=ALU.add), [Rr["ar"]], [Rr["am1"]])
        tt(dv, V["t1"], V["am1"], lr, ALU.mult, [Rr["am1"], r_are], [Rr["t1"]])
        tt(dv, V["t2"], V["ai"], li, ALU.mult, [Rr["ai"], r_aim], [Rr["t2"]])
        tt(dv, V["t3"], V["t1"], V["t2"], ALU.add, [Rr["t1"], Rr["t2"]], [Rr["t3"]])
        tt(dv, V["fre"], V["t3"], V["den"], ALU.mult, [Rr["t3"], Rr["den"]], [Rr["fre"]])
        tt(dv, V["t1"], V["ai"], lr, ALU.mult, [Rr["ai"], r_are], [Rr["t1"]])
        tt(dv, V["t2"], V["am1"], li, ALU.mult, [Rr["am1"], r_aim], [Rr["t2"]])
        tt(dv, V["t3"], V["t1"], V["t2"], ALU.subtract, [Rr["t1"], Rr["t2"]], [Rr["t3"]])
        tt(dv, V["fim"], V["t3"], V["den"], ALU.mult, [Rr["t3"], Rr["den"]], [Rr["fim"]])
        Bt = {}
        for nm in ("b_re", "b_im", "c_re", "c_im"):
            Bt[nm] = T("c_" + nm, [128, 2, 8, 16])
            for dr in range(2):
                a_ = W["s5_" + nm][l, dr]
                if nm[0] == "b":
                    src = bass.AP(tensor=a_.tensor, offset=a_.offset, ap=[[16, 128], [2048, 8], [1, 16]])
                    fw.dma(fw.SP, lambda e, nm=nm, dr=dr, src=src: e.dma_start(
                        out=Bt[nm][0][:, dr, :, :], in_=src, allow_slow_non_contiguous=True), [], [Bt[nm][1]])
                else:
                    for gh in range(2):
                        for pt_ in range(8):
                            src = bass.AP(tensor=a_.tensor, offset=a_.offset + gh * 1024 + pt_ * 2048, ap=[[1, 64], [64, 16]])
                            fw.dma(fw.SP, lambda e, nm=nm, dr=dr, src=src, gh=gh, pt_=pt_: e.dma_start(
                                out=Bt[nm][0][gh * 64:(gh + 1) * 64, dr, pt_, :], in_=src, allow_slow_non_contiguous=True), [], [Bt[nm][1]])
        bt_re, r_btre = T("c_btre", [128, 2, 8, 16])
        bt_im, r_btim = T("c_btim", [128, 2, 8, 16])
        tmpb = [T(f"c_tmpb{i}", [128, 8, 16]) for i in range(2)]
        for dr in range(2):
            fr_b = bc_ap(V["fre"][:, dr * 8:(dr + 1) * 8], 1, 16)
            fi_b = bc_ap(V["fim"][:, dr * 8:(dr + 1) * 8], 1, 16)
            bre, bim = Bt["b_re"][0][:, dr, :, :], Bt["b_im"][0][:, dr, :, :]
            rb = [Bt["b_re"][1], Bt["b_im"][1], Rr["fre"], Rr["fim"]]
            tt(dv, tmpb[0][0][:], bre, fr_b, ALU.mult, rb, [tmpb[0][1]])
            tt(dv, tmpb[1][0][:], bim, fi_b, ALU.mult, rb, [tmpb[1][1]])
            tt(dv, bt_re[:, dr, :, :], tmpb[0][0][:], tmpb[1][0][:], ALU.subtract, [tmpb[0][1], tmpb[1][1]], [r_btre])
            tt(dv, tmpb[0][0][:], bim, fr_b, ALU.mult, rb, [tmpb[0][1]])
            tt(dv, tmpb[1][0][:], bre, fi_b, ALU.mult, rb, [tmpb[1][1]])
            tt(dv, bt_im[:, dr, :, :], tmpb[0][0][:], tmpb[1][0][:], ALU.add, [tmpb[0][1], tmpb[1][1]], [r_btim])
        BT = {}
        CT = {}
        xm, r_xm = T("c_xm", [128, 2, 16])
        for dr in range(2):
            for pt_ in range(8):
                for ri, (src_t, rs_) in enumerate(((bt_re, r_btre), (bt_im, r_btim))):
                    t_, r_ = T(f"c_BT{dr}_{pt_}_{ri}", [128, 128], BF16)
                    BT[(dr, pt_, ri)] = (t_, r_)
                    fw.op(fw.POOL, lambda e, t_=t_: e.memset(t_[:], 0.0), [], [r_])
                    for g2 in range(2):
                        fw.op(dv, lambda e, src_t=src_t, dr=dr, pt_=pt_, g2=g2: e.tensor_scalar(
                            out=xm[:, g2, :], in0=src_t[:, dr, pt_, :], scalar1=mg[:, g2:g2 + 1], scalar2=None, op0=ALU.mult),
                            [rs_, r_mg], [r_xm])
                    pp, prp = self.ps()
                    fw.op(fw.PE, lambda e, pp=pp: e.transpose(out=pp[0:32, 0:128], in_=xm[:].rearrange("p a b -> p (a b)"),
                                                              identity=self.idf[:]), [r_xm, self.R("idf")], [prp])
                    r0 = (pt_ % 4) * 32
                    fw.op(fw.ACT, lambda e, pp=pp, t_=t_, r0=r0: e.activation(out=t_[r0:r0 + 32, :], in_=pp[0:32, 0:128], func=AF.Copy),
                          [prp], [r_])
                for ri, (nm, sgn) in enumerate((("c_re", 1.0), ("c_im", -1.0))):
                    t_, r_ = T(f"c_CT{dr}_{pt_}_{ri}", [128, 128], BF16)
                    CT[(dr, pt_, ri)] = (t_, r_)
                    fw.op(fw.POOL, lambda e, t_=t_: e.memset(t_[:], 0.0), [], [r_])
                    for g2 in range(2):
                        c0 = (pt_ % 4) * 32 + g2 * 16
                        fw.op(dv, lambda e, t_=t_, nm=nm, dr=dr, pt_=pt_, g2=g2, c0=c0, sgn=sgn: e.tensor_scalar(
                            out=t_[:, c0:c0 + 16], in0=Bt[nm][0][:, dr, pt_, :], scalar1=mg[:, g2:g2 + 1], scalar2=sgn,
                            op0=ALU.mult, op1=ALU.mult), [Bt[nm][1], r_mg], [r_])
        return self.pC_main(l, dict(locals()))

    def pC_main(self, l, ctx):
        fw, W = self.fw, self.W
        globals_needed = None
        T, tt, V, Rr, dv, BT, CT, TB = ctx['T'], ctx['tt'], ctx['V'], ctx['Rr'], ctx['dv'], ctx['BT'], ctx['CT'], ctx['TB']
        uT = [T(f"c_uT{i}", [128, 2, TB], BF16) for i in range(2)]
        bu = [T(f"c_bu{ri}", [128, 8, TB]) for ri in range(2)]
        H = [T(f"c_H{ri}", [128, 8, TB + 1]) for ri in range(2)]
        Hb = [T(f"c_Hb{ri}", [128, 8, TB], BF16) for ri in range(2)]
        sc = {k_: T("c_s" + k_, [128, 8]) for k_ in ("a", "b", "c", "d", "e", "f")}
        yst = [T(f"c_yst{i}", [128, TB]) for i in range(2)]
        yc = [T(f"c_yc{i}", [128, TB]) for i in range(2)]
        gfp = [T(f"c_g{i}", [128, TB]) for i in range(2)]
        gbf = [T(f"c_gb{i}", [128, TB], BF16) for i in range(2)]
        sg = [T(f"c_sg{i}", [128, TB]) for i in range(2)]
        ocb = [T(f"c_oc{i}", [128, TB], BF16) for i in range(2)]
        dcol, r_dcol = T("c_dcol", [128, 2])
        gbcol, r_gbcol = T("c_gbcol", [128, 2])
        glu, r_glu = T("c_glu", [128, 2, 256], BF16)
        fw.dma(fw.SP, lambda e: e.dma_start(out=dcol[:], in_=W["s5_d"][l].rearrange("(hf p) -> p hf", p=128),
                                            allow_slow_non_contiguous=True), [], [r_dcol])
        fw.dma(fw.SP, lambda e: e.dma_start(out=gbcol[:], in_=W["s5_glu_b"][l].rearrange("(hf p) -> p hf", p=128),
                                            allow_slow_non_contiguous=True), [], [r_gbcol])
        fw.dma(fw.SP, lambda e: e.dma_start(out=glu[:], in_=self.Wb[("glu", l)].rearrange("(kt p) c -> p kt c", p=128)),
               [self.R(f"wb_glu_{l}")], [r_glu])
        if "YC" not in self.dram:
            self.dscratch("YC", [256, self.T], F32)
        YC = self.dram["YC"].ap()
        t0 = 0
        ui = 0
        for s, L in enumerate(self.seqs):
            nb = L // TB
            for dr in range(2):
                arv = V["ar"][:, dr * 8:(dr + 1) * 8]
                aiv = V["ai"][:, dr * 8:(dr + 1) * 8]
                for ri in range(2):
                    cin = 0 if dr == 0 else TB
                    fw.op(dv, lambda e, ri=ri, cin=cin: e.memset(H[ri][0][:, :, cin:cin + 1], 0.0), [], [H[ri][1]])
                blocks = range(nb) if dr == 0 else range(nb - 1, -1, -1)
                for bi, b in enumerate(blocks):
                    tok0 = t0 + b * TB
                    u_, ru = uT[ui % 2]
                    ui += 1
                    fw.dma(fw.SP, lambda e, u_=u_, tok0=tok0: e.dma_start(
                        out=u_[:], in_=self.dram["U"].ap()[:, tok0:tok0 + TB].rearrange("(hf p) t -> p hf t", p=128)),
                        [self.R("U")], [ru])
                    for pt_ in range(8):
                        for ri in range(2):
                            pp, prp = self.ps()
                            bt_, rbt = BT[(dr, pt_, ri)]
                            fw.op(fw.PE, lambda e, pp=pp, bt_=bt_, u_=u_, pt_=pt_: e.matmul(
                                pp[:], lhsT=bt_[:], rhs=u_[:, pt_ // 4, :], start=True, stop=True), [rbt, ru], [prp])
                            fw.op(fw.ACT, lambda e, pp=pp, ri=ri, pt_=pt_: e.activation(out=bu[ri][0][:, pt_, :], in_=pp[:], func=AF.Copy),
                                  [prp], [bu[ri][1]])
                    if bi > 0:
                        for ri in range(2):
                            if dr == 0:
                                fw.op(dv, lambda e, ri=ri: e.tensor_copy(out=H[ri][0][:, :, 0:1], in_=H[ri][0][:, :, TB:TB + 1]),
                                      [H[ri][1]], [H[ri][1]])
                            else:
                                fw.op(dv, lambda e, ri=ri: e.tensor_copy(out=H[ri][0][:, :, TB:TB + 1], in_=H[ri][0][:, :, 0:1]),
                                      [H[ri][1]], [H[ri][1]])
                    Hre, rHre = H[0]
                    Him, rHim = H[1]
                    trange = range(TB) if dr == 0 else range(TB - 1, -1, -1)
                    for t in trange:
                        pc = t if dr == 0 else t + 1
                        oc_ = t + 1 if dr == 0 else t
                        pre, pim = Hre[:, :, pc], Him[:, :, pc]
                        tt(dv, sc["a"][0][:], pre, arv, ALU.mult, [rHre, Rr["ar"]], [sc["a"][1]])
                        tt(dv, sc["b"][0][:], pim, aiv, ALU.mult, [rHim, Rr["ai"]], [sc["b"][1]])
                        tt(dv, sc["c"][0][:], sc["a"][0][:], sc["b"][0][:], ALU.subtract, [sc["a"][1], sc["b"][1]], [sc["c"][1]])
                        tt(fw.POOL, sc["d"][0][:], pim, arv, ALU.mult, [rHim, Rr["ar"]], [sc["d"][1]])
                        tt(fw.POOL, sc["e"][0][:], pre, aiv, ALU.mult, [rHre, Rr["ai"]], [sc["e"][1]])
                        tt(fw.POOL, sc["f"][0][:], sc["d"][0][:], sc["e"][0][:], ALU.add, [sc["d"][1], sc["e"][1]], [sc["f"][1]])
                        tt(dv, Hre[:, :, oc_], sc["c"][0][:], bu[0][0][:, :, t], ALU.add, [sc["c"][1], bu[0][1]], [rHre])
                        tt(fw.POOL, Him[:, :, oc_], sc["f"][0][:], bu[1][0][:, :, t], ALU.add, [sc["f"][1], bu[1][1]], [rHim])
                    off = 1 if dr == 0 else 0
                    for ri in range(2):
                        fw.op(fw.ACT, lambda e, ri=ri, off=off: e.activation(out=Hb[ri][0][:], in_=H[ri][0][:, :, off:off + TB], func=AF.Copy),
                              [H[ri][1]], [Hb[ri][1]])
                    for hf in range(2):
                        py, pry = self.ps()
                        k_ = 0
                        for pt_ in range(hf * 4, hf * 4 + 4):
                            for ri in range(2):
                                ct_, rct = CT[(dr, pt_, ri)]
                                fw.op(fw.PE, lambda e, py=py, ct_=ct_, ri=ri, pt_=pt_, k_=k_: e.matmul(
                                    py[:], lhsT=ct_[:], rhs=Hb[ri][0][:, pt_, :], start=(k_ == 0), stop=(k_ == 7)),
                                    [rct, Hb[ri][1]], [pry])
                                k_ += 1
                        ycd = YC[hf * 128:(hf + 1) * 128, tok0:tok0 + TB]
                        if dr == 0:
                            ys, rys = yst[hf]
                            fw.op(fw.ACT, lambda e, py=py, ys=ys: e.activation(out=ys[:], in_=py[:], func=AF.Copy), [pry], [rys])
                            fw.dma(fw.ACT, lambda e, ycd=ycd, ys=ys: e.dma_start(out=ycd, in_=ys[:]), [rys], [self.R("YC")])
                        else:
                            y_, ry = yc[hf]
                            fw.dma(fw.SP, lambda e, ycd=ycd, y_=y_: e.dma_start(out=y_[:], in_=ycd), [self.R("YC")], [ry])
                            ys, rys = yst[hf]
                            tt(dv, ys[:], py[:], y_[:], ALU.add, [pry, ry], [rys])
                            fw.op(dv, lambda e, ys=ys, u_=u_, hf=hf: e.scalar_tensor_tensor(
                                out=ys[:], in0=u_[:, hf, :], scalar=dcol[:, hf:hf + 1], in1=ys[:], op0=ALU.mult, op1=ALU.add),
                                [ru, rys, r_dcol], [rys])
                            fw.op(fw.ACT, lambda e, ys=ys, hf=hf: e.activation(out=gfp[hf][0][:], in_=ys[:], func=AF.Gelu_apprx_tanh),
                                  [rys], [gfp[hf][1]])
                            fw.op(fw.ACT, lambda e, hf=hf: e.activation(out=gbf[hf][0][:], in_=gfp[hf][0][:], func=AF.Copy),
                                  [gfp[hf][1]], [gbf[hf][1]])
                    if dr == 1:
                        for hf in range(2):
                            pz, prz = self.ps()
                            for k2 in range(2):
                                fw.op(fw.PE, lambda e, pz=pz, k2=k2, hf=hf: e.matmul(
                                    pz[:], lhsT=glu[:, k2, hf * 128:(hf + 1) * 128], rhs=gbf[k2][0][:], start=(k2 == 0), stop=(k2 == 1)),
                                    [r_glu, gbf[0][1], gbf[1][1]], [prz])
                            fw.op(fw.ACT, lambda e, pz=pz, hf=hf: e.activation(out=sg[hf][0][:], in_=pz[:], func=AF.Sigmoid,
                                                                              bias=gbcol[:, hf:hf + 1]), [prz, r_gbcol], [sg[hf][1]])
                            tt(dv, ocb[hf][0][:], gfp[hf][0][:], sg[hf][0][:], ALU.mult, [gfp[hf][1], sg[hf][1]], [ocb[hf][1]])
                            dst = self.dram["OC"].ap()[hf * 128:(hf + 1) * 128, tok0:tok0 + TB]
                            fw.dma(fw.ACT, lambda e, dst=dst, hf=hf: e.dma_start(out=dst, in_=ocb[hf][0][:]), [ocb[hf][1]], [self.R("OC")])
            t0 += L

import math
import numpy as np
import concourse.bass as bass
import concourse.mybir as mybir


def rev_ap(a):
    ap = [list(x) for x in a.ap]
    step, cnt = ap[-1]
    ap[-1] = [-step, cnt]
    return bass.AP(tensor=a.tensor, offset=a.offset + step * (cnt - 1), ap=ap)


class K6(K5):
    def pC_main(self, l, ctx):
        fw, W = self.fw, self.W
        T, tt, V, Rr, dv, BT, CT, TB = (ctx[k_] for k_ in ("T", "tt", "V", "Rr", "dv", "BT", "CT", "TB"))
        pl = fw.POOL
        ck = [(V["cs"], Rr["cs"])]
        sk = [(V["sn"], Rr["sn"])]
        for k_ in range(1, 10):
            c_t, c_r = T(f"c_ck{k_}", [128, 16])
            s_t, s_r = T(f"c_sk{k_}", [128, 16])
            pc, pcr = ck[-1]
            ps_, psr = sk[-1]
            tt(dv, V["t1"], pc, pc, ALU.mult, [pcr], [Rr["t1"]])
            tt(dv, V["t2"], ps_, ps_, ALU.mult, [psr], [Rr["t2"]])
            tt(dv, c_t[:], V["t1"], V["t2"], ALU.subtract, [Rr["t1"], Rr["t2"]], [c_r])
            fw.op(dv, lambda e, s_t=s_t, pc=pc, ps_=ps_: e.scalar_tensor_tensor(
                out=s_t[:], in0=ps_, scalar=2.0, in1=pc, op0=ALU.mult, op1=ALU.mult), [pcr, psr], [s_r])
            ck.append((c_t[:], c_r))
            sk.append((s_t[:], s_r))
        Ere, rEre = T("c_Ere", [128, 8, TB])
        Eim, rEim = T("c_Eim", [128, 8, TB])
        et = [T(f"c_et{i}", [128, 8, TB // 2]) for i in range(2)]
        uT = [T(f"c_uT{i}", [128, 2, TB], BF16) for i in range(2)]
        shr, r_shr = T("c_shr", [128, 8, TB])
        shi, r_shi = T("c_shi", [128, 8, TB])
        Hb = [T(f"c_Hb{ri}", [128, 8, TB], BF16) for ri in range(2)]
        tm = [[T(f"c_tm{j}_{i}", [128, TB]) for i in range(2)] for j in range(4)]
        vr = [T(f"c_vr{i}", [128, TB]) for i in range(2)]
        vi = [T(f"c_vi{i}", [128, TB]) for i in range(2)]
        un = [[T(f"c_un{j}_{i}", [128, TB]) for i in range(2)] for j in range(4)]
        ini = [T(f"c_ini{ri}", [128, 8]) for ri in range(2)]
        it_ = [T(f"c_it{j}", [128, 8]) for j in range(4)]
        yst = [T(f"c_yst{i}", [128, TB]) for i in range(2)]
        yc = [T(f"c_yc{i}", [128, TB]) for i in range(2)]
        gfp = [T(f"c_g{i}", [128, TB]) for i in range(2)]
        gbf = [T(f"c_gb{i}", [128, TB], BF16) for i in range(2)]
        sg = [T(f"c_sg{i}", [128, TB]) for i in range(2)]
        ocb = [T(f"c_oc{i}", [128, TB], BF16) for i in range(2)]
        dcol, r_dcol = T("c_dcol", [128, 2])
        gbcol, r_gbcol = T("c_gbcol", [128, 2])
        glu, r_glu = T("c_glu", [128, 2, 256], BF16)
        fw.dma(fw.SP, lambda e: e.dma_start(out=dcol[:], in_=W["s5_d"][l].rearrange("(hf p) -> p hf", p=128),
                                            allow_slow_non_contiguous=True), [], [r_dcol])
        fw.dma(fw.SP, lambda e: e.dma_start(out=gbcol[:], in_=W["s5_glu_b"][l].rearrange("(hf p) -> p hf", p=128),
                                            allow_slow_non_contiguous=True), [], [r_gbcol])
        fw.dma(fw.SP, lambda e: e.dma_start(out=glu[:], in_=self.Wb[("glu", l)].rearrange("(kt p) c -> p kt c", p=128)),
               [self.R(f"wb_glu_{l}")], [r_glu])
        if "YC" not in self.dram:
            self.dscratch("YC", [256, self.T], F32)
        YC = self.dram["YC"].ap()
        ui = 0
        cnt = 0
        for dr in range(2):
            sl = slice(dr * 8, (dr + 1) * 8)
            fw.op(dv, lambda e: e.memset(Ere[:, :, 0:1], 1.0), [], [rEre])
            fw.op(dv, lambda e: e.memset(Eim[:, :, 0:1], 0.0), [], [rEim])
            for k_ in range(9):
                n = 1 << k_
                cb = bc_ap(ck[k_][0][:, sl], 1, n)
                sb_ = bc_ap(sk[k_][0][:, sl], 1, n)
                rd = [rEre, rEim, ck[k_][1], sk[k_][1]]
                e0, r0 = et[0]
                e1, r1 = et[1]
                tt(dv, e0[:, :, 0:n], Ere[:, :, 0:n], cb, ALU.mult, rd, [r0])
                tt(dv, e1[:, :, 0:n], Eim[:, :, 0:n], sb_, ALU.mult, rd, [r1])
                tt(dv, Ere[:, :, n:2 * n], e0[:, :, 0:n], e1[:, :, 0:n], ALU.subtract, [r0, r1], [rEre])
                tt(dv, e0[:, :, 0:n], Ere[:, :, 0:n], sb_, ALU.mult, rd, [r0])
                tt(dv, e1[:, :, 0:n], Eim[:, :, 0:n], cb, ALU.mult, rd, [r1])
                tt(dv, Eim[:, :, n:2 * n], e0[:, :, 0:n], e1[:, :, 0:n], ALU.add, [r0, r1], [rEim])
            c9 = ck[9][0][:, sl]
            s9 = sk[9][0][:, sl]
            t0 = 0
            for s, L in enumerate(self.seqs):
                nb = L // TB
                for bi in range(nb):
                    b = bi if dr == 0 else nb - 1 - bi
                    tok0 = t0 + b * TB
                    u_, ru = uT[ui % 2]
                    ui += 1
                    fw.dma(fw.SP, lambda e, u_=u_, tok0=tok0: e.dma_start(
                        out=u_[:], in_=self.dram["U"].ap()[:, tok0:tok0 + TB].rearrange("(hf p) t -> p hf t", p=128)),
                        [self.R("U")], [ru])
                    if bi == 0:
                        for ri in range(2):
                            fw.op(dv, lambda e, ri=ri: e.memset(ini[ri][0][:], 0.0), [], [ini[ri][1]])
                    else:
                        lre, lim = shr[:, :, TB - 1], shi[:, :, TB - 1]
                        rdd = [r_shr, r_shi, ck[9][1], sk[9][1]]
                        tt(dv, it_[0][0][:], lre, c9, ALU.mult, rdd, [it_[0][1]])
                        tt(dv, it_[1][0][:], lim, s9, ALU.mult, rdd, [it_[1][1]])
                        tt(dv, it_[2][0][:], lim, c9, ALU.mult, rdd, [it_[2][1]])
                        tt(dv, it_[3][0][:], lre, s9, ALU.mult, rdd, [it_[3][1]])
                        tt(dv, ini[0][0][:], it_[0][0][:], it_[1][0][:], ALU.subtract, [it_[0][1], it_[1][1]], [ini[0][1]])
                        tt(dv, ini[1][0][:], it_[2][0][:], it_[3][0][:], ALU.add, [it_[2][1], it_[3][1]], [ini[1][1]])
                    for pt_ in range(8):
                        cb2 = cnt % 2
                        cnt += 1
                        pre, prre = self.ps()
                        pim, prim = self.ps()
                        for ri, (pp, prp) in enumerate(((pre, prre), (pim, prim))):
                            bt_, rbt = BT[(dr, pt_, ri)]
                            fw.op(fw.PE, lambda e, pp=pp, bt_=bt_, u_=u_, pt_=pt_: e.matmul(
                                pp[:], lhsT=bt_[:], rhs=u_[:, pt_ // 4, :], start=True, stop=True), [rbt, ru], [prp])
                        sre = pre[:] if dr == 0 else rev_ap(pre[:])
                        sim = pim[:] if dr == 0 else rev_ap(pim[:])
                        er, ei = Ere[:, pt_, :], Eim[:, pt_, :]
                        t1, t2, t3, t4 = (tm[j][cb2] for j in range(4))
                        tt(dv, t1[0][:], sre, er, ALU.mult, [prre, rEre], [t1[1]])
                        tt(dv, t2[0][:], sim, ei, ALU.mult, [prim, rEim], [t2[1]])
                        tt(dv, t3[0][:], sim, er, ALU.mult, [prim, rEre], [t3[1]])
                        tt(dv, t4[0][:], sre, ei, ALU.mult, [prre, rEim], [t4[1]])
                        vre, rvre = vr[cb2]
                        vim, rvim = vi[cb2]
                        tt(pl, vre[:], t1[0][:], t2[0][:], ALU.add, [t1[1], t2[1]], [rvre])
                        tt(pl, vim[:], t3[0][:], t4[0][:], ALU.subtract, [t3[1], t4[1]], [rvim])
                        a_ = V["mag"][:, dr * 8 + pt_: dr * 8 + pt_ + 1]
                        rbc = bass.AP(tensor=a_.tensor, offset=a_.offset, ap=[list(a_.ap[0]), [0, TB]])
                        fw.op(dv, lambda e, pt_=pt_, vre=vre, rbc=rbc: e.tensor_tensor_scan(
                            out=shr[:, pt_, :], data0=rbc, data1=vre[:], initial=ini[0][0][:, pt_:pt_ + 1], op0=ALU.mult, op1=ALU.add),
                            [rvre, Rr["mag"], ini[0][1]], [r_shr])
                        fw.op(dv, lambda e, pt_=pt_, vim=vim, rbc=rbc: e.tensor_tensor_scan(
                            out=shi[:, pt_, :], data0=rbc, data1=vim[:], initial=ini[1][0][:, pt_:pt_ + 1], op0=ALU.mult, op1=ALU.add),
                            [rvim, Rr["mag"], ini[1][1]], [r_shi])
                        u1, u2, u3, u4 = (un[j][cb2] for j in range(4))
                        tt(dv, u1[0][:], shr[:, pt_, :], er, ALU.mult, [r_shr, rEre], [u1[1]])
                        tt(dv, u2[0][:], shi[:, pt_, :], ei, ALU.mult, [r_shi, rEim], [u2[1]])
                        tt(dv, Hb[0][0][:, pt_, :], u1[0][:], u2[0][:], ALU.subtract, [u1[1], u2[1]], [Hb[0][1]])
                        tt(pl, u3[0][:], shi[:, pt_, :], er, ALU.mult, [r_shi, rEre], [u3[1]])
                        tt(pl, u4[0][:], shr[:, pt_, :], ei, ALU.mult, [r_shr, rEim], [u4[1]])
                        tt(pl, Hb[1][0][:, pt_, :], u3[0][:], u4[0][:], ALU.add, [u3[1], u4[1]], [Hb[1][1]])
                    for hf in range(2):
                        py, pry = self.ps()
                        k_ = 0
                        for pt_ in range(hf * 4, hf * 4 + 4):
                            for ri in range(2):
                                ct_, rct = CT[(dr, pt_, ri)]
                                fw.op(fw.PE, lambda e, py=py, ct_=ct_, ri=ri, pt_=pt_, k_=k_: e.matmul(
                                    py[:], lhsT=ct_[:], rhs=Hb[ri][0][:, pt_, :], start=(k_ == 0), stop=(k_ == 7)),
                                    [rct, Hb[ri][1]], [pry])
                                k_ += 1
                        ycd = YC[hf * 128:(hf + 1) * 128, tok0:tok0 + TB]
                        if dr == 0:
                            ys, rys = yst[hf]
                            fw.op(fw.ACT, lambda e, py=py, ys=ys: e.activation(out=ys[:], in_=py[:], func=AF.Copy), [pry], [rys])
                            fw.dma(fw.ACT, lambda e, ycd=ycd, ys=ys: e.dma_start(out=ycd, in_=ys[:]), [rys], [self.R("YC")])
                        else:
                            y_, ry = yc[hf]
                            fw.dma(fw.SP, lambda e, ycd=ycd, y_=y_: e.dma_start(out=y_[:], in_=ycd), [self.R("YC")], [ry])
                            ys, rys = yst[hf]
                            tt(dv, ys[:], rev_ap(py[:]), y_[:], ALU.add, [pry, ry], [rys])
                            fw.op(dv, lambda e, ys=ys, u_=u_, hf=hf: e.scalar_tensor_tensor(
                                out=ys[:], in0=u_[:, hf, :], scalar=dcol[:, hf:hf + 1], in1=ys[:], op0=ALU.mult, op1=ALU.add),
                                [ru, rys, r_dcol], [rys])
                            fw.op(fw.ACT, lambda e, ys=ys, hf=hf: e.activation(out=gfp[hf][0][:], in_=ys[:], func=AF.Gelu_apprx_tanh),
                                  [rys], [gfp[hf][1]])
                            fw.op(fw.ACT, lambda e, hf=hf: e.activation(out=gbf[hf][0][:], in_=gfp[hf][0][:], func=AF.Copy),
                                  [gfp[hf][1]], [gbf[hf][1]])
                    if dr == 1:
                        for hf in range(2):
                            pz, prz = self.ps()
                            for k2 in range(2):
                                fw.op(fw.PE, lambda e, pz=pz, k2=k2, hf=hf: e.matmul(
                                    pz[:], lhsT=glu[:, k2, hf * 128:(hf + 1) * 128], rhs=gbf[k2][0][:], start=(k2 == 0), stop=(k2 == 1)),
                                    [r_glu, gbf[0][1], gbf[1][1]], [prz])
                            fw.op(fw.ACT, lambda e, pz=pz, hf=hf: e.activation(out=sg[hf][0][:], in_=pz[:], func=AF.Sigmoid,
                                                                              bias=gbcol[:, hf:hf + 1]), [prz, r_gbcol], [sg[hf][1]])
                            tt(dv, ocb[hf][0][:], gfp[hf][0][:], sg[hf][0][:], ALU.mult, [gfp[hf][1], sg[hf][1]], [ocb[hf][1]])
                            dst = self.dram["OC"].ap()[hf * 128:(hf + 1) * 128, tok0:tok0 + TB]
                            fw.dma(fw.ACT, lambda e, dst=dst, hf=hf: e.dma_start(out=dst, in_=ocb[hf][0][:]), [ocb[hf][1]], [self.R("OC")])
                t0 += L

import ml_dtypes
ROPE_THETA = 500000.0


def _host_consts(Lmax):
    half = 4
    inv = np.power(np.float32(ROPE_THETA), -np.arange(half, dtype=np.float32) / half).astype(np.float32)
    pos = np.arange(Lmax, dtype=np.float32)
    ang = pos[None, :] * inv[:, None]
    cos = np.ones((32, Lmax), np.float32); sin = np.zeros((32, Lmax), np.float32)
    cos[0:4] = np.cos(ang); cos[4:8] = np.cos(ang); sin[0:4] = np.sin(ang); sin[4:8] = np.sin(ang)
    p = np.arange(128)
    d = dict(gmask=(p[:, None] // 32 == np.arange(4)[None, :]).astype(np.float32),
             ret_rel=(p[None, :] - p[:, None]).astype(np.float32),
             ret_col4=np.tile((np.arange(512) % 128).astype(np.float32)[None, :], (128, 1)),
             ret_pidx=p.astype(np.float32)[:, None],
             rope_cos=cos, rope_sin=sin, ident_f=np.eye(128, dtype=np.float32),
             ident_b=np.eye(128).astype(ml_dtypes.bfloat16))
    d.update(K6.host_consts_extra())
    return d


def kernel(**inputs):
    xp = np.asarray(inputs["x_prompt"]); xs = np.asarray(inputs["x_sample"])
    cp = np.asarray(inputs["c_prompt"]); cs = np.asarray(inputs["c_sample"])
    Lp, Ls = xp.shape[1], xs.shape[1]
    seqs = [Lp, Ls, Ls]
    k = K6(seqs, phases=("p0", "p1", "pA", "pB", "pC", "pD", "p3", "p4"))
    nc = k.build()
    hc = _host_consts(max(seqs))
    in_maps = []
    for c in range(8):
        m = {}
        m["x"] = np.ascontiguousarray(np.concatenate([xp[c // 4], xs[2 * c], xs[2 * c + 1]], axis=0))
        cc = np.stack([cp[c // 4], cs[2 * c], cs[2 * c + 1]], axis=0)
        m["cT"] = np.ascontiguousarray(cc.reshape(3, 8, 128).transpose(2, 1, 0))
        for kk in k.W:
            m[kk] = np.ascontiguousarray(np.asarray(inputs[kk]))
        for kk, v in hc.items():
            if kk in k.dram:
                m[kk] = v
        in_maps.append(m)
    res = run_bass_kernel_spmd(nc, in_maps, core_ids=list(range(8)))
    yp = np.zeros_like(xp); ys = np.zeros_like(xs)
    q = Lp // 4
    for c in range(8):
        y = res.results[c]["y"]
        r = c % 4
        yp[c // 4, r * q:(r + 1) * q] = y[r * q:(r + 1) * q]
        ys[2 * c] = y[Lp:Lp + Ls]
        ys[2 * c + 1] = y[Lp + Ls:Lp + 2 * Ls]
    return (yp, ys)
```
